# Optimizing a Trainium2 kernel written in Bass

```python
import math
import jax, jax.numpy as jnp
from jax import lax
import numpy as np


D_MODEL = 1024
BATCH = 16
SEQ = 2048
DEPTH = 1
DEC_BATCH = 8
DEC_SEQ = 8192
PAST_LEN = 128

RET_HEADS = 4
RET_DK = 256
RET_DV = 256
RET_QK_W = RET_HEADS * RET_DK
RET_V_W = RET_HEADS * RET_DV
RET_CHUNK = 128
ROPE_BASE = 10000.0
POOL_WINDOWS = (2, 4, 8, 16)
POOL_GROUPS = 4
POOL_GW = 256
POOL_W = POOL_GROUPS * POOL_GW
MEM_TOKENS = 256
MEM_HEADS = 4
MEM_HD = 256
MEM_W = MEM_HEADS * MEM_HD
N_BRANCH = 3
D_FF = 2816
CONV_W = 3
EPS = 1e-6

IN_SPLITS = (RET_QK_W, 2 * RET_QK_W, 2 * RET_QK_W + RET_V_W, 2 * RET_QK_W + 2 * RET_V_W,
             2 * RET_QK_W + 2 * RET_V_W + POOL_W, 2 * RET_QK_W + 2 * RET_V_W + POOL_W + MEM_W)
IN_COLS = 2 * RET_QK_W + 2 * RET_V_W + POOL_W + MEM_W + N_BRANCH * D_MODEL

kernel_name = "hybrid_retention_pool_memory_encoder"


def rmsnorm(x, g):
    xf = x.astype(jnp.float32)
    y = xf * lax.rsqrt(jnp.mean(xf * xf, axis=-1, keepdims=True) + EPS)
    return (y * g.astype(jnp.float32)).astype(x.dtype)


def rotary(x):
    s, d = x.shape[1], x.shape[-1]
    half = d // 2
    inv = ROPE_BASE ** (-jnp.arange(half, dtype=jnp.float32) / half)
    ang = jnp.arange(s, dtype=jnp.float32)[:, None] * inv[None, :]
    cos = jnp.cos(ang)[None, :, None, :]
    sin = jnp.sin(ang)[None, :, None, :]
    xf = x.astype(jnp.float32)
    x1, x2 = xf[..., :half], xf[..., half:]
    return jnp.concatenate([x1 * cos - x2 * sin, x1 * sin + x2 * cos], axis=-1)


def retention_one_direction(q, k, v, log_gamma, inclusive):
    b, h, s, dk = q.shape
    dv = v.shape[-1]
    c = RET_CHUNK
    n = s // c
    qc = q.reshape(b, h, n, c, dk)
    kc = k.reshape(b, h, n, c, dk)
    vc = v.reshape(b, h, n, c, dv)
    idx = jnp.arange(c, dtype=jnp.float32)
    diff = idx[:, None] - idx[None, :]
    mask = diff >= 0 if inclusive else diff > 0
    decay = jnp.where(mask[None], jnp.exp(log_gamma[:, None, None] * jnp.maximum(diff, 0.0)[None]), 0.0)
    scores = jnp.einsum('bhncd,bhnmd->bhncm', qc, kc) * decay[None, :, None]
    y_intra = jnp.einsum('bhncm,bhnme->bhnce', scores, vc)
    q_dec = jnp.exp(log_gamma[:, None] * (idx + 1.0))[None, :, :, None]
    k_dec = jnp.exp(log_gamma[:, None] * (c - 1.0 - idx))[None, :, :, None]
    chunk_dec = jnp.exp(log_gamma * c)[None, :, None, None]

    def step(state, inp):
        q_i, k_i, v_i = inp
        y = jnp.einsum('bhcd,bhde->bhce', q_i, state) * q_dec
        state = state * chunk_dec + jnp.einsum('bhcd,bhce->bhde', k_i * k_dec, v_i)
        return state, y

    init = jnp.zeros((b, h, dk, dv), jnp.float32)
    xs = (jnp.moveaxis(qc, 2, 0), jnp.moveaxis(kc, 2, 0), jnp.moveaxis(vc, 2, 0))
    _, y_cross = lax.scan(step, init, xs)
    y = y_intra + jnp.moveaxis(y_cross, 0, 2)
    return y.reshape(b, h, s, dv)


def retention_branch(rq, rk, rv, rg, decay_fwd, decay_bwd, ret_gn):
    b, s, _ = rq.shape
    q = rotary(rq.reshape(b, s, RET_HEADS, RET_DK)).transpose(0, 2, 1, 3)
    k = (rotary(rk.reshape(b, s, RET_HEADS, RET_DK)) * (RET_DK ** -0.5)).transpose(0, 2, 1, 3)
    v = rv.astype(jnp.float32).reshape(b, s, RET_HEADS, RET_DV).transpose(0, 2, 1, 3)
    lg_f = jax.nn.log_sigmoid(decay_fwd.astype(jnp.float32))
    lg_b = jax.nn.log_sigmoid(decay_bwd.astype(jnp.float32))
    y_f = retention_one_direction(q, k, v, lg_f, True)
    y_b = jnp.flip(retention_one_direction(jnp.flip(q, 2), jnp.flip(k, 2), jnp.flip(v, 2), lg_b, False), 2)
    y = y_f + y_b
    mu = jnp.mean(y, axis=-1, keepdims=True)
    var = jnp.mean(jnp.square(y - mu), axis=-1, keepdims=True)
    y = (y - mu) * lax.rsqrt(var + EPS)
    y = y.transpose(0, 2, 1, 3).reshape(b, s, RET_V_W) * ret_gn.astype(jnp.float32)
    y = jax.nn.silu(rg.astype(jnp.float32)) * y
    return y.astype(rq.dtype)


def pool_branch(p, pool_w, pool_scale):
    b, s, _ = p.shape
    pf = p.astype(jnp.float32)
    cs = jnp.concatenate([jnp.zeros((b, 1, POOL_W), jnp.float32), jnp.cumsum(pf, axis=1)], axis=1)
    pos = jnp.arange(s)
    outs = []
    for gi, w in enumerate(POOL_WINDOWS):
        lo = jnp.maximum(pos - w // 2, 0)
        hi = jnp.minimum(pos + w // 2, s)
        sl = slice(gi * POOL_GW, (gi + 1) * POOL_GW)
        csg = cs[:, :, sl]
        mean = (csg[:, hi] - csg[:, lo]) / (hi - lo).astype(jnp.float32)[None, :, None]
        outs.append(mean - pf[:, :, sl])
    d = jnp.stack(outs, axis=2)
    y = jnp.einsum('bsgc,gcd->bsgd', d, pool_w.astype(jnp.float32)).reshape(b, s, POOL_W)
    return (y * pool_scale.astype(jnp.float32)).astype(p.dtype)


def memory_branch(mq, mem, g_mem, w_mem_kv):
    b, s, _ = mq.shape
    m = mem.shape[1]
    kv = rmsnorm(mem, g_mem) @ w_mem_kv
    k, v = jnp.split(kv, 2, axis=-1)
    k = k.reshape(b, m, MEM_HEADS, MEM_HD)
    v = v.reshape(b, m, MEM_HEADS, MEM_HD)
    q = mq.reshape(b, s, MEM_HEADS, MEM_HD)
    scores = jnp.einsum('bshd,bmhd->bhsm', q, k).astype(jnp.float32) * (MEM_HD ** -0.5)
    probs = jax.nn.softmax(scores, axis=-1).astype(v.dtype)
    out = jnp.einsum('bhsm,bmhd->bshd', probs, v)
    return out.reshape(b, s, MEM_W)


def conv_ffn(h, w_up, conv_w, conv_b, w_down):
    u = h @ w_up
    u = lax.conv_general_dilated(u, conv_w[:, None, :].astype(u.dtype), (1,), ((CONV_W // 2, CONV_W // 2),),
                                 dimension_numbers=('NWC', 'WIO', 'NWC'),
                                 feature_group_count=2 * D_FF) + conv_b
    gate, val = jnp.split(u, 2, axis=-1)
    return (jax.nn.gelu(gate, approximate=True) * val) @ w_down


def encoder_layer(x, mem, g_mix_pre, g_mix_post, g_mem, w_in, decay_fwd, decay_bwd, ret_gn, w_ret_out,
                  pool_w, pool_scale, w_pool_out, w_mem_kv, w_mem_out, w_o,
                  g_ffn_pre, g_ffn_post, w_up, conv_w, conv_b, w_down):
    b, s, _ = x.shape
    h = rmsnorm(x, g_mix_pre)
    z = h @ w_in
    rq, rk, rv, rg, pin, mq, gl = jnp.split(z, list(IN_SPLITS), axis=-1)
    gates = jax.nn.sigmoid(gl.astype(jnp.float32)).reshape(b, s, N_BRANCH, D_MODEL).astype(x.dtype)
    y_ret = retention_branch(rq, rk, rv, rg, decay_fwd, decay_bwd, ret_gn) @ w_ret_out
    y_pool = pool_branch(pin, pool_w, pool_scale) @ w_pool_out
    y_mem = memory_branch(mq, mem, g_mem, w_mem_kv) @ w_mem_out
    merged = gates[:, :, 0] * y_ret + gates[:, :, 1] * y_pool + gates[:, :, 2] * y_mem
    x = x + rmsnorm(merged @ w_o, g_mix_post)
    h2 = rmsnorm(x, g_ffn_pre)
    x = x + rmsnorm(conv_ffn(h2, w_up, conv_w, conv_b, w_down), g_ffn_post)
    return x


def trunk(x, mem, weights):
    for l in range(DEPTH):
        x = encoder_layer(x, mem, *[w[l] for w in weights])
    return x


def setup_inputs(seed: int = 0) -> dict:
    key = jax.random.key(seed)
    ks = jax.random.split(key, 26)
    f32 = jnp.float32

    def nrm(k, shape, scale):
        return jax.random.normal(k, shape, f32) * scale

    def gain(k, shape):
        return 1.0 + 0.05 * jax.random.normal(k, shape, f32)

    decay_base = jnp.log(2.0 ** (5.0 + jnp.arange(RET_HEADS, dtype=f32)) - 1.0)
    L = DEPTH
    return {
        "x_prompt": nrm(ks[0], (BATCH, SEQ, D_MODEL), 1.0),
        "x_sample": nrm(ks[1], (DEC_BATCH, DEC_SEQ, D_MODEL), 1.0),
        "mem_prompt": nrm(ks[2], (BATCH, MEM_TOKENS, D_MODEL), 1.0),
        "mem_sample": nrm(ks[3], (DEC_BATCH, MEM_TOKENS, D_MODEL), 1.0),
        "g_mix_pre": gain(ks[4], (L, D_MODEL)),
        "g_mix_post": gain(ks[5], (L, D_MODEL)),
        "g_mem": gain(ks[6], (L, D_MODEL)),
        "w_in": nrm(ks[7], (L, D_MODEL, IN_COLS), D_MODEL ** -0.5),
        "decay_fwd": decay_base[None, :] + 0.1 * jax.random.normal(ks[8], (L, RET_HEADS), f32),
        "decay_bwd": decay_base[None, :] + 0.1 * jax.random.normal(ks[9], (L, RET_HEADS), f32),
        "ret_gn": gain(ks[10], (L, RET_V_W)),
        "w_ret_out": nrm(ks[11], (L, RET_V_W, D_MODEL), RET_V_W ** -0.5),
        "pool_w": nrm(ks[12], (L, POOL_GROUPS, POOL_GW, POOL_GW), POOL_GW ** -0.5),
        "pool_scale": gain(ks[13], (L, POOL_W)),
        "w_pool_out": nrm(ks[14], (L, POOL_W, D_MODEL), POOL_W ** -0.5),
        "w_mem_kv": nrm(ks[15], (L, D_MODEL, 2 * MEM_W), D_MODEL ** -0.5),
        "w_mem_out": nrm(ks[16], (L, MEM_W, D_MODEL), MEM_W ** -0.5),
        "w_o": nrm(ks[17], (L, D_MODEL, D_MODEL), D_MODEL ** -0.5),
        "g_ffn_pre": gain(ks[18], (L, D_MODEL)),
        "g_ffn_post": gain(ks[19], (L, D_MODEL)),
        "w_up": nrm(ks[20], (L, D_MODEL, 2 * D_FF), D_MODEL ** -0.5),
        "conv_w": nrm(ks[21], (L, CONV_W, 2 * D_FF), CONV_W ** -0.5),
        "conv_b": nrm(ks[22], (L, 2 * D_FF), 0.02),
        "w_down": nrm(ks[23], (L, D_FF, D_MODEL), D_FF ** -0.5),
    }


def reference(x_prompt, x_sample, mem_prompt, mem_sample, g_mix_pre, g_mix_post, g_mem, w_in,
              decay_fwd, decay_bwd, ret_gn, w_ret_out, pool_w, pool_scale, w_pool_out,
              w_mem_kv, w_mem_out, w_o, g_ffn_pre, g_ffn_post, w_up, conv_w, conv_b, w_down):
    weights = (g_mix_pre, g_mix_post, g_mem, w_in, decay_fwd, decay_bwd, ret_gn, w_ret_out,
               pool_w, pool_scale, w_pool_out, w_mem_kv, w_mem_out, w_o,
               g_ffn_pre, g_ffn_post, w_up, conv_w, conv_b, w_down)
    y_prompt = trunk(x_prompt, mem_prompt, weights)
    y_sample = trunk(x_sample, mem_sample, weights)
    return (y_prompt, y_sample)
```

```python
import math
import numpy as np
import concourse.bass as bass
import concourse.mybir as mybir
from concourse.bass_types import AP
from concourse.bass_utils import run_bass_kernel_spmd
from contextlib import ExitStack

F32 = mybir.dt.float32
BF16 = mybir.dt.bfloat16
I32 = mybir.dt.int32
ALU = mybir.AluOpType
AF = mybir.ActivationFunctionType

ENGS = ("sync", "tensor", "vector", "scalar", "gpsimd")
PH = ["init"]
EPOCH = 6000
NDMASEM = 24


class Res:
    __slots__ = ("name", "last_w", "readers")

    def __init__(self, name):
        self.name = name
        self.last_w = None
        self.readers = []


class Instr:
    __slots__ = ("eng", "fn", "is_dma", "deps", "signal", "sig_no", "dma_slot", "dma_val", "tag")

    def __init__(self, eng, fn, is_dma):
        self.eng = eng
        self.fn = fn
        self.is_dma = is_dma
        self.deps = []
        self.signal = False
        self.sig_no = None
        self.dma_slot = None
        self.dma_val = None


class Sched:
    def __init__(self, nc):
        self.nc = nc
        self.streams = {e: [] for e in ENGS}
        self.ndma = {e: 0 for e in ENGS}

    def res(self, name):
        return Res(name)

    def op(self, eng, fn, reads=(), writes=(), is_dma=False):
        ins = Instr(eng, fn, is_dma)
        ins.tag = PH[0]
        deps = {}
        for r in reads:
            lw = r.last_w
            if lw is not None:
                deps[id(lw)] = (lw, True)
        for w in writes:
            lw = w.last_w
            if lw is not None:
                deps[id(lw)] = (lw, True)
            for rd in w.readers:
                if id(rd) not in deps:
                    deps[id(rd)] = (rd, False)
        for d, hard in deps.values():
            if (not d.is_dma) and (not is_dma) and d.eng == eng:
                if eng == "tensor" or not hard:
                    continue
            ins.deps.append(d)
            d.signal = True
        if is_dma:
            ins.signal = True
            k = self.ndma[eng]
            self.ndma[eng] = k + 1
            ins.dma_slot = k % NDMASEM
            ins.dma_val = 16 * (k // NDMASEM + 1)
        for r in reads:
            r.readers.append(ins)
        for w in writes:
            w.last_w = ins
            w.readers = []
        self.streams[eng].append(ins)
        return ins

    def emit(self):
        nc = self.nc
        with ExitStack() as es:
            csem = {}
            for e in ENGS:
                n = 0
                for ins in self.streams[e]:
                    if ins.signal and not ins.is_dma:
                        n += 1
                        ins.sig_no = n
                nep = max((n + EPOCH - 1) // EPOCH, 1)
                csem[e] = [es.enter_context(nc.semaphore(f"c_{e}_{k}")) for k in range(nep)]
            dsem = {}
            for e in ENGS:
                if self.ndma[e]:
                    dsem[e] = [es.enter_context(nc.semaphore(f"d_{e}_{k}"))
                               for k in range(min(NDMASEM, self.ndma[e]))]
            streams = self.streams

            def emit_stream(ename, eng):
                waited = {}
                maxep = {}
                dma_hist = {}
                for ins in streams[ename]:
                    need = []
                    for d in ins.deps:
                        if d.is_dma:
                            need.append((("d", d.eng, d.dma_slot), dsem[d.eng][d.dma_slot], d.dma_val))
                        else:
                            ep = (d.sig_no - 1) // EPOCH
                            if maxep.get(d.eng, -1) > ep:
                                continue
                            need.append((("c", d.eng, ep), csem[d.eng][ep], d.sig_no - ep * EPOCH))
                    if ins.is_dma:
                        prev = dma_hist.get(ins.dma_slot)
                        if prev is not None:
                            need.append((("d", ename, ins.dma_slot), dsem[ename][ins.dma_slot], prev.dma_val))
                        dma_hist[ins.dma_slot] = ins
                    best = {}
                    for key, sem, val in need:
                        if waited.get(key, 0) >= val:
                            continue
                        if key not in best or best[key][1] < val:
                            best[key] = (sem, val)
                    for key, (sem, val) in best.items():
                        eng.wait_ge(sem, val)
                        waited[key] = val
                        if key[0] == "c":
                            maxep[key[1]] = max(maxep.get(key[1], -1), key[2])
                    h = ins.fn(eng)
                    if ins.is_dma:
                        h.then_inc(dsem[ename][ins.dma_slot], 16)
                    elif ins.signal:
                        ep = (ins.sig_no - 1) // EPOCH
                        h.then_inc(csem[ename][ep], 1)
                for slot, prev in dma_hist.items():
                    if waited.get(("d", ename, slot), 0) < prev.dma_val:
                        eng.wait_ge(dsem[ename][slot], prev.dma_val)

            with nc.Block() as block:
                @block.sync
                def _(e):
                    emit_stream("sync", e)

                @block.tensor
                def _(e):
                    emit_stream("tensor", e)

                @block.vector
                def _(e):
                    emit_stream("vector", e)

                @block.scalar
                def _(e):
                    emit_stream("scalar", e)

                @block.gpsimd
                def _(e):
                    emit_stream("gpsimd", e)


class View:
    def __init__(self, base, K, N, res, name=""):
        self.base = base
        self.K = K
        self.N = N
        self.res = res
        self.p = list(base.ap[0])
        self.name = name

    def ap(self, dims, off=0):
        return AP(tensor=self.base.tensor, offset=self.base.offset + off,
                  ap=[self.p] + [list(d) for d in dims])

    def k(self, k, c0=0, c1=None):
        c1 = self.N if c1 is None else c1
        return self.ap([[1, c1 - c0]], k * self.N + c0)

    def ks(self, k0, k1, c0=0, c1=None):
        c1 = self.N if c1 is None else c1
        return self.ap([[self.N, k1 - k0], [1, c1 - c0]], k0 * self.N + c0)

    def all(self):
        return self.ap([[1, self.K * self.N]])


PAGE = 1024


class Arena:
    def __init__(self, S, tens, nbytes):
        self.t = tens
        self.np_ = nbytes // PAGE
        self.free = [True] * self.np_
        self.res = [S.res(f"pg{i}") for i in range(self.np_)]
        self.peak = 0

    def alloc(self, name, dtype, K, N, rot=False):
        esz = 4 if dtype in (F32, I32) else 2
        npg = (K * N * esz + PAGE - 1) // PAGE
        i0 = -1
        ptr = getattr(self, "ptr", 0)
        if npg >= 4 and not rot:
            run = 0
            for i in range(self.np_ - 1, -1, -1):
                run = run + 1 if self.free[i] else 0
                if run == npg:
                    i0 = i
                    break
            if i0 >= 0:
                for i in range(i0, i0 + npg):
                    self.free[i] = False
                self.peak = max(self.peak, self.np_ - sum(self.free))
                b = self.t[:, i0 * PAGE // 4:(i0 + npg) * PAGE // 4]
                if dtype != F32:
                    b = b.bitcast(dtype)
                v = View(b, K, N, self.res[i0:i0 + npg], name)
                v.pages = (i0, npg)
                return v
        for lo, hi in ((ptr, self.np_), (0, self.np_)):
            run = 0
            for i in range(lo, hi):
                run = run + 1 if self.free[i] else 0
                if run == npg:
                    i0 = i - npg + 1
                    break
            if i0 >= 0:
                break
        if i0 >= 0:
            self.ptr = i0 + npg
        if i0 < 0:
            raise RuntimeError(f"arena full allocating {name} ({npg} pages); free={sum(self.free)}")
        for i in range(i0, i0 + npg):
            self.free[i] = False
        self.peak = max(self.peak, self.np_ - sum(self.free))
        b = self.t[:, i0 * PAGE // 4:(i0 + npg) * PAGE // 4]
        if dtype != F32:
            b = b.bitcast(dtype)
        v = View(b, K, N, self.res[i0:i0 + npg], name)
        v.pages = (i0, npg)
        return v

    def release(self, *views):
        for v in views:
            i0, npg = v.pages
            for i in range(i0, i0 + npg):
                assert not self.free[i], v.name
                self.free[i] = True


D = 1024
DFF = 2816
NPAIR = 22
NMEM = 256
EPS = 1e-6
TWO_PI = 2.0 * math.pi


def build_program(seq_lens, T):
    G = T // 128
    NSEQ = len(seq_lens)
    NTOK = sum(seq_lens)
    SMAX = max(seq_lens)
    for s in seq_lens:
        assert s % T == 0 and s // T >= 2
    nc = bass.Bass("TRN2", target_bir_lowering=False)

    def din(name, shape):
        return nc.dram_tensor(name, shape, F32, kind="ExternalInput").ap()

    x_d = din("x", [NTOK, D])
    mem_d = din("mem", [NSEQ * NMEM, D])
    g_mix_pre_d = din("g_mix_pre", [D])
    g_mix_post_d = din("g_mix_post", [1, D])
    g_mem_d = din("g_mem", [D])
    w_in_d = din("w_in", [D, 9216])
    decay_fwd_d = din("decay_fwd", [1, 4])
    decay_bwd_d = din("decay_bwd", [1, 4])
    ret_gn_d = din("ret_gn", [D])
    w_ret_out_d = din("w_ret_out", [D, D])
    pool_w_d = din("pool_w", [4, 256, 256])
    pool_scale_d = din("pool_scale", [D])
    w_pool_out_d = din("w_pool_out", [D, D])
    w_mem_kv_d = din("w_mem_kv", [D, 2048])
    w_mem_out_d = din("w_mem_out", [D, D])
    w_o_d = din("w_o", [D, D])
    g_ffn_pre_d = din("g_ffn_pre", [D])
    g_ffn_post_d = din("g_ffn_post", [1, D])
    w_up_d = din("w_up", [D, 2 * DFF])
    conv_w_d = din("conv_w", [3, 2 * DFF])
    conv_b_d = din("conv_b", [2 * DFF])
    w_down_d = din("w_down", [DFF, D])
    y_d = nc.dram_tensor("y", [NTOK, D], F32, kind="ExternalOutput").ap()

    BLK = {}
    nb = 0
    for name, n in [("in", 18), ("ret_out", 2), ("pool_out", 2), ("mem_out", 2), ("o", 2),
                    ("poolw", 1), ("up", 11), ("down", 6), ("memkv", 4)]:
        BLK[name] = list(range(nb, nb + n))
        nb += n
    NBLK = nb
    wb_d = nc.dram_tensor("wb_scr", [NBLK, 128, 8 * 512], BF16, kind="Internal").ap()
    cs_d = nc.dram_tensor("cs_scr", [2, 128, SMAX], F32, kind="Internal").ap()
    NCHT = NTOK // 128
    sb_d = nc.dram_tensor("sb_scr", [NCHT, 128, 2048], BF16, kind="Internal").ap()
    NT2T = NTOK // T
    kts_d = nc.dram_tensor("kts_scr", [NT2T, 128, 8 * T], BF16, kind="Internal").ap()
    vts_d = nc.dram_tensor("vts_scr", [NT2T, 128, G * D], BF16, kind="Internal").ap()

    S = Sched(nc)
    es = ExitStack()

    def sbt(name, shape, dt):
        return es.enter_context(nc.sbuf_tensor(name, shape, dt))

    def fixed(name, dt, K, N):
        t = sbt(name, [128, K * N], dt)
        return View(t[:], K, N, [S.res(name)], name)

    def flat(lst):
        out = []
        for a in lst:
            if a is None:
                continue
            if hasattr(a, "res"):
                out.extend(a.res)
            elif isinstance(a, (list, tuple)):
                out.extend(flat(a))
            else:
                out.append(a)
        return out

    def OP(eng, meth, reads, writes, **kw):
        S.op(eng, lambda e: getattr(e, meth)(**kw), flat(reads), flat(writes))

    def DMA(eng, out, in_, reads, writes, **kw):
        S.op(eng, lambda e: e.dma_start(out=out, in_=in_, **kw), flat(reads), flat(writes), is_dma=True)

    R_W = 5
    wring = [fixed(f"wring{i}", BF16, 8, 512) for i in range(R_W)]
    ident = fixed("ident", BF16, 1, 128)
    identf = fixed("identf", F32, 1, 128)
    ones_bf = fixed("ones_bf", BF16, 1, 128)
    onesf = fixed("onesf", F32, 1, 128)
    dec = fixed("dec", F32, 1, 8)
    lg = fixed("lg", F32, 1, 8)
    tmp8 = fixed("tmp8", F32, 1, 8)
    iot_i = fixed("iot_i", I32, 1, 512)
    io_c1 = fixed("io_c1", F32, 1, 128)
    io_rc = fixed("io_rc", F32, 1, 128)
    io_mc = fixed("io_mc", F32, 1, 128)
    io_p = fixed("io_p", F32, 1, 1)
    a127 = fixed("a127", F32, 1, 1)
    p1 = fixed("p1", F32, 1, 1)
    AFt = fixed("AFt", F32, 4, 128)
    ABt = fixed("ABt", F32, 4, 128)
    DTt = fixed("DTt", F32, 4, 128)
    kdf = fixed("kdf", F32, 1, 4)
    kdb = fixed("kdb", F32, 1, 4)
    cdf = fixed("cdf", F32, 1, 4)
    cdb = fixed("cdb", F32, 1, 4)
    e1col = fixed("e1col", F32, 1, 4)
    nhalf = fixed("nhalf", F32, 1, 1)
    inv = fixed("inv", F32, 1, 1)
    gpre = fixed("gpre", F32, 1, 8)
    gfpre = fixed("gfpre", F32, 1, 8)
    gmem = fixed("gmem", F32, 1, 8)
    rgn = fixed("rgn", F32, 1, 8)
    psc = fixed("psc", F32, 1, 8)
    cw = fixed("cw", F32, 3, 44)
    cb = fixed("cb", F32, 1, 44)
    gpost = fixed("gpost", F32, 1, D)
    gfpost = fixed("gfpost", F32, 1, D)
    icf = fixed("icf", F32, 4, 8)
    icl = fixed("icl", F32, 4, 8)
    small = fixed("small", F32, 1, 64)
    dummy = {e: fixed(f"dummy_{e}", F32, 1, 4) for e in ("vector", "scalar", "gpsimd")}
    tmpA = fixed("tmpA", F32, 1, 128)
    csbuf = [fixed(f"csbuf{i}", F32, 2, T) for i in range(2)]
    tmpB = fixed("tmpB", F32, 1, 128)

    rem = nc.sbuf_bytes_remaining
    rem = rem() if callable(rem) else rem
    ARENA_BYTES = (rem - 2048) // PAGE * PAGE
    arena_t = sbt("arena", [128, ARENA_BYTES // 4], F32)
    AR = Arena(S, arena_t, ARENA_BYTES)

    pbanks = []
    for i in range(8):
        t = es.enter_context(nc.psum_tensor(f"bank{i}", [128, 512], F32))
        pbanks.append(t)
    half_res = [S.res(f"bankh{i}") for i in range(16)]
    half_free = [True] * 16
    half_stamp = [0] * 16
    stamp = [0]

    class Bank:
        def __init__(self, halves):
            self.halves = halves
            self.res = [half_res[h] for h in halves]
            t = pbanks[halves[0] // 2]
            if len(halves) == 2:
                base = t[:]
                self.f = View(base, 1, 512, self.res)
                self.h = View(base.bitcast(BF16), 1, 1024, self.res)
            else:
                o = 256 * (halves[0] % 2)
                base = t[:, o:o + 256]
                self.f = View(base, 1, 256, self.res)
                self.h = View(base.bitcast(BF16), 1, 512, self.res)

        def free(self):
            for h in self.halves:
                assert not half_free[h]
                half_free[h] = True
                stamp[0] += 1
                half_stamp[h] = stamp[0]

    def try_bank(half=False):
        best = None
        if half:
            for h in range(16):
                if half_free[h]:
                    key = (half_free[h ^ 1], half_stamp[h])
                    if best is None or key < best[0]:
                        best = (key, [h])
        else:
            for bnk in range(8):
                if half_free[2 * bnk] and half_free[2 * bnk + 1]:
                    key = max(half_stamp[2 * bnk], half_stamp[2 * bnk + 1])
                    if best is None or key < best[0]:
                        best = (key, [2 * bnk, 2 * bnk + 1])
        if best is None:
            return None
        for h in best[1]:
            half_free[h] = False
        return Bank(best[1])

    def BankNow(half=False):
        bk_ = try_bank(half)
        assert bk_ is not None, "no free PSUM bank"
        return bk_

    def gbanks(k):
        n = 0
        while True:
            nfree = sum(1 for b_ in range(8) if half_free[2 * b_] and half_free[2 * b_ + 1])
            if nfree >= k:
                return [try_bank(False) for _ in range(k)]
            n += 1
            assert n < 10000, "PSUM bank starvation"
            yield

    small_ctr = [0]
    small_res = [S.res(f"small{i}") for i in range(64)]

    class SmallV:
        def __init__(self, o, n):
            self.o = o
            self.n = n
            self.res = small_res[o:o + n]
            self.a = small.k(0, o, o + n)

        def sub(self, i):
            return small.k(0, self.o + i, self.o + i + 1)

    def sm(n=1):
        i = small_ctr[0]
        if i % 64 + n > 64:
            i = (i // 64 + 1) * 64
        small_ctr[0] = i + n
        return SmallV(i % 64, n)

    CONST = []

    def cvec(view, src, pattern, **kw):
        DMA("sync", view.all() if pattern is None else pattern, src, [], [view],
            allow_slow_non_contiguous=True, **kw)
        CONST.append(view)

    WB = [S.res(f"wb{i}") for i in range(NBLK)]

    def wbv(b):
        return wb_d[b].rearrange("p (k n) -> p k n", k=8)

    def wsrc(w, c0, c1):
        return w.rearrange("(k p) n -> p k n", p=128)[:, :, c0:c1]

    for i, b in enumerate(BLK["in"]):
        DMA("gpsimd", wbv(b), wsrc(w_in_d, 512 * i, 512 * i + 512), [], [WB[b]])
    for nm, w in (("ret_out", w_ret_out_d), ("pool_out", w_pool_out_d), ("mem_out", w_mem_out_d), ("o", w_o_d)):
        for i, b in enumerate(BLK[nm]):
            DMA("gpsimd", wbv(b), wsrc(w, 512 * i, 512 * i + 512), [], [WB[b]])
    b = BLK["poolw"][0]
    for g in range(4):
        DMA("gpsimd", wbv(b)[:, 2 * g:2 * g + 2, 0:256],
            pool_w_d[g].rearrange("(k p) n -> p k n", p=128), [], [WB[b]])
    for i, b in enumerate(BLK["up"]):
        DMA("gpsimd", wbv(b)[:, :, 0:256], wsrc(w_up_d, 256 * i, 256 * i + 256), [], [WB[b]])
        DMA("gpsimd", wbv(b)[:, :, 256:512], wsrc(w_up_d, DFF + 256 * i, DFF + 256 * i + 256), [], [WB[b]])
    for half in range(2):
        for kg in range(3):
            b = BLK["down"][half * 3 + kg]
            nk = min(8, NPAIR - 8 * kg)
            DMA("gpsimd", wbv(b)[:, 0:nk, :],
                w_down_d.rearrange("(k p) n -> p k n", p=128)[:, 8 * kg:8 * kg + nk, 512 * half:512 * half + 512],
                [], [WB[b]])
    for i, b in enumerate(BLK["memkv"]):
        DMA("gpsimd", wbv(b), wsrc(w_mem_kv_d, 512 * i, 512 * i + 512), [], [WB[b]])

    for v, src in ((gpre, g_mix_pre_d), (gfpre, g_ffn_pre_d), (gmem, g_mem_d), (rgn, ret_gn_d), (psc, pool_scale_d)):
        cvec(v, src.rearrange("(k p) -> p k", p=128), None)
    cvec(cw, conv_w_d.rearrange("t (c p) -> p t c", p=128), cw.ks(0, 3))
    cvec(cb, conv_b_d.rearrange("(c p) -> p c", p=128), None)

    def pbc(src, n):
        return AP(tensor=src.tensor, offset=src.offset, ap=[[0, 128], [1, n]])

    cvec(gpost, pbc(g_mix_post_d, D), None)
    cvec(gfpost, pbc(g_ffn_post_d, D), None)
    DMA("sync", dec.k(0, 0, 4), pbc(decay_fwd_d, 4), [], [dec])
    DMA("sync", dec.k(0, 4, 8), pbc(decay_bwd_d, 4), [], [dec])

    OP("gpsimd", "memset", [], [identf], ap=identf.all(), constant=0.0)
    OP("gpsimd", "memset", [], [onesf], ap=onesf.all(), constant=1.0)
    OP("gpsimd", "memset", [], [nhalf], ap=nhalf.all(), constant=-0.5)
    OP("gpsimd", "affine_select", [identf], [identf], out=identf.all(), in_=identf.all(), pattern=[[-1, 128]],
       compare_op=ALU.not_equal, fill=1.0, base=0, channel_multiplier=1)
    OP("vector", "tensor_copy", [identf], [ident], out=ident.all(), in_=identf.all())
    OP("vector", "tensor_copy", [onesf], [ones_bf], out=ones_bf.all(), in_=onesf.all())

    def iota_f(dst, pattern, base, cm, n):
        OP("gpsimd", "iota", [dst, iot_i], [iot_i], out=iot_i.k(0, 0, n), pattern=pattern, base=base, channel_multiplier=cm)
        OP("vector", "tensor_copy", [iot_i], [dst], out=dst.all(), in_=iot_i.k(0, 0, n))

    iota_f(io_c1, [[1, 128]], 1, 0, 128)
    iota_f(io_mc, [[-1, 128]], 0, 1, 128)
    iota_f(io_p, [[0, 1]], 0, 1, 1)
    OP("vector", "tensor_scalar", [io_c1], [io_rc], out=io_rc.all(), in0=io_c1.all(), scalar1=-1.0, scalar2=129.0,
       op0=ALU.mult, op1=ALU.add)
    OP("vector", "tensor_scalar", [io_p], [a127], out=a127.all(), in0=io_p.all(), scalar1=-1.0, scalar2=127.0,
       op0=ALU.mult, op1=ALU.add)
    OP("vector", "tensor_scalar", [io_p], [p1], out=p1.all(), in0=io_p.all(), scalar1=1.0, scalar2=None, op0=ALU.add)
    OP("scalar", "activation", [dec], [tmp8], out=tmp8.all(), in_=dec.all(), func=AF.Exp, scale=-1.0)
    OP("vector", "tensor_scalar", [tmp8], [tmp8], out=tmp8.all(), in0=tmp8.all(), scalar1=1.0, scalar2=None, op0=ALU.add)
    OP("scalar", "activation", [tmp8], [lg], out=lg.all(), in_=tmp8.all(), func=AF.Ln)
    OP("vector", "tensor_scalar", [lg], [lg], out=lg.all(), in0=lg.all(), scalar1=-1.0, scalar2=None, op0=ALU.mult)
    for h in range(4):
        lf = lg.k(0, h, h + 1)
        lb = lg.k(0, 4 + h, 5 + h)
        OP("scalar", "activation", [io_c1, lg], [AFt], out=AFt.k(h), in_=io_c1.all(), func=AF.Exp, scale=lf)
        OP("scalar", "activation", [io_rc, lg], [ABt], out=ABt.k(h), in_=io_rc.all(), func=AF.Exp, scale=lb)
        OP("scalar", "activation", [a127, lg], [kdf], out=kdf.k(0, h, h + 1), in_=a127.all(), func=AF.Exp, scale=lf)
        OP("scalar", "activation", [io_p, lg], [kdb], out=kdb.k(0, h, h + 1), in_=io_p.all(), func=AF.Exp, scale=lb)
        OP("scalar", "activation", [lg], [cdf], out=cdf.k(0, h, h + 1), in_=lf, func=AF.Exp, scale=128.0)
        OP("scalar", "activation", [lg], [cdb], out=cdb.k(0, h, h + 1), in_=lb, func=AF.Exp, scale=128.0)
        OP("vector", "tensor_scalar", [lg], [tmp8], out=tmp8.k(0, 0, 1), in0=lf, scalar1=-1.0, scalar2=None, op0=ALU.mult)
        OP("scalar", "activation", [p1, tmp8], [e1col], out=e1col.k(0, h, h + 1), in_=p1.all(), func=AF.Exp,
           scale=tmp8.k(0, 0, 1))
        OP("vector", "tensor_scalar", [io_mc, lg], [tmpA], out=tmpA.all(), in0=io_mc.all(), scalar1=lb, scalar2=None, op0=ALU.mult)
        OP("vector", "tensor_scalar", [io_c1, lg], [tmpB], out=tmpB.all(), in0=io_c1.all(), scalar1=lf, scalar2=None, op0=ALU.mult)
        OP("vector", "tensor_tensor", [tmpA, tmpB], [tmpA], out=tmpA.all(), in0=tmpA.all(), in1=tmpB.all(), op=ALU.subtract)
        OP("scalar", "activation", [tmpA], [tmpA], out=tmpA.all(), in_=tmpA.all(), func=AF.Exp)
        OP("gpsimd", "affine_select", [tmpA], [tmpA], out=tmpA.all(), in_=tmpA.all(), pattern=[[-1, 128]],
           compare_op=ALU.is_gt, fill=0.0, base=0, channel_multiplier=1)
        OP("vector", "tensor_scalar", [onesf, e1col], [tmpB], out=tmpB.all(), in0=onesf.all(), scalar1=e1col.k(0, h, h + 1),
           scalar2=None, op0=ALU.mult)
        OP("gpsimd", "affine_select", [tmpB], [tmpB], out=tmpB.all(), in_=tmpB.all(), pattern=[[1, 128]],
           compare_op=ALU.is_ge, fill=0.0, base=0, channel_multiplier=-1)
        OP("vector", "tensor_tensor", [tmpA, tmpB], [tmpA], out=tmpA.all(), in0=tmpA.all(), in1=tmpB.all(), op=ALU.add)
        OP("vector", "tensor_scalar", [tmpA], [DTt], out=DTt.k(h), in0=tmpA.all(), scalar1=1.0 / 16, scalar2=None, op0=ALU.mult)
    OP("vector", "tensor_scalar", [kdf], [kdf], out=kdf.all(), in0=kdf.all(), scalar1=1.0 / 16, scalar2=None, op0=ALU.mult)
    OP("vector", "tensor_scalar", [kdb], [kdb], out=kdb.all(), in0=kdb.all(), scalar1=1.0 / 16, scalar2=None, op0=ALU.mult)
    OP("gpsimd", "iota", [iot_i], [iot_i], out=iot_i.k(0, 0, 8), pattern=[[1, 8]], base=0, channel_multiplier=0)
    OP("vector", "tensor_copy", [iot_i], [tmpB], out=tmpB.k(0, 0, 8), in_=iot_i.k(0, 0, 8))
    for gi, w in enumerate((2, 4, 8, 16)):
        OP("vector", "tensor_scalar", [tmpB], [icf], out=icf.k(gi), in0=tmpB.k(0, 0, 8), scalar1=float(w // 2), scalar2=float(w),
           op0=ALU.add, op1=ALU.min)
        OP("vector", "tensor_scalar", [tmpB], [icl], out=icl.k(gi), in0=tmpB.k(0, 0, 8), scalar1=-1.0, scalar2=float(8 + w // 2),
           op0=ALU.mult, op1=ALU.add)
        OP("vector", "tensor_scalar", [icl], [icl], out=icl.k(gi), in0=icl.k(gi), scalar1=float(w), scalar2=None, op0=ALU.min)
    OP("vector", "reciprocal", [icf], [icf], out=icf.all(), in_=icf.all())
    OP("vector", "reciprocal", [icl], [icl], out=icl.all(), in_=icl.all())
    OP("scalar", "activation", [io_p], [inv], out=inv.all(), in_=io_p.all(), func=AF.Exp, scale=-math.log(10000.0) / 128.0)
    CS = S.res("cs_scr")
    rt = [AR.alloc(f"rt{i}", F32, 1, 512) for i in range(4)]
    rti = AR.alloc("rti", I32, 1, 512)
    for blk in range(SMAX // 512 if SMAX % 512 == 0 else SMAX // 512 + 1):
        n = min(512, SMAX - blk * 512)
        OP("gpsimd", "iota", [iot_i], [iot_i], out=iot_i.k(0, 0, n), pattern=[[1, n]], base=blk * 512, channel_multiplier=0)
        OP("vector", "tensor_copy", [iot_i], [rt[0]], out=rt[0].k(0, 0, n), in_=iot_i.k(0, 0, n))
        OP("vector", "tensor_scalar", [rt[0], inv], [rt[0]], out=rt[0].k(0, 0, n), in0=rt[0].k(0, 0, n), scalar1=inv.all(),
           scalar2=None, op0=ALU.mult)
        for which in range(2):
            a = rt[1 + which]
            OP("vector", "tensor_scalar", [rt[0]], [a], out=a.k(0, 0, n), in0=rt[0].k(0, 0, n),
               scalar1=(math.pi / 2 if which == 0 else 0.0), scalar2=None, op0=ALU.add)
            OP("vector", "tensor_scalar", [a], [rt[3]], out=rt[3].k(0, 0, n), in0=a.k(0, 0, n), scalar1=1.0 / TWO_PI,
               scalar2=None, op0=ALU.mult)
            OP("vector", "tensor_copy", [rt[3]], [rti], out=rti.k(0, 0, n), in_=rt[3].k(0, 0, n))
            OP("vector", "tensor_copy", [rti], [rt[3]], out=rt[3].k(0, 0, n), in_=rti.k(0, 0, n))
            OP("vector", "scalar_tensor_tensor", [rt[3], a], [a], out=a.k(0, 0, n), in0=rt[3].k(0, 0, n), scalar=-TWO_PI,
               in1=a.k(0, 0, n), op0=ALU.mult, op1=ALU.add)
            OP("vector", "tensor_scalar", [a], [a], out=a.k(0, 0, n), in0=a.k(0, 0, n), scalar1=-3.1415925, scalar2=3.1415925,
               op0=ALU.max, op1=ALU.min)
            OP("scalar", "activation", [a], [a], out=a.k(0, 0, n), in_=a.k(0, 0, n), func=AF.Sin)
            DMA("sync", cs_d[which][:, blk * 512:blk * 512 + n], a.k(0, 0, n), [a], [CS])
    AR.release(*rt, rti)

    CONST += [ident, ones_bf, onesf, AFt, ABt, DTt, kdf, kdb, cdf, cdb, nhalf, icf, icl]
    for e in ("vector", "scalar", "gpsimd"):
        if e == "scalar":
            OP(e, "activation", CONST, [dummy[e]], out=dummy[e].k(0, 0, 1), in_=nhalf.all(), func=AF.Copy)
        else:
            OP(e, "memset", CONST, [dummy[e]], ap=dummy[e].all(), constant=0.0)
    fb = BankNow()
    OP("tensor", "transpose", CONST, [fb], out=fb.h.k(0, 0, 128), in_=ident.all(), identity=ident.all())
    fb.free()

    wq = []
    inflight = []
    wslot = [0]

    def pump():
        while len(inflight) < R_W and wq:
            b = wq.pop(0)
            v = wring[wslot[0] % R_W]
            wslot[0] += 1
            DMA("sync", v.all(), wb_d[b], [WB[b]], [v])
            inflight.append((b, v))

    def announce(blocks):
        wq.extend(blocks)
        pump()

    def wuse(b):
        bb, v = inflight[0]
        assert bb == b, (bb, b)
        return v

    def wdone():
        inflight.pop(0)
        pump()

    def rstd_from(ss_list, n):
        r = sm()
        if len(ss_list) == 2:
            OP("vector", "tensor_tensor", ss_list, [r], out=r.a, in0=ss_list[0].a, in1=ss_list[1].a, op=ALU.add)
            src = r
        else:
            src = ss_list[0]
        OP("vector", "tensor_scalar", [src], [r], out=r.a, in0=src.a, scalar1=1.0 / n, scalar2=EPS, op0=ALU.mult, op1=ALU.add)
        r2 = sm()
        OP("gpsimd", "tensor_tensor", [r], [r2], out=r2.a, in0=r.a, in1=nhalf.all(), op=ALU.pow)
        return r2

    def norm_transpose(xin_ap, xin_res, gvec, dstT, col0):
        PH[0] = f"{PH[0].split('/')[0]}/norm"
        junk = AR.alloc("junk", BF16, 1, D)
        xn = AR.alloc("xn", BF16, 1, D)
        ss = sm()
        OP("scalar", "activation", [xin_res], [junk, ss], out=junk.all(), in_=xin_ap, func=AF.Square, accum_out=ss.a)
        r = rstd_from([ss], D)
        OP("scalar", "activation", [xin_res, r], [xn], out=xn.all(), in_=xin_ap, func=AF.Copy, scale=r.a)
        bk = BankNow()
        for k in range(8):
            OP("tensor", "transpose", [xn], [bk], out=bk.h.k(0, 128 * k, 128 * k + 128), in_=xn.k(0, 128 * k, 128 * k + 128),
               identity=ident.all())
        gb = AP(tensor=gvec.base.tensor, offset=gvec.base.offset, ap=[gvec.p, [1, 8], [0, 128]])
        OP("vector", "tensor_tensor", [bk], [dstT], out=dstT.ks(0, 8, col0, col0 + 128),
           in0=bk.h.ap([[128, 8], [1, 128]]), in1=gb, op=ALU.mult)
        bk.free()
        AR.release(junk, xn)

    def projA(blkname, idx, rhsT, ncols, evac, nk=8):
        PH[0] = f"{PH[0].split('/')[0]}/A_{blkname}{idx}"
        b = BLK[blkname][idx]
        w = wuse(b)
        for oc in range(4):
            bk = BankNow()
            for k in range(nk):
                OP("tensor", "matmul", [w, rhsT], [bk], out=bk.f.k(0, 0, ncols), lhsT=w.k(k, 128 * oc, 128 * oc + 128),
                   rhs=rhsT.k(k, 0, ncols), start=(k == 0), stop=(k == nk - 1))
            if not evac(oc, bk):
                bk.free()
        wdone()

    def projB(blkname, idx, lhsT_view, ngroups, evac):
        PH[0] = f"{PH[0].split('/')[0]}/B_{blkname}{idx}"
        b = BLK[blkname][idx]
        w = wuse(b)
        for g in range(ngroups):
            bk = BankNow()
            for k in range(8):
                OP("tensor", "matmul", [w, lhsT_view], [bk], out=bk.f.all(), lhsT=lhsT_view.k(k, 128 * g, 128 * g + 128),
                   rhs=w.k(k), start=(k == 0), stop=(k == 7))
            evac(g, bk)
            bk.free()
        wdone()

    def rotary_pair(b1, b2, cs, out1, out2, post=None):
        t = [AR.alloc(f"rot{i}", F32, 1, T) for i in range(4)]
        OP("vector", "tensor_tensor", [b1, cs], [t[0]], out=t[0].all(), in0=b1.f.k(0, 0, T), in1=cs.k(0), op=ALU.mult)
        OP("vector", "tensor_tensor", [b2, cs], [t[1]], out=t[1].all(), in0=b2.f.k(0, 0, T), in1=cs.k(1), op=ALU.mult)
        OP("vector", "tensor_tensor", [b1, cs], [t[2]], out=t[2].all(), in0=b1.f.k(0, 0, T), in1=cs.k(1), op=ALU.mult)
        OP("vector", "tensor_tensor", [b2, cs], [t[3]], out=t[3].all(), in0=b2.f.k(0, 0, T), in1=cs.k(0), op=ALU.mult)
        out1(t[0], t[1], ALU.subtract)
        out2(t[2], t[3], ALU.add)
        AR.release(*t)

    def _tick(act):
        for g_ in list(act):
            try:
                next(g_)
            except StopIteration:
                act.remove(g_)

    def skewed(makers):
        act = []
        for mk in makers:
            act.append(mk())
            _tick(act)
            yield
        while act:
            _tick(act)
            yield

    def run_threads(threads):
        act = list(threads)
        while act:
            _tick(act)

    tok0 = 0
    ch0 = 0
    for si, SL in enumerate(seq_lens):
        NT = SL // T
        NCH = SL // 128
        SBR = [S.res(f"sbs{si}_{c}") for c in range(NCH)]

        def xrows(t0, n=128):
            return x_d[tok0 + t0:tok0 + t0 + n, :]

        PH[0] = "memkv"
        kmT = AR.alloc("kmT", BF16, 8, NMEM)
        vm = AR.alloc("vm", BF16, 2, D)
        mhT = AR.alloc("mhT", BF16, 8, NMEM)
        announce(BLK["memkv"])
        for g in range(2):
            xm = AR.alloc("xm", F32, 1, D)
            DMA("sync", xm.all(), mem_d[si * NMEM + 128 * g:si * NMEM + 128 * g + 128, :], [], [xm])
            norm_transpose(xm.all(), xm, gmem, mhT, 128 * g)
            AR.release(xm)
        for i in range(2):
            def ev(oc, bk, i=i):
                OP("scalar", "activation", [bk], [kmT], out=kmT.k(4 * i + oc), in_=bk.f.k(0, 0, NMEM), func=AF.Copy)
            projA("memkv", i, mhT, NMEM, ev)
        for i in range(2):
            def ev(g, bk, i=i):
                OP("scalar", "activation", [bk], [vm], out=vm.k(g, 512 * i, 512 * i + 512), in_=bk.f.all(), func=AF.Copy)
            projB("memkv", 2 + i, mhT, 2, ev)
        AR.release(mhT)


        T1 = max(t_ for t_ in (512, 256) if SL % t_ == 0 and t_ >= T)
        G1 = T1 // 128
        RR = T1 // T
        NT1 = SL // T1
        S32 = AR.alloc("S32b", F32, 4, 512)
        S32r = [S.res(f"S32b_{h}") for h in range(4)]
        KTS = [S.res(f"kts{si}_{t}") for t in range(NT)]
        VTS = [S.res(f"vts{si}_{t}") for t in range(NT)]
        tile0 = tok0 // T
        Sbf = [AR.alloc(f"Sbfb{i}", BF16, 4, 512) for i in range(2)]
        OP("gpsimd", "memset", [], [S32] + S32r, ap=S32.all(), constant=0.0)
        OP("gpsimd", "memset", [], [Sbf[0]], ap=Sbf[0].all(), constant=0.0)
        p1_blocks = [BLK["in"][i] for i in (2, 3, 4, 5)]
        tiles1 = list(reversed(range(NT1)))
        xorder = [(t1, g) for t1 in tiles1 for g in range(G1)]
        xfifo = []
        xnext = [0]

        def xpump(depth=6):
            while len(xfifo) < depth and xnext[0] < len(xorder):
                t1, g = xorder[xnext[0]]
                xnext[0] += 1
                xp = AR.alloc("xp", F32, 1, D)
                DMA("sync", xp.all(), xrows(t1 * T1 + 128 * g), [], [xp])
                xfifo.append(xp)

        def front(t1):
            PH[0] = "p1"
            announce(p1_blocks)
            hT1 = AR.alloc("hT1", BF16, 8, T1)
            for g in range(G1):
                xpump()
                xp = xfifo.pop(0)
                norm_transpose(xp.all(), xp, gpre, hT1, 128 * g)
                AR.release(xp)
                xpump()
            cs1 = AR.alloc("cs1", F32, 2, T1)
            for which in range(2):
                DMA("sync", cs1.k(which), cs_d[which][:, t1 * T1:t1 * T1 + T1], [CS], [cs1])
            kT1 = [AR.alloc(f"kT1_{r}", BF16, 8, T) for r in range(RR)]
            PH[0] = "p1/k"
            for i in range(2):
                w = wuse(BLK["in"][2 + i])
                for pr_ in range(2):
                    bks = []
                    for oc in (2 * pr_, 2 * pr_ + 1):
                        bk = BankNow()
                        for k in range(8):
                            OP("tensor", "matmul", [w, hT1], [bk], out=bk.f.k(0, 0, T1), lhsT=w.k(k, 128 * oc, 128 * oc + 128),
                               rhs=hT1.k(k), start=(k == 0), stop=(k == 7))
                        bks.append(bk)
                    c = 4 * i + 2 * pr_
                    t = [AR.alloc(f"rot{q}", F32, 1, T1) for q in range(4)]
                    OP("vector", "tensor_tensor", [bks[0], cs1], [t[0]], out=t[0].all(), in0=bks[0].f.k(0, 0, T1), in1=cs1.k(0), op=ALU.mult)
                    OP("vector", "tensor_tensor", [bks[1], cs1], [t[1]], out=t[1].all(), in0=bks[1].f.k(0, 0, T1), in1=cs1.k(1), op=ALU.mult)
                    OP("vector", "tensor_tensor", [bks[0], cs1], [t[2]], out=t[2].all(), in0=bks[0].f.k(0, 0, T1), in1=cs1.k(1), op=ALU.mult)
                    OP("vector", "tensor_tensor", [bks[1], cs1], [t[3]], out=t[3].all(), in0=bks[1].f.k(0, 0, T1), in1=cs1.k(0), op=ALU.mult)
                    bks[0].free()
                    bks[1].free()
                    for r in range(RR):
                        OP("gpsimd", "tensor_tensor", [t[0], t[1]], [kT1[r]], out=kT1[r].k(c), in0=t[0].k(0, r * T, r * T + T),
                           in1=t[1].k(0, r * T, r * T + T), op=ALU.subtract)
                        OP("gpsimd", "tensor_tensor", [t[2], t[3]], [kT1[r]], out=kT1[r].k(c + 1), in0=t[2].k(0, r * T, r * T + T),
                           in1=t[3].k(0, r * T, r * T + T), op=ALU.add)
                    AR.release(*t)
                wdone()
            AR.release(cs1)
            for r in range(RR):
                DMA("sync", kts_d[tile0 + t1 * RR + r], kT1[r].all(), [kT1[r]], [KTS[t1 * RR + r]])
            PH[0] = "p1/v"
            vt1 = AR.alloc("vt1", BF16, G1, D)
            for i in range(2):
                w = wuse(BLK["in"][4 + i])
                for g in range(G1):
                    bk = BankNow()
                    for k in range(8):
                        OP("tensor", "matmul", [w, hT1], [bk], out=bk.f.all(), lhsT=hT1.k(k, 128 * g, 128 * g + 128),
                           rhs=w.k(k), start=(k == 0), stop=(k == 7))
                    OP("scalar", "activation", [bk], [vt1], out=vt1.k(g, 512 * i, 512 * i + 512), in_=bk.f.all(), func=AF.Copy)
                    bk.free()
                wdone()
            AR.release(hT1)
            for r in range(RR):
                DMA("sync", vts_d[tile0 + t1 * RR + r], vt1.ap([[1, G * D]], r * G * D), [vt1], [VTS[t1 * RR + r]])
            PH[0] = "p1/kb"
            kb1 = AR.alloc("kb1", BF16, G1, D)
            for g in range(G1):
                bk = BankNow()
                for c in range(8):
                    OP("tensor", "transpose", [kT1[g // G]], [bk], out=bk.h.k(0, 128 * c, 128 * c + 128), in_=kT1[g // G].k(c, 128 * (g % G), 128 * (g % G) + 128),
                       identity=ident.all())
                for h in range(4):
                    OP("vector", "tensor_scalar", [bk], [kb1], out=kb1.k(g, 256 * h, 256 * h + 256), in0=bk.h.k(0, 256 * h, 256 * h + 256),
                       scalar1=kdb.k(0, h, h + 1), scalar2=None, op0=ALU.mult)
                bk.free()
            AR.release(*kT1)
            return (t1, kb1, vt1)

        def back(fr):
            t1, kb1, vt1 = fr
            PH[0] = "p1/state"
            for g in reversed(range(G1)):
                gc = t1 * G1 + g
                par_ = (NCH - 1 - gc) % 2
                DMA("sync", sb_d[ch0 + gc], Sbf[par_].all(), [Sbf[par_]], [SBR[gc]])
                for h in range(4):
                    bk = BankNow()
                    for dd in range(2):
                        OP("tensor", "matmul", [kb1, vt1], [bk], out=bk.f.k(0, 256 * dd, 256 * dd + 256),
                           lhsT=kb1.k(g, 256 * h + 128 * dd, 256 * h + 128 * dd + 128), rhs=vt1.k(g, 256 * h, 256 * h + 256),
                           start=True, stop=True)
                    OP("vector", "scalar_tensor_tensor", [S32r[h], bk], [S32r[h]], out=S32.k(h), in0=S32.k(h), scalar=cdb.k(0, h, h + 1),
                       in1=bk.f.all(), op0=ALU.mult, op1=ALU.add)
                    bk.free()
                    OP("scalar", "activation", [S32r[h]], [Sbf[1 - par_]], out=Sbf[1 - par_].k(h), in_=S32.k(h), func=AF.Copy)
            AR.release(kb1, vt1)

        prev = None
        for t1 in tiles1:
            cur = front(t1)
            if prev is not None:
                back(prev)
            prev = cur
        back(prev)
        assert not xfifo
        AR.release(S32, *Sbf)

        NX = 2 * G + 1
        xring = [AR.alloc(f"xr{i}", F32, 1, D) for i in range(NX)]
        pring = AR.alloc("pring", F32, 8, T + 16)
        wa_p = AR.alloc("wa", F32, 2, T + 16)
        Gr = AR.alloc("Gring", BF16, NPAIR, T + 128)
        carry = AR.alloc("h2carry", BF16, 8, 2)
        S32 = AR.alloc("S32f", F32, 4, 512)
        S32res = [S.res(f"S32f_{h}") for h in range(4)]
        _sbf1 = AR.alloc("Sbff", BF16, 4, 512)
        Sbf = [_sbf1, _sbf1]
        SbfR = [S.res(f"SbfR{h}") for h in range(4)]
        UW = T + 3
        for v in (pring, carry):
            OP("gpsimd", "memset", [], [v], ap=v.all(), constant=0.0)
        OP("gpsimd", "memset", [], [_sbf1] + SbfR, ap=_sbf1.all(), constant=0.0)
        OP("gpsimd", "memset", [], [S32] + S32res, ap=S32.all(), constant=0.0)
        par = 0

        def load_x(ti):
            for g in range(G):
                v = xring[(ti * G + g) % NX]
                DMA("sync", v.all(), xrows(ti * T + 128 * g), [], [v])

        def step_blocks(is_last):
            return ([BLK["in"][i] for i in (8, 9, 10, 11, 6, 7)]
                    + BLK["poolw"]
                    + [BLK["in"][i] for i in (12, 13, 14, 15, 16, 17)]
                    + [BLK["pool_out"][0], BLK["mem_out"][0], BLK["ret_out"][0],
                       BLK["pool_out"][1], BLK["mem_out"][1], BLK["ret_out"][1]]
                    + BLK["o"] + BLK["up"]
                    + ([] if is_last else [BLK["in"][i] for i in (0, 1)]) + BLK["down"])

        def ffn_down(groups, x1slots, ytoks):
            ng = len(groups)
            y2 = [AR.alloc(f"y2_{i}", F32, 1, D) for i in range(ng)]
            for half in range(2):
                bks = [BankNow() for _ in range(ng)]
                for kg in range(3):
                    b = BLK["down"][half * 3 + kg]
                    w = wuse(b)
                    nk = min(8, NPAIR - 8 * kg)
                    for gi, gcol in enumerate(groups):
                        for k in range(nk):
                            kk = 8 * kg + k
                            OP("tensor", "matmul", [w, Gr], [bks[gi]], out=bks[gi].f.all(),
                               lhsT=Gr.k(kk, 128 * gcol, 128 * gcol + 128), rhs=w.k(k), start=(kk == 0), stop=(kk == NPAIR - 1))
                    wdone()
                for gi in range(ng):
                    OP("scalar", "activation", [bks[gi]], [y2[gi]], out=y2[gi].k(0, 512 * half, 512 * half + 512),
                       in_=bks[gi].f.all(), func=AF.Copy)
                    bks[gi].free()
            for gi in range(ng):
                junk = AR.alloc("junk2", BF16, 1, D)
                ss = sm()
                OP("scalar", "activation", [y2[gi]], [junk, ss], out=junk.all(), in_=y2[gi].all(), func=AF.Square, accum_out=ss.a)
                r = rstd_from([ss], D)
                OP("vector", "scalar_tensor_tensor", [y2[gi], r], [y2[gi]], out=y2[gi].all(), in0=y2[gi].all(), scalar=r.a,
                   in1=gfpost.all(), op0=ALU.mult, op1=ALU.mult)
                xs = x1slots[gi]
                OP("gpsimd", "tensor_tensor", [y2[gi], xs], [xs], out=xs.all(), in0=y2[gi].all(), in1=xs.all(), op=ALU.add)
                DMA("gpsimd", y_d[tok0 + ytoks[gi]:tok0 + ytoks[gi] + 128, :], xs.all(), [xs], [])
                AR.release(junk)
            AR.release(*y2)

        def head(ti, hT):
            PH[0] = "p2h"
            cs = csbuf[ti % 2]
            for which in range(2):
                DMA("sync", cs.k(which), cs_d[which][:, ti * T:ti * T + T], [CS], [cs])
            kT = AR.alloc("kT", BF16, 8, T, rot=True)
            DMA("sync", kT.all(), kts_d[tile0 + ti], [KTS[ti]], [kT])
            vt = AR.alloc("vt", BF16, G, D, rot=True)
            DMA("sync", vt.all(), vts_d[tile0 + ti], [VTS[ti]], [vt])
            qf = AR.alloc("qf", BF16, 8, T)
            qb = AR.alloc("qb", BF16, 8, T)
            for i in range(2):
                held = {}

                def ev(oc, bk, i=i, held=held):
                    if oc % 2 == 0:
                        held[0] = bk
                        return True
                    c = 4 * i + oc - 1
                    h = c // 2
                    o = [AR.alloc("qo0", F32, 1, T), AR.alloc("qo1", F32, 1, T)]

                    def mk(j):
                        def f(a, b_, op, j=j):
                            OP("gpsimd", "tensor_tensor", [a, b_], [o[j]], out=o[j].all(), in0=a.all(), in1=b_.all(), op=op)
                            src = o[j].ap([[128, G], [1, 128]])
                            for tab, dst in ((AFt, qf), (ABt, qb)):
                                tb = AP(tensor=tab.base.tensor, offset=tab.base.offset + 128 * h, ap=[tab.p, [0, G], [1, 128]])
                                OP("vector", "tensor_tensor", [o[j]], [dst], out=dst.ap([[128, G], [1, 128]], (c + j) * T),
                                   in0=src, in1=tb, op=ALU.mult)
                        return f
                    rotary_pair(held[0], bk, cs, mk(0), mk(1))
                    held[0].free()
                    AR.release(*o)
                projA("in", 0 + i, hT, T, ev)
            return dict(kT=kT, vt=vt, qf=qf, qb=qb, hT=hT)

        load_x(0)
        announce([BLK["in"][i] for i in (0, 1)])
        announce(step_blocks(NT == 1))
        hT_next = [None]
        hT0 = AR.alloc("hT", BF16, 8, T)
        for g in range(G):
            norm_transpose(xring[g % NX].all(), xring[g % NX], gpre, hT0, 128 * g)
        HD = head(0, hT0)
        for ti in range(NT):
            first = ti == 0
            last = ti == NT - 1
            PH[0] = "p2"
            if not last:
                load_x(ti + 1)
                announce(step_blocks(ti + 1 == NT - 1))
            xs = [xring[(ti * G + g) % NX] for g in range(G)]
            Sb = [AR.alloc(f"Sb{g}", BF16, 4, 512) for g in range(G)]
            for g in range(G):
                DMA("sync", Sb[g].all(), sb_d[ch0 + ti * G + g], [SBR[ti * G + g]], [Sb[g]])
            kT, vt, qf, qb, hT = HD["kT"], HD["vt"], HD["qf"], HD["qb"], HD["hT"]
            if not last:
                PH[0] = "p2pre"
                hT_next[0] = AR.alloc("hT", BF16, 8, T)
                for g in range(G):
                    xn_ = xring[((ti + 1) * G + g) % NX]
                    norm_transpose(xn_.all(), xn_, gpre, hT_next[0], 128 * g)
            PH[0] = "p2"
            for i in range(2):
                PH[0] = f"p2/A_in{8 + i}"
                w = wuse(BLK["in"][8 + i])
                for oc in range(4):
                    bk = BankNow()
                    for k in range(8):
                        OP("tensor", "matmul", [w, hT], [bk], out=bk.f.k(0, 0, T), lhsT=w.k(k, 128 * oc, 128 * oc + 128),
                           rhs=hT.k(k), start=(k == 0), stop=(k == 7))
                    nw = T
                    if not last:
                        for k in range(8):
                            OP("tensor", "matmul", [w, hT_next[0]], [bk], out=bk.f.k(0, T, T + 8), lhsT=w.k(k, 128 * oc, 128 * oc + 128),
                               rhs=hT_next[0].k(k, 0, 8), start=(k == 0), stop=(k == 7))
                        nw = T + 8
                    OP("scalar", "activation", [bk], [pring], out=pring.k(4 * i + oc, 8, 8 + nw), in_=bk.f.k(0, 0, nw), func=AF.Copy)
                    bk.free()
                wdone()
            mq = AR.alloc("mq", BF16, 8, T)
            for i in range(2):
                def ev(oc, bk, i=i):
                    OP("scalar", "activation", [bk], [mq], out=mq.k(4 * i + oc), in_=bk.f.k(0, 0, T), func=AF.Copy)
                projA("in", 10 + i, hT, T, ev)
            PH[0] = "p2"
            silu = AR.alloc("silu", BF16, 8, T)
            for i in range(2):
                def ev(oc, bk, i=i):
                    OP("scalar", "activation", [bk], [silu], out=silu.k(4 * i + oc), in_=bk.f.k(0, 0, T), func=AF.Silu)
                projA("in", 6 + i, hT, T, ev)
            PH[0] = "p2/kf_tr"
            kf = AR.alloc("kf", BF16, G, D)
            for g in range(G):
                bk = BankNow()
                for c in range(8):
                    OP("tensor", "transpose", [kT], [bk], out=bk.h.k(0, 128 * c, 128 * c + 128), in_=kT.k(c, 128 * g, 128 * g + 128),
                       identity=ident.all())
                for h in range(4):
                    OP("vector", "tensor_scalar", [bk], [kf], out=kf.k(g, 256 * h, 256 * h + 256), in0=bk.h.k(0, 256 * h, 256 * h + 256),
                       scalar1=kdf.k(0, h, h + 1), scalar2=None, op0=ALU.mult)
                bk.free()
            yn = AR.alloc("yn", BF16, G, D)
            PTs = {}

            def scores_gen(g):
                c0, c1 = 128 * g, 128 * g + 128
                (bs,) = yield from gbanks(1)
                PH[0] = "p2/ret_sc"
                for h in range(4):
                    for dd in range(2):
                        OP("tensor", "matmul", [kT, qf], [bs], out=bs.f.k(0, 128 * h, 128 * h + 128), lhsT=kT.k(2 * h + dd, c0, c1),
                           rhs=qf.k(2 * h + dd, c0, c1), start=(dd == 0), stop=(dd == 1))
                PT = AR.alloc("PT", BF16, 4, 128)
                PTs[g] = PT
                OP("vector", "tensor_tensor", [bs], [PT], out=PT.all(), in0=bs.f.all(), in1=DTt.all(), op=ALU.mult)
                bs.free()
                yield

            def head_gen(g, h):
                c0, c1 = 128 * g, 128 * g + 128
                pr_ = (ti * G + g) % 2
                PT = PTs[g]
                by, bk = yield from gbanks(2)
                PH[0] = "p2/ret_hd"
                yo = by.f.k(0, 0, 256)
                OP("tensor", "matmul", [PT, vt], [by], out=yo, lhsT=PT.k(h), rhs=vt.k(g, 256 * h, 256 * h + 256), start=True, stop=False)
                for dd in range(2):
                    OP("tensor", "matmul", [qf, SbfR[h]], [by], out=yo, lhsT=qf.k(2 * h + dd, c0, c1),
                       rhs=Sbf[pr_].k(h, 256 * dd, 256 * dd + 256), start=False, stop=False)
                for dd in range(2):
                    OP("tensor", "matmul", [qb, Sb[g]], [by], out=yo, lhsT=qb.k(2 * h + dd, c0, c1),
                       rhs=Sb[g].k(h, 256 * dd, 256 * dd + 256), start=False, stop=(dd == 1))
                for dd in range(2):
                    OP("tensor", "matmul", [kf, vt], [bk], out=bk.f.k(0, 256 * dd, 256 * dd + 256),
                       lhsT=kf.k(g, 256 * h + 128 * dd, 256 * h + 128 * dd + 128), rhs=vt.k(g, 256 * h, 256 * h + 256),
                       start=True, stop=True)
                yield
                st = sm(6)
                mv = sm(2)
                OP("vector", "bn_stats", [by], [st], out=st.a, in_=yo)
                OP("vector", "bn_aggr", [st], [mv], out=mv.a, in_=st.a)
                rs = sm()
                OP("vector", "tensor_scalar", [mv], [rs], out=rs.a, in0=mv.sub(1), scalar1=EPS, scalar2=None, op0=ALU.add)
                OP("vector", "scalar_tensor_tensor", [S32res[h], bk], [S32res[h]], out=S32.k(h), in0=S32.k(h), scalar=cdf.k(0, h, h + 1),
                   in1=bk.f.all(), op0=ALU.mult, op1=ALU.add)
                bk.free()
                yield
                rs2 = sm()
                OP("gpsimd", "tensor_tensor", [rs], [rs2], out=rs2.a, in0=rs.a, in1=nhalf.all(), op=ALU.pow)
                OP("scalar", "activation", [S32res[h]], [SbfR[h]], out=Sbf[1 - pr_].k(h), in_=S32.k(h), func=AF.Copy)
                yield
                OP("vector", "tensor_scalar", [by, mv, rs2], [yn], out=yn.k(g, 256 * h, 256 * h + 256), in0=yo, scalar1=mv.sub(0), scalar2=rs2.a,
                   op0=ALU.subtract, op1=ALU.mult)
                by.free()
                if h == 3:
                    AR.release(PT)
                yield

            def ret_thread():
                mk = []
                for g in range(G):
                    mk.append(lambda g=g: scores_gen(g))
                    for h in range(4):
                        mk.append(lambda g=g, h=h: head_gen(g, h))
                yield from skewed(mk)

            dsl = AR.alloc("dsl", BF16, 8, T)
            poolp = AR.alloc("poolp", BF16, 8, T)

            def pool_thread():
                if last:
                    OP("gpsimd", "memset", [], [pring], ap=pring.ks(0, 8, T + 8, T + 16), constant=0.0)
                W = T + 16
                for gi, w in enumerate((2, 4, 8, 16)):
                    wa = wa_p
                    sk = 2 * gi
                    ln = W
                    step = 1
                    cur = None
                    bufs = [wa, wa]
                    bi = 0
                    while step < w:
                        ln2 = ln - step
                        dst = bufs[bi]
                        if cur is None:
                            in0 = pring.ks(sk, sk + 2, 0, ln2)
                            in1 = pring.ks(sk, sk + 2, step, step + ln2)
                            rd = [pring]
                        else:
                            in0 = cur.ks(0, 2, 0, ln2)
                            in1 = cur.ks(0, 2, step, step + ln2)
                            rd = [cur]
                        OP("gpsimd", "tensor_tensor", rd, [dst], out=dst.ks(0, 2, 0, ln2), in0=in0, in1=in1, op=ALU.add)
                        cur = dst
                        bi = 1 - bi
                        ln = ln2
                        step *= 2
                    yield
                    o = 8 - w // 2
                    OP("vector", "scalar_tensor_tensor", [cur, pring], [dsl], out=dsl.ks(sk, sk + 2), in0=cur.ks(0, 2, o, o + T),
                       scalar=1.0 / w, in1=pring.ks(sk, sk + 2, 8, 8 + T), op0=ALU.mult, op1=ALU.subtract)
                    if first or last:
                        e0 = 0 if first else T - 8
                        ic = icf if first else icl
                        tt = AR.alloc("edge", F32, 2, 8)
                        icb = AP(tensor=ic.base.tensor, offset=ic.base.offset + 8 * gi, ap=[ic.p, [0, 2], [1, 8]])
                        OP("gpsimd", "tensor_tensor", [cur], [tt], out=tt.ks(0, 2), in0=cur.ks(0, 2, o + e0, o + e0 + 8), in1=icb, op=ALU.mult)
                        OP("gpsimd", "tensor_tensor", [tt, pring], [dsl], out=dsl.ks(sk, sk + 2, e0, e0 + 8), in0=tt.ks(0, 2),
                           in1=pring.ks(sk, sk + 2, 8 + e0, 16 + e0), op=ALU.subtract)
                        AR.release(tt)
                    yield
                OP("gpsimd", "tensor_copy", [pring], [pring], out=pring.ks(0, 8, 0, 8), in_=pring.ks(0, 8, T, T + 8))
                w = wuse(BLK["poolw"][0])
                for gi in range(4):
                    for oc in range(2):
                        (bk,) = yield from gbanks(1)
                        PH[0] = "p2/pool"
                        for k in range(2):
                            OP("tensor", "matmul", [w, dsl], [bk], out=bk.f.k(0, 0, T), lhsT=w.k(2 * gi + k, 128 * oc, 128 * oc + 128),
                               rhs=dsl.k(2 * gi + k), start=(k == 0), stop=(k == 1))
                        c = 2 * gi + oc
                        OP("scalar", "activation", [bk], [poolp], out=poolp.k(c), in_=bk.f.k(0, 0, T), func=AF.Copy, scale=psc.k(0, c, c + 1))
                        bk.free()
                    yield
                wdone()

            memT = AR.alloc("memT", BF16, 8, T)

            def mem_head(h):
                pr = AR.alloc("probs", BF16, 2, T)
                bks = []
                bks_ = yield from gbanks(2)
                PH[0] = "p2/mem1"
                for mc in range(2):
                    bk = bks_[mc]
                    for dd in range(2):
                        OP("tensor", "matmul", [kmT, mq], [bk], out=bk.f.k(0, 0, T), lhsT=kmT.k(2 * h + dd, 128 * mc, 128 * mc + 128),
                           rhs=mq.k(2 * h + dd), start=(dd == 0), stop=(dd == 1))
                    bks.append(bk)
                yield
                for mc in range(2):
                    OP("scalar", "activation", [bks[mc]], [pr], out=pr.k(mc), in_=bks[mc].f.k(0, 0, T), func=AF.Exp, scale=1.0 / 16)
                    bks[mc].free()
                yield
                bd, bo0, bo1 = yield from gbanks(3)
                PH[0] = "p2/mem2"
                for mc in range(2):
                    OP("tensor", "matmul", [pr], [bd], out=bd.f.k(0, 0, T), lhsT=ones_bf.all(), rhs=pr.k(mc), start=(mc == 0), stop=(mc == 1))
                bos = []
                for ec in range(2):
                    bo = (bo0, bo1)[ec]
                    for mc in range(2):
                        OP("tensor", "matmul", [vm, pr], [bo], out=bo.f.k(0, 0, T), lhsT=vm.k(mc, 256 * h + 128 * ec, 256 * h + 128 * ec + 128),
                           rhs=pr.k(mc), start=(mc == 0), stop=(mc == 1))
                    bos.append(bo)
                yield
                rec = AR.alloc("rec", F32, 1, T)
                OP("scalar", "activation", [bd], [rec], out=rec.all(), in_=bd.f.k(0, 0, T), func=AF.Ln)
                OP("scalar", "activation", [rec], [rec], out=rec.all(), in_=rec.all(), func=AF.Exp, scale=-1.0)
                bd.free()
                yield
                for ec in range(2):
                    OP("vector", "tensor_tensor", [bos[ec], rec], [memT], out=memT.k(2 * h + ec), in0=bos[ec].f.k(0, 0, T), in1=rec.all(), op=ALU.mult)
                    bos[ec].free()
                AR.release(pr, rec)
                yield

            def mem_thread():
                for h in range(4):
                    yield from mem_head(h)

            run_threads([ret_thread(), pool_thread(), mem_thread()])
            AR.release(kT, vt, qf, qb, kf, *Sb)
            AR.release(dsl, mq)
            PH[0] = "p2"
            gates = [AR.alloc(f"gates{j}", BF16, 8, T) for j in range(3)]
            for i in range(6):
                def ev(oc, bk, i=i):
                    OP("scalar", "activation", [bk], [gates[(4 * i + oc) // 8]], out=gates[(4 * i + oc) // 8].k((4 * i + oc) % 8), in_=bk.f.k(0, 0, T), func=AF.Sigmoid)
                projA("in", 12 + i, hT, T, ev)
            AR.release(hT)
            PH[0] = "p2/merge"
            merged = AR.alloc("merged", BF16, 8, T)
            retT = AR.alloc("retT", BF16, 8, T)
            srcs = (poolp, memT, retT)
            MNAMES = ("pool_out", "mem_out", "ret_out")
            GIDX = (1, 2, 0)

            def retT_gen():
                for fc in range(8):
                    (bk,) = yield from gbanks(1)
                    PH[0] = "p2/retT"
                    for g in range(G):
                        OP("tensor", "transpose", [yn], [bk], out=bk.h.k(0, 128 * g, 128 * g + 128), in_=yn.k(g, 128 * fc, 128 * fc + 128),
                           identity=ident.all())
                    OP("vector", "scalar_tensor_tensor", [bk, silu], [retT], out=retT.k(fc), in0=bk.h.k(0, 0, T), scalar=rgn.k(0, fc, fc + 1),
                       in1=silu.k(fc), op0=ALU.mult, op1=ALU.mult)
                    bk.free()
                AR.release(yn, silu)
                yield
            accs8 = [AR.alloc(f"macc{q}", F32, 1, T) for q in range(8)]
            wsm = {}
            if True:
                def mg(half, j, oc4):
                    accs = accs8[4 * half:4 * half + 4]
                    oc = 4 * half + oc4
                    (bk,) = yield from gbanks(1)
                    PH[0] = "p2/merge"
                    if oc4 == 0:
                        wsm[0] = wuse(BLK[MNAMES[j]][half])
                    w = wsm[0]
                    for k in range(8):
                        OP("tensor", "matmul", [w, srcs[j]], [bk], out=bk.f.k(0, 0, T), lhsT=w.k(k, 128 * oc4, 128 * oc4 + 128),
                           rhs=srcs[j].k(k), start=(k == 0), stop=(k == 7))
                    if oc4 == 3:
                        wdone()
                    yield
                    if j == 0:
                        OP("vector", "tensor_tensor", [bk, gates[GIDX[j]]], [accs[oc4]], out=accs[oc4].all(), in0=bk.f.k(0, 0, T), in1=gates[GIDX[j]].k(oc), op=ALU.mult)
                        bk.free()
                        yield
                    else:
                        tmp = AR.alloc("mtmp", F32, 1, T)
                        OP("vector", "tensor_tensor", [bk, gates[GIDX[j]]], [tmp], out=tmp.all(), in0=bk.f.k(0, 0, T), in1=gates[GIDX[j]].k(oc), op=ALU.mult)
                        bk.free()
                        yield
                        if j == 1:
                            OP("gpsimd", "tensor_tensor", [accs[oc4], tmp], [accs[oc4]], out=accs[oc4].all(), in0=accs[oc4].all(), in1=tmp.all(), op=ALU.add)
                        else:
                            OP("gpsimd", "tensor_tensor", [accs[oc4], tmp], [merged], out=merged.k(oc), in0=accs[oc4].all(), in1=tmp.all(), op=ALU.add)
                        AR.release(tmp)
                        yield

                mk_ = []
                for half in range(2):
                    for j in range(3):
                        if half == 0 and j == 2:
                            mk_.append(retT_gen)
                        for oc4 in range(4):
                            mk_.append(lambda half=half, j=j, oc4=oc4: mg(half, j, oc4))
                run_threads([skewed(mk_)])
                AR.release(*accs8)
            AR.release(retT, poolp, memT, *gates)
            PH[0] = "p2wo"
            h2T = AR.alloc("h2T", BF16, 8, T + 3)
            OP("gpsimd", "tensor_copy", [carry], [h2T], out=h2T.ks(0, 8, 0, 2), in_=carry.ks(0, 8))
            if last:
                OP("gpsimd", "memset", [], [h2T], ap=h2T.ks(0, 8, T + 2, T + 3), constant=0.0)
            wo = [inflight[0][1], inflight[1][1]]
            assert inflight[0][0] == BLK["o"][0] and inflight[1][0] == BLK["o"][1]
            for g in range(G):
                bk2 = []
                for half in range(2):
                    bk = BankNow()
                    for k in range(8):
                        OP("tensor", "matmul", [wo[half], merged], [bk], out=bk.f.all(), lhsT=merged.k(k, 128 * g, 128 * g + 128),
                           rhs=wo[half].k(k), start=(k == 0), stop=(k == 7))
                    bk2.append(bk)
                junk = AR.alloc("junk3", BF16, 1, 512)
                ssl = []
                for half in range(2):
                    ss = sm()
                    OP("scalar", "activation", [bk2[half]], [junk, ss], out=junk.all(), in_=bk2[half].f.all(), func=AF.Square, accum_out=ss.a)
                    ssl.append(ss)
                r = rstd_from(ssl, D)
                tt = AR.alloc("wot", F32, 1, D)
                for half in range(2):
                    OP("vector", "scalar_tensor_tensor", [bk2[half], r], [tt], out=tt.k(0, 512 * half, 512 * half + 512),
                       in0=bk2[half].f.all(), scalar=r.a, in1=gpost.k(0, 512 * half, 512 * half + 512), op0=ALU.mult, op1=ALU.mult)
                OP("gpsimd", "tensor_tensor", [tt, xs[g]], [xs[g]], out=xs[g].all(), in0=tt.all(), in1=xs[g].all(), op=ALU.add)
                for bk in bk2:
                    bk.free()
                AR.release(junk, tt)
                norm_transpose(xs[g].all(), xs[g], gfpre, h2T, 2 + 128 * g)
            wdone()
            wdone()
            AR.release(merged)
            OP("gpsimd", "tensor_copy", [h2T], [carry], out=carry.ks(0, 8), in_=h2T.ks(0, 8, T, T + 2))
            NO = T + 1 if last else T
            NC = NO + 2
            wcur = {}

            def pair_gen(j):
                bi_, jj = j // 2, j % 2
                bg, bv = yield from gbanks(2)
                PH[0] = "p2/ffn_up"
                if jj == 0:
                    wcur[0] = wuse(BLK["up"][bi_])
                w = wcur[0]
                for k in range(8):
                    OP("tensor", "matmul", [w, h2T], [bg], out=bg.f.k(0, 0, NC), lhsT=w.k(k, 128 * jj, 128 * jj + 128), rhs=h2T.k(k, 0, NC),
                       start=(k == 0), stop=(k == 7))
                for k in range(8):
                    OP("tensor", "matmul", [w, h2T], [bv], out=bv.f.k(0, 0, NC), lhsT=w.k(k, 256 + 128 * jj, 256 + 128 * jj + 128),
                       rhs=h2T.k(k, 0, NC), start=(k == 0), stop=(k == 7))
                if jj == 1:
                    wdone()
                yield
                cvs = []
                for (bk, ci) in ((bg, j), (bv, NPAIR + j)):
                    cv = AR.alloc("cv", F32, 1, T + 1)
                    OP("scalar", "activation", [bk], [cv], out=cv.k(0, 0, NO), in_=bk.f.k(0, 0, NO), func=AF.Identity,
                       scale=cw.k(0, ci, ci + 1), bias=cb.k(0, ci, ci + 1))
                    cvs.append(cv)
                yield
                for (bk, cv, ci) in ((bg, cvs[0], j), (bv, cvs[1], NPAIR + j)):
                    OP("vector", "scalar_tensor_tensor", [bk, cv], [cv], out=cv.k(0, 0, NO), in0=bk.f.k(0, 1, 1 + NO), scalar=cw.k(1, ci, ci + 1),
                       in1=cv.k(0, 0, NO), op0=ALU.mult, op1=ALU.add)
                    OP("vector", "scalar_tensor_tensor", [bk, cv], [cv], out=cv.k(0, 0, NO), in0=bk.f.k(0, 2, 2 + NO), scalar=cw.k(2, ci, ci + 1),
                       in1=cv.k(0, 0, NO), op0=ALU.mult, op1=ALU.add)
                    bk.free()
                cg, cv = cvs
                yield
                sq = AR.alloc("gsq", F32, 1, T + 1)
                OP("scalar", "activation", [cg], [sq], out=sq.k(0, 0, NO), in_=cg.k(0, 0, NO), func=AF.Square)
                gm = AR.alloc("gm", F32, 1, T + 1)
                OP("gpsimd", "tensor_tensor", [cg, cv], [gm], out=gm.k(0, 0, NO), in0=cg.k(0, 0, NO), in1=cv.k(0, 0, NO), op=ALU.mult)
                yield
                OP("gpsimd", "tensor_scalar", [sq], [sq], out=sq.k(0, 0, NO), in0=sq.k(0, 0, NO), scalar1=0.044715, scalar2=1.0,
                   op0=ALU.mult, op1=ALU.add)
                OP("gpsimd", "tensor_tensor", [sq, cg], [sq], out=sq.k(0, 0, NO), in0=sq.k(0, 0, NO), in1=cg.k(0, 0, NO), op=ALU.mult)
                yield
                OP("scalar", "activation", [sq], [sq], out=sq.k(0, 0, NO), in_=sq.k(0, 0, NO), func=AF.Sigmoid, scale=1.5957691216057308)
                yield
                OP("vector", "tensor_tensor", [gm, sq], [Gr], out=Gr.k(j, 127, 127 + NO), in0=gm.k(0, 0, NO), in1=sq.k(0, 0, NO), op=ALU.mult)
                AR.release(cg, cv, sq, gm)
                yield

            run_threads([skewed([(lambda j=j: pair_gen(j)) for j in range(NPAIR)])])
            AR.release(h2T)
            if not last:
                HD = head(ti + 1, hT_next[0])
                hT_next[0] = None
            PH[0] = "p2/ffn_down"
            groups = list(range(0 if not first else 1, G + (1 if last else 0)))
            slots = []
            toks = []
            for gcol in groups:
                gidx = ti * G + gcol - 1
                slots.append(xring[gidx % NX])
                toks.append(128 * gidx)
            ffn_down(groups, slots, toks)
            if not last:
                OP("gpsimd", "tensor_copy", [Gr], [Gr], out=Gr.ks(0, NPAIR, 0, 128), in_=Gr.ks(0, NPAIR, T, T + 128))
        AR.release(*xring, pring, wa_p, Gr, carry, S32, _sbf1, kmT, vm)
        tok0 += SL
        ch0 += NCH

    assert not wq and not inflight, (wq, inflight)
    S.emit()
    es.close()
    build_program.tags = {e: [i.tag for i in S.streams[e] if not i.is_dma] for e in ENGS}
    build_program.stats = dict(peak_pages=AR.peak, pages=AR.np_,
                               ninstr={e: len(S.streams[e]) for e in ENGS})
    return nc


T_TILE = 256
_cache = {}


def kernel(**inputs):
    f = lambda a: np.ascontiguousarray(np.asarray(a, dtype=np.float32))
    xp = f(inputs["x_prompt"])
    xs = f(inputs["x_sample"])
    mp = f(inputs["mem_prompt"])
    ms = f(inputs["mem_sample"])
    NC = 8
    nb_p = xp.shape[0] // NC
    nb_s = xs.shape[0] // NC
    SP, SS = xp.shape[1], xs.shape[1]
    seq_lens = [SP] * nb_p + [SS] * nb_s
    key = (tuple(seq_lens), T_TILE)
    if key not in _cache:
        _cache[key] = build_program(seq_lens, T_TILE)
    nc = _cache[key]
    shared = {}
    for nm in ("g_mix_pre", "g_mem", "ret_gn", "pool_scale", "g_ffn_pre", "conv_b"):
        shared[nm] = f(inputs[nm])[0].reshape(-1)
    for nm in ("g_mix_post", "g_ffn_post", "decay_fwd", "decay_bwd"):
        shared[nm] = f(inputs[nm])[0].reshape(1, -1)
    for nm in ("w_in", "w_ret_out", "pool_w", "w_pool_out", "w_mem_kv", "w_mem_out", "w_o", "w_up", "conv_w", "w_down"):
        shared[nm] = f(inputs[nm])[0]
    in_maps = []
    for c in range(NC):
        xc = np.concatenate([xp[c * nb_p + i] for i in range(nb_p)] + [xs[c * nb_s + i] for i in range(nb_s)], axis=0)
        mc = np.concatenate([mp[c * nb_p + i] for i in range(nb_p)] + [ms[c * nb_s + i] for i in range(nb_s)], axis=0)
        m = dict(shared)
        m["x"] = np.ascontiguousarray(xc)
        m["mem"] = np.ascontiguousarray(mc)
        in_maps.append(m)
    res = run_bass_kernel_spmd(nc, in_maps, core_ids=list(range(NC)))
    yp = np.empty_like(xp)
    ys = np.empty_like(xs)
    for c in range(NC):
        y = res.results[c]["y"]
        o = 0
        for i in range(nb_p):
            yp[c * nb_p + i] = y[o:o + SP]
            o += SP
        for i in range(nb_s):
            ys[c * nb_s + i] = y[o:o + SS]
            o += SS
    return (yp, ys)
```

```python
import math
import numpy as np
import concourse.bass as bass
import concourse.mybir as mybir
from concourse.bass_types import AP
from concourse.bass_utils import run_bass_kernel_spmd
from contextlib import ExitStack

F32 = mybir.dt.float32
BF16 = mybir.dt.bfloat16
I32 = mybir.dt.int32
ALU = mybir.AluOpType
AF = mybir.ActivationFunctionType

ENGS = ("sync", "tensor", "vector", "scalar", "gpsimd")
PH = ["init"]
EPOCH = 6000
NDMASEM = 24


class Res:
    __slots__ = ("name", "last_w", "readers")

    def __init__(self, name):
        self.name = name
        self.last_w = None
        self.readers = []


class Instr:
    __slots__ = ("eng", "fn", "is_dma", "deps", "signal", "sig_no", "dma_slot", "dma_val", "tag")

    def __init__(self, eng, fn, is_dma):
        self.eng = eng
        self.fn = fn
        self.is_dma = is_dma
        self.deps = []
        self.signal = False
        self.sig_no = None
        self.dma_slot = None
        self.dma_val = None


class Sched:
    def __init__(self, nc):
        self.nc = nc
        self.streams = {e: [] for e in ENGS}
        self.ndma = {e: 0 for e in ENGS}

    def res(self, name):
        return Res(name)

    def op(self, eng, fn, reads=(), writes=(), is_dma=False):
        ins = Instr(eng, fn, is_dma)
        ins.tag = PH[0]
        deps = {}
        for r in reads:
            lw = r.last_w
            if lw is not None:
                deps[id(lw)] = (lw, True)
        for w in writes:
            lw = w.last_w
            if lw is not None:
                deps[id(lw)] = (lw, True)
            for rd in w.readers:
                if id(rd) not in deps:
                    deps[id(rd)] = (rd, False)
        for d, hard in deps.values():
            if (not d.is_dma) and (not is_dma) and d.eng == eng:
                if eng == "tensor" or not hard:
                    continue
            ins.deps.append(d)
            d.signal = True
        if is_dma:
            ins.signal = True
            k = self.ndma[eng]
            self.ndma[eng] = k + 1
            ins.dma_slot = k % NDMASEM
            ins.dma_val = 16 * (k // NDMASEM + 1)
        for r in reads:
            r.readers.append(ins)
        for w in writes:
            w.last_w = ins
            w.readers = []
        self.streams[eng].append(ins)
        return ins

    def emit(self):
        nc = self.nc
        with ExitStack() as es:
            csem = {}
            for e in ENGS:
                n = 0
                for ins in self.streams[e]:
                    if ins.signal and not ins.is_dma:
                        n += 1
                        ins.sig_no = n
                nep = max((n + EPOCH - 1) // EPOCH, 1)
                csem[e] = [es.enter_context(nc.semaphore(f"c_{e}_{k}")) for k in range(nep)]
            dsem = {}
            for e in ENGS:
                if self.ndma[e]:
                    dsem[e] = [es.enter_context(nc.semaphore(f"d_{e}_{k}"))
                               for k in range(min(NDMASEM, self.ndma[e]))]
            streams = self.streams

            def emit_stream(ename, eng):
                waited = {}
                maxep = {}
                dma_hist = {}
                for ins in streams[ename]:
                    need = []
                    for d in ins.deps:
                        if d.is_dma:
                            need.append((("d", d.eng, d.dma_slot), dsem[d.eng][d.dma_slot], d.dma_val))
                        else:
                            ep = (d.sig_no - 1) // EPOCH
                            if maxep.get(d.eng, -1) > ep:
                                continue
                            need.append((("c", d.eng, ep), csem[d.eng][ep], d.sig_no - ep * EPOCH))
                    if ins.is_dma:
                        prev = dma_hist.get(ins.dma_slot)
                        if prev is not None:
                            need.append((("d", ename, ins.dma_slot), dsem[ename][ins.dma_slot], prev.dma_val))
                        dma_hist[ins.dma_slot] = ins
                    best = {}
                    for key, sem, val in need:
                        if waited.get(key, 0) >= val:
                            continue
                        if key not in best or best[key][1] < val:
                            best[key] = (sem, val)
                    for key, (sem, val) in best.items():
                        eng.wait_ge(sem, val)
                        waited[key] = val
                        if key[0] == "c":
                            maxep[key[1]] = max(maxep.get(key[1], -1), key[2])
                    h = ins.fn(eng)
                    if ins.is_dma:
                        h.then_inc(dsem[ename][ins.dma_slot], 16)
                    elif ins.signal:
                        ep = (ins.sig_no - 1) // EPOCH
                        h.then_inc(csem[ename][ep], 1)
                for slot, prev in dma_hist.items():
                    if waited.get(("d", ename, slot), 0) < prev.dma_val:
                        eng.wait_ge(dsem[ename][slot], prev.dma_val)

            with nc.Block() as block:
                @block.sync
                def _(e):
                    emit_stream("sync", e)

                @block.tensor
                def _(e):
                    emit_stream("tensor", e)

                @block.vector
                def _(e):
                    emit_stream("vector", e)

                @block.scalar
                def _(e):
                    emit_stream("scalar", e)

                @block.gpsimd
                def _(e):
                    emit_stream("gpsimd", e)


class View:
    def __init__(self, base, K, N, res, name=""):
        self.base = base
        self.K = K
        self.N = N
        self.res = res
        self.p = list(base.ap[0])
        self.name = name

    def ap(self, dims, off=0):
        return AP(tensor=self.base.tensor, offset=self.base.offset + off,
                  ap=[self.p] + [list(d) for d in dims])

    def k(self, k, c0=0, c1=None):
        c1 = self.N if c1 is None else c1
        return self.ap([[1, c1 - c0]], k * self.N + c0)

    def ks(self, k0, k1, c0=0, c1=None):
        c1 = self.N if c1 is None else c1
        return self.ap([[self.N, k1 - k0], [1, c1 - c0]], k0 * self.N + c0)

    def all(self):
        return self.ap([[1, self.K * self.N]])


PAGE = 1024


class Arena:
    def __init__(self, S, tens, nbytes):
        self.t = tens
        self.np_ = nbytes // PAGE
        self.free = [True] * self.np_
        self.res = [S.res(f"pg{i}") for i in range(self.np_)]
        self.peak = 0

    def alloc(self, name, dtype, K, N):
        esz = 4 if dtype in (F32, I32) else 2
        npg = (K * N * esz + PAGE - 1) // PAGE
        i0 = -1
        ptr = getattr(self, "ptr", 0)
        if npg >= 4:
            run = 0
            for i in range(self.np_ - 1, -1, -1):
                run = run + 1 if self.free[i] else 0
                if run == npg:
                    i0 = i
                    break
            if i0 >= 0:
                for i in range(i0, i0 + npg):
                    self.free[i] = False
                self.peak = max(self.peak, self.np_ - sum(self.free))
                b = self.t[:, i0 * PAGE // 4:(i0 + npg) * PAGE // 4]
                if dtype != F32:
                    b = b.bitcast(dtype)
                v = View(b, K, N, self.res[i0:i0 + npg], name)
                v.pages = (i0, npg)
                return v
        for lo, hi in ((ptr, self.np_), (0, self.np_)):
            run = 0
            for i in range(lo, hi):
                run = run + 1 if self.free[i] else 0
                if run == npg:
                    i0 = i - npg + 1
                    break
            if i0 >= 0:
                break
        if i0 >= 0:
            self.ptr = i0 + npg
        if i0 < 0:
            raise RuntimeError(f"arena full allocating {name} ({npg} pages); free={sum(self.free)}")
        for i in range(i0, i0 + npg):
            self.free[i] = False
        self.peak = max(self.peak, self.np_ - sum(self.free))
        b = self.t[:, i0 * PAGE // 4:(i0 + npg) * PAGE // 4]
        if dtype != F32:
            b = b.bitcast(dtype)
        v = View(b, K, N, self.res[i0:i0 + npg], name)
        v.pages = (i0, npg)
        return v

    def release(self, *views):
        for v in views:
            i0, npg = v.pages
            for i in range(i0, i0 + npg):
                assert not self.free[i], v.name
                self.free[i] = True


D = 1024
DFF = 2816
NPAIR = 22
NMEM = 256
EPS = 1e-6
TWO_PI = 2.0 * math.pi


def build_program(seq_lens, T):
    G = T // 128
    NSEQ = len(seq_lens)
    NTOK = sum(seq_lens)
    SMAX = max(seq_lens)
    for s in seq_lens:
        assert s % T == 0 and s // T >= 2
    nc = bass.Bass("TRN2", target_bir_lowering=False)

    def din(name, shape):
        return nc.dram_tensor(name, shape, F32, kind="ExternalInput").ap()

    x_d = din("x", [NTOK, D])
    mem_d = din("mem", [NSEQ * NMEM, D])
    g_mix_pre_d = din("g_mix_pre", [D])
    g_mix_post_d = din("g_mix_post", [1, D])
    g_mem_d = din("g_mem", [D])
    w_in_d = din("w_in", [D, 9216])
    decay_fwd_d = din("decay_fwd", [1, 4])
    decay_bwd_d = din("decay_bwd", [1, 4])
    ret_gn_d = din("ret_gn", [D])
    w_ret_out_d = din("w_ret_out", [D, D])
    pool_w_d = din("pool_w", [4, 256, 256])
    pool_scale_d = din("pool_scale", [D])
    w_pool_out_d = din("w_pool_out", [D, D])
    w_mem_kv_d = din("w_mem_kv", [D, 2048])
    w_mem_out_d = din("w_mem_out", [D, D])
    w_o_d = din("w_o", [D, D])
    g_ffn_pre_d = din("g_ffn_pre", [D])
    g_ffn_post_d = din("g_ffn_post", [1, D])
    w_up_d = din("w_up", [D, 2 * DFF])
    conv_w_d = din("conv_w", [3, 2 * DFF])
    conv_b_d = din("conv_b", [2 * DFF])
    w_down_d = din("w_down", [DFF, D])
    y_d = nc.dram_tensor("y", [NTOK, D], F32, kind="ExternalOutput").ap()

    BLK = {}
    nb = 0
    for name, n in [("in", 18), ("ret_out", 2), ("pool_out", 2), ("mem_out", 2), ("o", 2),
                    ("poolw", 1), ("up", 11), ("down", 6), ("memkv", 4)]:
        BLK[name] = list(range(nb, nb + n))
        nb += n
    NBLK = nb
    wb_d = nc.dram_tensor("wb_scr", [NBLK, 128, 8 * 512], BF16, kind="Internal").ap()
    cs_d = nc.dram_tensor("cs_scr", [2, 128, SMAX], F32, kind="Internal").ap()
    NCHT = NTOK // 128
    sb_d = nc.dram_tensor("sb_scr", [NCHT, 128, 2048], BF16, kind="Internal").ap()
    NT2T = NTOK // T
    kts_d = nc.dram_tensor("kts_scr", [NT2T, 128, 8 * T], BF16, kind="Internal").ap()
    vts_d = nc.dram_tensor("vts_scr", [NT2T, 128, G * D], BF16, kind="Internal").ap()

    S = Sched(nc)
    es = ExitStack()

    def sbt(name, shape, dt):
        return es.enter_context(nc.sbuf_tensor(name, shape, dt))

    def fixed(name, dt, K, N):
        t = sbt(name, [128, K * N], dt)
        return View(t[:], K, N, [S.res(name)], name)

    def flat(lst):
        out = []
        for a in lst:
            if a is None:
                continue
            if hasattr(a, "res"):
                out.extend(a.res)
            elif isinstance(a, (list, tuple)):
                out.extend(flat(a))
            else:
                out.append(a)
        return out

    def OP(eng, meth, reads, writes, **kw):
        S.op(eng, lambda e: getattr(e, meth)(**kw), flat(reads), flat(writes))

    def DMA(eng, out, in_, reads, writes, **kw):
        S.op(eng, lambda e: e.dma_start(out=out, in_=in_, **kw), flat(reads), flat(writes), is_dma=True)

    R_W = 5
    wring = [fixed(f"wring{i}", BF16, 8, 512) for i in range(R_W)]
    ident = fixed("ident", BF16, 1, 128)
    identf = fixed("identf", F32, 1, 128)
    ones_bf = fixed("ones_bf", BF16, 1, 128)
    onesf = fixed("onesf", F32, 1, 128)
    dec = fixed("dec", F32, 1, 8)
    lg = fixed("lg", F32, 1, 8)
    tmp8 = fixed("tmp8", F32, 1, 8)
    iot_i = fixed("iot_i", I32, 1, 512)
    io_c1 = fixed("io_c1", F32, 1, 128)
    io_rc = fixed("io_rc", F32, 1, 128)
    io_mc = fixed("io_mc", F32, 1, 128)
    io_p = fixed("io_p", F32, 1, 1)
    a127 = fixed("a127", F32, 1, 1)
    p1 = fixed("p1", F32, 1, 1)
    AFt = fixed("AFt", F32, 4, 128)
    ABt = fixed("ABt", F32, 4, 128)
    DTt = fixed("DTt", F32, 4, 128)
    kdf = fixed("kdf", F32, 1, 4)
    kdb = fixed("kdb", F32, 1, 4)
    cdf = fixed("cdf", F32, 1, 4)
    cdb = fixed("cdb", F32, 1, 4)
    e1col = fixed("e1col", F32, 1, 4)
    nhalf = fixed("nhalf", F32, 1, 1)
    inv = fixed("inv", F32, 1, 1)
    gpre = fixed("gpre", F32, 1, 8)
    gfpre = fixed("gfpre", F32, 1, 8)
    gmem = fixed("gmem", F32, 1, 8)
    rgn = fixed("rgn", F32, 1, 8)
    psc = fixed("psc", F32, 1, 8)
    cw = fixed("cw", F32, 3, 44)
    cb = fixed("cb", F32, 1, 44)
    gpost = fixed("gpost", F32, 1, D)
    gfpost = fixed("gfpost", F32, 1, D)
    icf = fixed("icf", F32, 4, 8)
    icl = fixed("icl", F32, 4, 8)
    small = fixed("small", F32, 1, 64)
    dummy = {e: fixed(f"dummy_{e}", F32, 1, 4) for e in ("vector", "scalar", "gpsimd")}
    tmpA = fixed("tmpA", F32, 1, 128)
    csbuf = [fixed(f"csbuf{i}", F32, 2, T) for i in range(2)]
    tmpB = fixed("tmpB", F32, 1, 128)

    rem = nc.sbuf_bytes_remaining
    rem = rem() if callable(rem) else rem
    ARENA_BYTES = (rem - 2048) // PAGE * PAGE
    arena_t = sbt("arena", [128, ARENA_BYTES // 4], F32)
    AR = Arena(S, arena_t, ARENA_BYTES)

    pbanks = []
    for i in range(8):
        t = es.enter_context(nc.psum_tensor(f"bank{i}", [128, 512], F32))
        pbanks.append(t)
    half_res = [S.res(f"bankh{i}") for i in range(16)]
    half_free = [True] * 16
    half_stamp = [0] * 16
    stamp = [0]

    class Bank:
        def __init__(self, halves):
            self.halves = halves
            self.res = [half_res[h] for h in halves]
            t = pbanks[halves[0] // 2]
            if len(halves) == 2:
                base = t[:]
                self.f = View(base, 1, 512, self.res)
                self.h = View(base.bitcast(BF16), 1, 1024, self.res)
            else:
                o = 256 * (halves[0] % 2)
                base = t[:, o:o + 256]
                self.f = View(base, 1, 256, self.res)
                self.h = View(base.bitcast(BF16), 1, 512, self.res)

        def free(self):
            for h in self.halves:
                assert not half_free[h]
                half_free[h] = True
                stamp[0] += 1
                half_stamp[h] = stamp[0]

    def try_bank(half=False):
        best = None
        if half:
            for h in range(16):
                if half_free[h]:
                    key = (half_free[h ^ 1], half_stamp[h])
                    if best is None or key < best[0]:
                        best = (key, [h])
        else:
            for bnk in range(8):
                if half_free[2 * bnk] and half_free[2 * bnk + 1]:
                    key = max(half_stamp[2 * bnk], half_stamp[2 * bnk + 1])
                    if best is None or key < best[0]:
                        best = (key, [2 * bnk, 2 * bnk + 1])
        if best is None:
            return None
        for h in best[1]:
            half_free[h] = False
        return Bank(best[1])

    def BankNow(half=False):
        bk_ = try_bank(half)
        assert bk_ is not None, "no free PSUM bank"
        return bk_

    def gbanks(k):
        n = 0
        while True:
            nfree = sum(1 for b_ in range(8) if half_free[2 * b_] and half_free[2 * b_ + 1])
            if nfree >= k:
                return [try_bank(False) for _ in range(k)]
            n += 1
            assert n < 10000, "PSUM bank starvation"
            yield

    small_ctr = [0]
    small_res = [S.res(f"small{i}") for i in range(64)]

    class SmallV:
        def __init__(self, o, n):
            self.o = o
            self.n = n
            self.res = small_res[o:o + n]
            self.a = small.k(0, o, o + n)

        def sub(self, i):
            return small.k(0, self.o + i, self.o + i + 1)

    def sm(n=1):
        i = small_ctr[0]
        if i % 64 + n > 64:
            i = (i // 64 + 1) * 64
        small_ctr[0] = i + n
        return SmallV(i % 64, n)

    CONST = []

    def cvec(view, src, pattern, **kw):
        DMA("sync", view.all() if pattern is None else pattern, src, [], [view],
            allow_slow_non_contiguous=True, **kw)
        CONST.append(view)

    WB = [S.res(f"wb{i}") for i in range(NBLK)]

    def wbv(b):
        return wb_d[b].rearrange("p (k n) -> p k n", k=8)

    def wsrc(w, c0, c1):
        return w.rearrange("(k p) n -> p k n", p=128)[:, :, c0:c1]

    for i, b in enumerate(BLK["in"]):
        DMA("gpsimd", wbv(b), wsrc(w_in_d, 512 * i, 512 * i + 512), [], [WB[b]])
    for nm, w in (("ret_out", w_ret_out_d), ("pool_out", w_pool_out_d), ("mem_out", w_mem_out_d), ("o", w_o_d)):
        for i, b in enumerate(BLK[nm]):
            DMA("gpsimd", wbv(b), wsrc(w, 512 * i, 512 * i + 512), [], [WB[b]])
    b = BLK["poolw"][0]
    for g in range(4):
        DMA("gpsimd", wbv(b)[:, 2 * g:2 * g + 2, 0:256],
            pool_w_d[g].rearrange("(k p) n -> p k n", p=128), [], [WB[b]])
    for i, b in enumerate(BLK["up"]):
        DMA("gpsimd", wbv(b)[:, :, 0:256], wsrc(w_up_d, 256 * i, 256 * i + 256), [], [WB[b]])
        DMA("gpsimd", wbv(b)[:, :, 256:512], wsrc(w_up_d, DFF + 256 * i, DFF + 256 * i + 256), [], [WB[b]])
    for half in range(2):
        for kg in range(3):
            b = BLK["down"][half * 3 + kg]
            nk = min(8, NPAIR - 8 * kg)
            DMA("gpsimd", wbv(b)[:, 0:nk, :],
                w_down_d.rearrange("(k p) n -> p k n", p=128)[:, 8 * kg:8 * kg + nk, 512 * half:512 * half + 512],
                [], [WB[b]])
    for i, b in enumerate(BLK["memkv"]):
        DMA("gpsimd", wbv(b), wsrc(w_mem_kv_d, 512 * i, 512 * i + 512), [], [WB[b]])

    for v, src in ((gpre, g_mix_pre_d), (gfpre, g_ffn_pre_d), (gmem, g_mem_d), (rgn, ret_gn_d), (psc, pool_scale_d)):
        cvec(v, src.rearrange("(k p) -> p k", p=128), None)
    cvec(cw, conv_w_d.rearrange("t (c p) -> p t c", p=128), cw.ks(0, 3))
    cvec(cb, conv_b_d.rearrange("(c p) -> p c", p=128), None)

    def pbc(src, n):
        return AP(tensor=src.tensor, offset=src.offset, ap=[[0, 128], [1, n]])

    cvec(gpost, pbc(g_mix_post_d, D), None)
    cvec(gfpost, pbc(g_ffn_post_d, D), None)
    DMA("sync", dec.k(0, 0, 4), pbc(decay_fwd_d, 4), [], [dec])
    DMA("sync", dec.k(0, 4, 8), pbc(decay_bwd_d, 4), [], [dec])

    OP("gpsimd", "memset", [], [identf], ap=identf.all(), constant=0.0)
    OP("gpsimd", "memset", [], [onesf], ap=onesf.all(), constant=1.0)
    OP("gpsimd", "memset", [], [nhalf], ap=nhalf.all(), constant=-0.5)
    OP("gpsimd", "affine_select", [identf], [identf], out=identf.all(), in_=identf.all(), pattern=[[-1, 128]],
       compare_op=ALU.not_equal, fill=1.0, base=0, channel_multiplier=1)
    OP("vector", "tensor_copy", [identf], [ident], out=ident.all(), in_=identf.all())
    OP("vector", "tensor_copy", [onesf], [ones_bf], out=ones_bf.all(), in_=onesf.all())

    def iota_f(dst, pattern, base, cm, n):
        OP("gpsimd", "iota", [dst, iot_i], [iot_i], out=iot_i.k(0, 0, n), pattern=pattern, base=base, channel_multiplier=cm)
        OP("vector", "tensor_copy", [iot_i], [dst], out=dst.all(), in_=iot_i.k(0, 0, n))

    iota_f(io_c1, [[1, 128]], 1, 0, 128)
    iota_f(io_mc, [[-1, 128]], 0, 1, 128)
    iota_f(io_p, [[0, 1]], 0, 1, 1)
    OP("vector", "tensor_scalar", [io_c1], [io_rc], out=io_rc.all(), in0=io_c1.all(), scalar1=-1.0, scalar2=129.0,
       op0=ALU.mult, op1=ALU.add)
    OP("vector", "tensor_scalar", [io_p], [a127], out=a127.all(), in0=io_p.all(), scalar1=-1.0, scalar2=127.0,
       op0=ALU.mult, op1=ALU.add)
    OP("vector", "tensor_scalar", [io_p], [p1], out=p1.all(), in0=io_p.all(), scalar1=1.0, scalar2=None, op0=ALU.add)
    OP("scalar", "activation", [dec], [tmp8], out=tmp8.all(), in_=dec.all(), func=AF.Exp, scale=-1.0)
    OP("vector", "tensor_scalar", [tmp8], [tmp8], out=tmp8.all(), in0=tmp8.all(), scalar1=1.0, scalar2=None, op0=ALU.add)
    OP("scalar", "activation", [tmp8], [lg], out=lg.all(), in_=tmp8.all(), func=AF.Ln)
    OP("vector", "tensor_scalar", [lg], [lg], out=lg.all(), in0=lg.all(), scalar1=-1.0, scalar2=None, op0=ALU.mult)
    for h in range(4):
        lf = lg.k(0, h, h + 1)
        lb = lg.k(0, 4 + h, 5 + h)
        OP("scalar", "activation", [io_c1, lg], [AFt], out=AFt.k(h), in_=io_c1.all(), func=AF.Exp, scale=lf)
        OP("scalar", "activation", [io_rc, lg], [ABt], out=ABt.k(h), in_=io_rc.all(), func=AF.Exp, scale=lb)
        OP("scalar", "activation", [a127, lg], [kdf], out=kdf.k(0, h, h + 1), in_=a127.all(), func=AF.Exp, scale=lf)
        OP("scalar", "activation", [io_p, lg], [kdb], out=kdb.k(0, h, h + 1), in_=io_p.all(), func=AF.Exp, scale=lb)
        OP("scalar", "activation", [lg], [cdf], out=cdf.k(0, h, h + 1), in_=lf, func=AF.Exp, scale=128.0)
        OP("scalar", "activation", [lg], [cdb], out=cdb.k(0, h, h + 1), in_=lb, func=AF.Exp, scale=128.0)
        OP("vector", "tensor_scalar", [lg], [tmp8], out=tmp8.k(0, 0, 1), in0=lf, scalar1=-1.0, scalar2=None, op0=ALU.mult)
        OP("scalar", "activation", [p1, tmp8], [e1col], out=e1col.k(0, h, h + 1), in_=p1.all(), func=AF.Exp,
           scale=tmp8.k(0, 0, 1))
        OP("vector", "tensor_scalar", [io_mc, lg], [tmpA], out=tmpA.all(), in0=io_mc.all(), scalar1=lb, scalar2=None, op0=ALU.mult)
        OP("vector", "tensor_scalar", [io_c1, lg], [tmpB], out=tmpB.all(), in0=io_c1.all(), scalar1=lf, scalar2=None, op0=ALU.mult)
        OP("vector", "tensor_tensor", [tmpA, tmpB], [tmpA], out=tmpA.all(), in0=tmpA.all(), in1=tmpB.all(), op=ALU.subtract)
        OP("scalar", "activation", [tmpA], [tmpA], out=tmpA.all(), in_=tmpA.all(), func=AF.Exp)
        OP("gpsimd", "affine_select", [tmpA], [tmpA], out=tmpA.all(), in_=tmpA.all(), pattern=[[-1, 128]],
           compare_op=ALU.is_gt, fill=0.0, base=0, channel_multiplier=1)
        OP("vector", "tensor_scalar", [onesf, e1col], [tmpB], out=tmpB.all(), in0=onesf.all(), scalar1=e1col.k(0, h, h + 1),
           scalar2=None, op0=ALU.mult)
        OP("gpsimd", "affine_select", [tmpB], [tmpB], out=tmpB.all(), in_=tmpB.all(), pattern=[[1, 128]],
           compare_op=ALU.is_ge, fill=0.0, base=0, channel_multiplier=-1)
        OP("vector", "tensor_tensor", [tmpA, tmpB], [tmpA], out=tmpA.all(), in0=tmpA.all(), in1=tmpB.all(), op=ALU.add)
        OP("vector", "tensor_scalar", [tmpA], [DTt], out=DTt.k(h), in0=tmpA.all(), scalar1=1.0 / 16, scalar2=None, op0=ALU.mult)
    OP("vector", "tensor_scalar", [kdf], [kdf], out=kdf.all(), in0=kdf.all(), scalar1=1.0 / 16, scalar2=None, op0=ALU.mult)
    OP("vector", "tensor_scalar", [kdb], [kdb], out=kdb.all(), in0=kdb.all(), scalar1=1.0 / 16, scalar2=None, op0=ALU.mult)
    OP("gpsimd", "iota", [iot_i], [iot_i], out=iot_i.k(0, 0, 8), pattern=[[1, 8]], base=0, channel_multiplier=0)
    OP("vector", "tensor_copy", [iot_i], [tmpB], out=tmpB.k(0, 0, 8), in_=iot_i.k(0, 0, 8))
    for gi, w in enumerate((2, 4, 8, 16)):
        OP("vector", "tensor_scalar", [tmpB], [icf], out=icf.k(gi), in0=tmpB.k(0, 0, 8), scalar1=float(w // 2), scalar2=float(w),
           op0=ALU.add, op1=ALU.min)
        OP("vector", "tensor_scalar", [tmpB], [icl], out=icl.k(gi), in0=tmpB.k(0, 0, 8), scalar1=-1.0, scalar2=float(8 + w // 2),
           op0=ALU.mult, op1=ALU.add)
        OP("vector", "tensor_scalar", [icl], [icl], out=icl.k(gi), in0=icl.k(gi), scalar1=float(w), scalar2=None, op0=ALU.min)
    OP("vector", "reciprocal", [icf], [icf], out=icf.all(), in_=icf.all())
    OP("vector", "reciprocal", [icl], [icl], out=icl.all(), in_=icl.all())
    OP("scalar", "activation", [io_p], [inv], out=inv.all(), in_=io_p.all(), func=AF.Exp, scale=-math.log(10000.0) / 128.0)
    CS = S.res("cs_scr")
    rt = [AR.alloc(f"rt{i}", F32, 1, 512) for i in range(4)]
    rti = AR.alloc("rti", I32, 1, 512)
    for blk in range(SMAX // 512 if SMAX % 512 == 0 else SMAX // 512 + 1):
        n = min(512, SMAX - blk * 512)
        OP("gpsimd", "iota", [iot_i], [iot_i], out=iot_i.k(0, 0, n), pattern=[[1, n]], base=blk * 512, channel_multiplier=0)
        OP("vector", "tensor_copy", [iot_i], [rt[0]], out=rt[0].k(0, 0, n), in_=iot_i.k(0, 0, n))
        OP("vector", "tensor_scalar", [rt[0], inv], [rt[0]], out=rt[0].k(0, 0, n), in0=rt[0].k(0, 0, n), scalar1=inv.all(),
           scalar2=None, op0=ALU.mult)
        for which in range(2):
            a = rt[1 + which]
            OP("vector", "tensor_scalar", [rt[0]], [a], out=a.k(0, 0, n), in0=rt[0].k(0, 0, n),
               scalar1=(math.pi / 2 if which == 0 else 0.0), scalar2=None, op0=ALU.add)
            OP("vector", "tensor_scalar", [a], [rt[3]], out=rt[3].k(0, 0, n), in0=a.k(0, 0, n), scalar1=1.0 / TWO_PI,
               scalar2=None, op0=ALU.mult)
            OP("vector", "tensor_copy", [rt[3]], [rti], out=rti.k(0, 0, n), in_=rt[3].k(0, 0, n))
            OP("vector", "tensor_copy", [rti], [rt[3]], out=rt[3].k(0, 0, n), in_=rti.k(0, 0, n))
            OP("vector", "scalar_tensor_tensor", [rt[3], a], [a], out=a.k(0, 0, n), in0=rt[3].k(0, 0, n), scalar=-TWO_PI,
               in1=a.k(0, 0, n), op0=ALU.mult, op1=ALU.add)
            OP("vector", "tensor_scalar", [a], [a], out=a.k(0, 0, n), in0=a.k(0, 0, n), scalar1=-3.1415925, scalar2=3.1415925,
               op0=ALU.max, op1=ALU.min)
            OP("scalar", "activation", [a], [a], out=a.k(0, 0, n), in_=a.k(0, 0, n), func=AF.Sin)
            DMA("sync", cs_d[which][:, blk * 512:blk * 512 + n], a.k(0, 0, n), [a], [CS])
    AR.release(*rt, rti)

    CONST += [ident, ones_bf, onesf, AFt, ABt, DTt, kdf, kdb, cdf, cdb, nhalf, icf, icl]
    for e in ("vector", "scalar", "gpsimd"):
        if e == "scalar":
            OP(e, "activation", CONST, [dummy[e]], out=dummy[e].k(0, 0, 1), in_=nhalf.all(), func=AF.Copy)
        else:
            OP(e, "memset", CONST, [dummy[e]], ap=dummy[e].all(), constant=0.0)
    fb = BankNow()
    OP("tensor", "transpose", CONST, [fb], out=fb.h.k(0, 0, 128), in_=ident.all(), identity=ident.all())
    fb.free()

    wq = []
    inflight = []
    wslot = [0]

    def pump():
        while len(inflight) < R_W and wq:
            b = wq.pop(0)
            v = wring[wslot[0] % R_W]
            wslot[0] += 1
            DMA("sync", v.all(), wb_d[b], [WB[b]], [v])
            inflight.append((b, v))

    def announce(blocks):
        wq.extend(blocks)
        pump()

    def wuse(b):
        bb, v = inflight[0]
        assert bb == b, (bb, b)
        return v

    def wdone():
        inflight.pop(0)
        pump()

    def rstd_from(ss_list, n):
        r = sm()
        if len(ss_list) == 2:
            OP("vector", "tensor_tensor", ss_list, [r], out=r.a, in0=ss_list[0].a, in1=ss_list[1].a, op=ALU.add)
            src = r
        else:
            src = ss_list[0]
        OP("vector", "tensor_scalar", [src], [r], out=r.a, in0=src.a, scalar1=1.0 / n, scalar2=EPS, op0=ALU.mult, op1=ALU.add)
        r2 = sm()
        OP("gpsimd", "tensor_tensor", [r], [r2], out=r2.a, in0=r.a, in1=nhalf.all(), op=ALU.pow)
        return r2

    def norm_transpose(xin_ap, xin_res, gvec, dstT, col0):
        PH[0] = f"{PH[0].split('/')[0]}/norm"
        junk = AR.alloc("junk", BF16, 1, D)
        xn = AR.alloc("xn", BF16, 1, D)
        ss = sm()
        OP("scalar", "activation", [xin_res], [junk, ss], out=junk.all(), in_=xin_ap, func=AF.Square, accum_out=ss.a)
        r = rstd_from([ss], D)
        OP("scalar", "activation", [xin_res, r], [xn], out=xn.all(), in_=xin_ap, func=AF.Copy, scale=r.a)
        bk = BankNow()
        for k in range(8):
            OP("tensor", "transpose", [xn], [bk], out=bk.h.k(0, 128 * k, 128 * k + 128), in_=xn.k(0, 128 * k, 128 * k + 128),
               identity=ident.all())
        gb = AP(tensor=gvec.base.tensor, offset=gvec.base.offset, ap=[gvec.p, [1, 8], [0, 128]])
        OP("vector", "tensor_tensor", [bk], [dstT], out=dstT.ks(0, 8, col0, col0 + 128),
           in0=bk.h.ap([[128, 8], [1, 128]]), in1=gb, op=ALU.mult)
        bk.free()
        AR.release(junk, xn)

    def projA(blkname, idx, rhsT, ncols, evac, nk=8):
        PH[0] = f"{PH[0].split('/')[0]}/A_{blkname}{idx}"
        b = BLK[blkname][idx]
        w = wuse(b)
        for oc in range(4):
            bk = BankNow()
            for k in range(nk):
                OP("tensor", "matmul", [w, rhsT], [bk], out=bk.f.k(0, 0, ncols), lhsT=w.k(k, 128 * oc, 128 * oc + 128),
                   rhs=rhsT.k(k, 0, ncols), start=(k == 0), stop=(k == nk - 1))
            if not evac(oc, bk):
                bk.free()
        wdone()

    def projB(blkname, idx, lhsT_view, ngroups, evac):
        PH[0] = f"{PH[0].split('/')[0]}/B_{blkname}{idx}"
        b = BLK[blkname][idx]
        w = wuse(b)
        for g in range(ngroups):
            bk = BankNow()
            for k in range(8):
                OP("tensor", "matmul", [w, lhsT_view], [bk], out=bk.f.all(), lhsT=lhsT_view.k(k, 128 * g, 128 * g + 128),
                   rhs=w.k(k), start=(k == 0), stop=(k == 7))
            evac(g, bk)
            bk.free()
        wdone()

    def rotary_pair(b1, b2, cs, out1, out2, post=None):
        t = [AR.alloc(f"rot{i}", F32, 1, T) for i in range(4)]
        OP("vector", "tensor_tensor", [b1, cs], [t[0]], out=t[0].all(), in0=b1.f.k(0, 0, T), in1=cs.k(0), op=ALU.mult)
        OP("vector", "tensor_tensor", [b2, cs], [t[1]], out=t[1].all(), in0=b2.f.k(0, 0, T), in1=cs.k(1), op=ALU.mult)
        OP("vector", "tensor_tensor", [b1, cs], [t[2]], out=t[2].all(), in0=b1.f.k(0, 0, T), in1=cs.k(1), op=ALU.mult)
        OP("vector", "tensor_tensor", [b2, cs], [t[3]], out=t[3].all(), in0=b2.f.k(0, 0, T), in1=cs.k(0), op=ALU.mult)
        out1(t[0], t[1], ALU.subtract)
        out2(t[2], t[3], ALU.add)
        AR.release(*t)

    def _tick(act):
        for g_ in list(act):
            try:
                next(g_)
            except StopIteration:
                act.remove(g_)

    def skewed(makers):
        act = []
        for mk in makers:
            act.append(mk())
            _tick(act)
            yield
        while act:
            _tick(act)
            yield

    def run_threads(threads):
        act = list(threads)
        while act:
            _tick(act)

    tok0 = 0
    ch0 = 0
    for si, SL in enumerate(seq_lens):
        NT = SL // T
        NCH = SL // 128
        SBR = [S.res(f"sbs{si}_{c}") for c in range(NCH)]

        def xrows(t0, n=128):
            return x_d[tok0 + t0:tok0 + t0 + n, :]

        PH[0] = "memkv"
        kmT = AR.alloc("kmT", BF16, 8, NMEM)
        vm = AR.alloc("vm", BF16, 2, D)
        mhT = AR.alloc("mhT", BF16, 8, NMEM)
        announce(BLK["memkv"])
        for g in range(2):
            xm = AR.alloc("xm", F32, 1, D)
            DMA("sync", xm.all(), mem_d[si * NMEM + 128 * g:si * NMEM + 128 * g + 128, :], [], [xm])
            norm_transpose(xm.all(), xm, gmem, mhT, 128 * g)
            AR.release(xm)
        for i in range(2):
            def ev(oc, bk, i=i):
                OP("scalar", "activation", [bk], [kmT], out=kmT.k(4 * i + oc), in_=bk.f.k(0, 0, NMEM), func=AF.Copy)
            projA("memkv", i, mhT, NMEM, ev)
        for i in range(2):
            def ev(g, bk, i=i):
                OP("scalar", "activation", [bk], [vm], out=vm.k(g, 512 * i, 512 * i + 512), in_=bk.f.all(), func=AF.Copy)
            projB("memkv", 2 + i, mhT, 2, ev)
        AR.release(mhT)


        T1 = max(t_ for t_ in (512, 256) if SL % t_ == 0 and t_ >= T)
        G1 = T1 // 128
        RR = T1 // T
        NT1 = SL // T1
        S32 = AR.alloc("S32b", F32, 4, 512)
        S32r = [S.res(f"S32b_{h}") for h in range(4)]
        KTS = [S.res(f"kts{si}_{t}") for t in range(NT)]
        VTS = [S.res(f"vts{si}_{t}") for t in range(NT)]
        tile0 = tok0 // T
        Sbf = [AR.alloc(f"Sbfb{i}", BF16, 4, 512) for i in range(2)]
        OP("gpsimd", "memset", [], [S32] + S32r, ap=S32.all(), constant=0.0)
        OP("gpsimd", "memset", [], [Sbf[0]], ap=Sbf[0].all(), constant=0.0)
        p1_blocks = [BLK["in"][i] for i in (2, 3, 4, 5)]
        tiles1 = list(reversed(range(NT1)))
        xorder = [(t1, g) for t1 in tiles1 for g in range(G1)]
        xfifo = []
        xnext = [0]

        def xpump(depth=6):
            while len(xfifo) < depth and xnext[0] < len(xorder):
                t1, g = xorder[xnext[0]]
                xnext[0] += 1
                xp = AR.alloc("xp", F32, 1, D)
                DMA("sync", xp.all(), xrows(t1 * T1 + 128 * g), [], [xp])
                xfifo.append(xp)

        def front(t1):
            PH[0] = "p1"
            announce(p1_blocks)
            hT1 = AR.alloc("hT1", BF16, 8, T1)
            for g in range(G1):
                xpump()
                xp = xfifo.pop(0)
                norm_transpose(xp.all(), xp, gpre, hT1, 128 * g)
                AR.release(xp)
                xpump()
            cs1 = AR.alloc("cs1", F32, 2, T1)
            for which in range(2):
                DMA("sync", cs1.k(which), cs_d[which][:, t1 * T1:t1 * T1 + T1], [CS], [cs1])
            kT1 = [AR.alloc(f"kT1_{r}", BF16, 8, T) for r in range(RR)]
            PH[0] = "p1/k"
            for i in range(2):
                w = wuse(BLK["in"][2 + i])
                for pr_ in range(2):
                    bks = []
                    for oc in (2 * pr_, 2 * pr_ + 1):
                        bk = BankNow()
                        for k in range(8):
                            OP("tensor", "matmul", [w, hT1], [bk], out=bk.f.k(0, 0, T1), lhsT=w.k(k, 128 * oc, 128 * oc + 128),
                               rhs=hT1.k(k), start=(k == 0), stop=(k == 7))
                        bks.append(bk)
                    c = 4 * i + 2 * pr_
                    t = [AR.alloc(f"rot{q}", F32, 1, T1) for q in range(4)]
                    OP("vector", "tensor_tensor", [bks[0], cs1], [t[0]], out=t[0].all(), in0=bks[0].f.k(0, 0, T1), in1=cs1.k(0), op=ALU.mult)
                    OP("vector", "tensor_tensor", [bks[1], cs1], [t[1]], out=t[1].all(), in0=bks[1].f.k(0, 0, T1), in1=cs1.k(1), op=ALU.mult)
                    OP("vector", "tensor_tensor", [bks[0], cs1], [t[2]], out=t[2].all(), in0=bks[0].f.k(0, 0, T1), in1=cs1.k(1), op=ALU.mult)
                    OP("vector", "tensor_tensor", [bks[1], cs1], [t[3]], out=t[3].all(), in0=bks[1].f.k(0, 0, T1), in1=cs1.k(0), op=ALU.mult)
                    bks[0].free()
                    bks[1].free()
                    for r in range(RR):
                        OP("gpsimd", "tensor_tensor", [t[0], t[1]], [kT1[r]], out=kT1[r].k(c), in0=t[0].k(0, r * T, r * T + T),
                           in1=t[1].k(0, r * T, r * T + T), op=ALU.subtract)
                        OP("gpsimd", "tensor_tensor", [t[2], t[3]], [kT1[r]], out=kT1[r].k(c + 1), in0=t[2].k(0, r * T, r * T + T),
                           in1=t[3].k(0, r * T, r * T + T), op=ALU.add)
                    AR.release(*t)
                wdone()
            AR.release(cs1)
            for r in range(RR):
                DMA("sync", kts_d[tile0 + t1 * RR + r], kT1[r].all(), [kT1[r]], [KTS[t1 * RR + r]])
            PH[0] = "p1/v"
            vt1 = AR.alloc("vt1", BF16, G1, D)
            for i in range(2):
                w = wuse(BLK["in"][4 + i])
                for g in range(G1):
                    bk = BankNow()
                    for k in range(8):
                        OP("tensor", "matmul", [w, hT1], [bk], out=bk.f.all(), lhsT=hT1.k(k, 128 * g, 128 * g + 128),
                           rhs=w.k(k), start=(k == 0), stop=(k == 7))
                    OP("scalar", "activation", [bk], [vt1], out=vt1.k(g, 512 * i, 512 * i + 512), in_=bk.f.all(), func=AF.Copy)
                    bk.free()
                wdone()
            AR.release(hT1)
            for r in range(RR):
                DMA("sync", vts_d[tile0 + t1 * RR + r], vt1.ap([[1, G * D]], r * G * D), [vt1], [VTS[t1 * RR + r]])
            PH[0] = "p1/kb"
            kb1 = AR.alloc("kb1", BF16, G1, D)
            for g in range(G1):
                bk = BankNow()
                for c in range(8):
                    OP("tensor", "transpose", [kT1[g // G]], [bk], out=bk.h.k(0, 128 * c, 128 * c + 128), in_=kT1[g // G].k(c, 128 * (g % G), 128 * (g % G) + 128),
                       identity=ident.all())
                for h in range(4):
                    OP("vector", "tensor_scalar", [bk], [kb1], out=kb1.k(g, 256 * h, 256 * h + 256), in0=bk.h.k(0, 256 * h, 256 * h + 256),
                       scalar1=kdb.k(0, h, h + 1), scalar2=None, op0=ALU.mult)
                bk.free()
            AR.release(*kT1)
            return (t1, kb1, vt1)

        def back(fr):
            t1, kb1, vt1 = fr
            PH[0] = "p1/state"
            for g in reversed(range(G1)):
                gc = t1 * G1 + g
                par_ = (NCH - 1 - gc) % 2
                DMA("sync", sb_d[ch0 + gc], Sbf[par_].all(), [Sbf[par_]], [SBR[gc]])
                for h in range(4):
                    bk = BankNow()
                    for dd in range(2):
                        OP("tensor", "matmul", [kb1, vt1], [bk], out=bk.f.k(0, 256 * dd, 256 * dd + 256),
                           lhsT=kb1.k(g, 256 * h + 128 * dd, 256 * h + 128 * dd + 128), rhs=vt1.k(g, 256 * h, 256 * h + 256),
                           start=True, stop=True)
                    OP("vector", "scalar_tensor_tensor", [S32r[h], bk], [S32r[h]], out=S32.k(h), in0=S32.k(h), scalar=cdb.k(0, h, h + 1),
                       in1=bk.f.all(), op0=ALU.mult, op1=ALU.add)
                    bk.free()
                    OP("scalar", "activation", [S32r[h]], [Sbf[1 - par_]], out=Sbf[1 - par_].k(h), in_=S32.k(h), func=AF.Copy)
            AR.release(kb1, vt1)

        prev = None
        for t1 in tiles1:
            cur = front(t1)
            if prev is not None:
                back(prev)
            prev = cur
        back(prev)
        assert not xfifo
        AR.release(S32, *Sbf)

        NX = 2 * G + 1
        xring = [AR.alloc(f"xr{i}", F32, 1, D) for i in range(NX)]
        pring = AR.alloc("pring", F32, 8, T + 16)
        wa_p = AR.alloc("wa", F32, 2, T + 16)
        Gr = AR.alloc("Gring", BF16, NPAIR, T + 128)
        carry = AR.alloc("h2carry", BF16, 8, 2)
        S32 = AR.alloc("S32f", F32, 4, 512)
        S32res = [S.res(f"S32f_{h}") for h in range(4)]
        _sbf1 = AR.alloc("Sbff", BF16, 4, 512)
        Sbf = [_sbf1, _sbf1]
        SbfR = [S.res(f"SbfR{h}") for h in range(4)]
        UW = T + 3
        for v in (pring, carry):
            OP("gpsimd", "memset", [], [v], ap=v.all(), constant=0.0)
        OP("gpsimd", "memset", [], [_sbf1] + SbfR, ap=_sbf1.all(), constant=0.0)
        OP("gpsimd", "memset", [], [S32] + S32res, ap=S32.all(), constant=0.0)
        par = 0

        def load_x(ti):
            for g in range(G):
                v = xring[(ti * G + g) % NX]
                DMA("sync", v.all(), xrows(ti * T + 128 * g), [], [v])

        def step_blocks(is_last):
            return ([BLK["in"][i] for i in (10, 11, 6, 7, 8, 9)]
                    + BLK["poolw"]
                    + [BLK["in"][i] for i in (12, 13, 14, 15, 16, 17)]
                    + [BLK["pool_out"][0], BLK["mem_out"][0], BLK["ret_out"][0],
                       BLK["pool_out"][1], BLK["mem_out"][1], BLK["ret_out"][1]]
                    + BLK["o"] + BLK["up"]
                    + ([] if is_last else [BLK["in"][i] for i in (0, 1)]) + BLK["down"])

        def ffn_down(groups, x1slots, ytoks):
            ng = len(groups)
            y2 = [AR.alloc(f"y2_{i}", F32, 1, D) for i in range(ng)]
            for half in range(2):
                bks = [BankNow() for _ in range(ng)]
                for kg in range(3):
                    b = BLK["down"][half * 3 + kg]
                    w = wuse(b)
                    nk = min(8, NPAIR - 8 * kg)
                    for gi, gcol in enumerate(groups):
                        for k in range(nk):
                            kk = 8 * kg + k
                            OP("tensor", "matmul", [w, Gr], [bks[gi]], out=bks[gi].f.all(),
                               lhsT=Gr.k(kk, 128 * gcol, 128 * gcol + 128), rhs=w.k(k), start=(kk == 0), stop=(kk == NPAIR - 1))
                    wdone()
                for gi in range(ng):
                    OP("scalar", "activation", [bks[gi]], [y2[gi]], out=y2[gi].k(0, 512 * half, 512 * half + 512),
                       in_=bks[gi].f.all(), func=AF.Copy)
                    bks[gi].free()
            for gi in range(ng):
                junk = AR.alloc("junk2", BF16, 1, D)
                ss = sm()
                OP("scalar", "activation", [y2[gi]], [junk, ss], out=junk.all(), in_=y2[gi].all(), func=AF.Square, accum_out=ss.a)
                r = rstd_from([ss], D)
                OP("vector", "scalar_tensor_tensor", [y2[gi], r], [y2[gi]], out=y2[gi].all(), in0=y2[gi].all(), scalar=r.a,
                   in1=gfpost.all(), op0=ALU.mult, op1=ALU.mult)
                xs = x1slots[gi]
                OP("gpsimd", "tensor_tensor", [y2[gi], xs], [xs], out=xs.all(), in0=y2[gi].all(), in1=xs.all(), op=ALU.add)
                DMA("gpsimd", y_d[tok0 + ytoks[gi]:tok0 + ytoks[gi] + 128, :], xs.all(), [xs], [])
                AR.release(junk)
            AR.release(*y2)

        def head(ti, hT):
            PH[0] = "p2h"
            cs = csbuf[ti % 2]
            for which in range(2):
                DMA("sync", cs.k(which), cs_d[which][:, ti * T:ti * T + T], [CS], [cs])
            kT = AR.alloc("kT", BF16, 8, T)
            DMA("sync", kT.all(), kts_d[tile0 + ti], [KTS[ti]], [kT])
            vt = AR.alloc("vt", BF16, G, D)
            DMA("sync", vt.all(), vts_d[tile0 + ti], [VTS[ti]], [vt])
            qf = AR.alloc("qf", BF16, 8, T)
            qb = AR.alloc("qb", BF16, 8, T)
            for i in range(2):
                held = {}

                def ev(oc, bk, i=i, held=held):
                    if oc % 2 == 0:
                        held[0] = bk
                        return True
                    c = 4 * i + oc - 1
                    h = c // 2
                    o = [AR.alloc("qo0", F32, 1, T), AR.alloc("qo1", F32, 1, T)]

                    def mk(j):
                        def f(a, b_, op, j=j):
                            OP("gpsimd", "tensor_tensor", [a, b_], [o[j]], out=o[j].all(), in0=a.all(), in1=b_.all(), op=op)
                            src = o[j].ap([[128, G], [1, 128]])
                            for tab, dst in ((AFt, qf), (ABt, qb)):
                                tb = AP(tensor=tab.base.tensor, offset=tab.base.offset + 128 * h, ap=[tab.p, [0, G], [1, 128]])
                                OP("vector", "tensor_tensor", [o[j]], [dst], out=dst.ap([[128, G], [1, 128]], (c + j) * T),
                                   in0=src, in1=tb, op=ALU.mult)
                        return f
                    rotary_pair(held[0], bk, cs, mk(0), mk(1))
                    held[0].free()
                    AR.release(*o)
                projA("in", 0 + i, hT, T, ev)
            return dict(kT=kT, vt=vt, qf=qf, qb=qb, hT=hT)

        load_x(0)
        announce([BLK["in"][i] for i in (0, 1)])
        announce(step_blocks(NT == 1))
        hT_next = [None]
        hT0 = AR.alloc("hT", BF16, 8, T)
        for g in range(G):
            norm_transpose(xring[g % NX].all(), xring[g % NX], gpre, hT0, 128 * g)
        HD = head(0, hT0)
        for ti in range(NT):
            first = ti == 0
            last = ti == NT - 1
            PH[0] = "p2"
            if not last:
                load_x(ti + 1)
                announce(step_blocks(ti + 1 == NT - 1))
            xs = [xring[(ti * G + g) % NX] for g in range(G)]
            Sb = [AR.alloc(f"Sb{g}", BF16, 4, 512) for g in range(G)]
            for g in range(G):
                DMA("sync", Sb[g].all(), sb_d[ch0 + ti * G + g], [SBR[ti * G + g]], [Sb[g]])
            kT, vt, qf, qb, hT = HD["kT"], HD["vt"], HD["qf"], HD["qb"], HD["hT"]
            mq = AR.alloc("mq", BF16, 8, T)
            for i in range(2):
                def ev(oc, bk, i=i):
                    OP("scalar", "activation", [bk], [mq], out=mq.k(4 * i + oc), in_=bk.f.k(0, 0, T), func=AF.Copy)
                projA("in", 10 + i, hT, T, ev)
            PH[0] = "p2"
            silu = AR.alloc("silu", BF16, 8, T)
            for i in range(2):
                def ev(oc, bk, i=i):
                    OP("scalar", "activation", [bk], [silu], out=silu.k(4 * i + oc), in_=bk.f.k(0, 0, T), func=AF.Silu)
                projA("in", 6 + i, hT, T, ev)
            if not last:
                PH[0] = "p2pre"
                hT_next[0] = AR.alloc("hT", BF16, 8, T)
                for g in range(G):
                    xn_ = xring[((ti + 1) * G + g) % NX]
                    norm_transpose(xn_.all(), xn_, gpre, hT_next[0], 128 * g)
            PH[0] = "p2"
            for i in range(2):
                PH[0] = f"p2/A_in{8 + i}"
                w = wuse(BLK["in"][8 + i])
                for oc in range(4):
                    bk = BankNow()
                    for k in range(8):
                        OP("tensor", "matmul", [w, hT], [bk], out=bk.f.k(0, 0, T), lhsT=w.k(k, 128 * oc, 128 * oc + 128),
                           rhs=hT.k(k), start=(k == 0), stop=(k == 7))
                    nw = T
                    if not last:
                        for k in range(8):
                            OP("tensor", "matmul", [w, hT_next[0]], [bk], out=bk.f.k(0, T, T + 8), lhsT=w.k(k, 128 * oc, 128 * oc + 128),
                               rhs=hT_next[0].k(k, 0, 8), start=(k == 0), stop=(k == 7))
                        nw = T + 8
                    OP("scalar", "activation", [bk], [pring], out=pring.k(4 * i + oc, 8, 8 + nw), in_=bk.f.k(0, 0, nw), func=AF.Copy)
                    bk.free()
                wdone()
            PH[0] = "p2/kf_tr"
            kf = AR.alloc("kf", BF16, G, D)
            for g in range(G):
                bk = BankNow()
                for c in range(8):
                    OP("tensor", "transpose", [kT], [bk], out=bk.h.k(0, 128 * c, 128 * c + 128), in_=kT.k(c, 128 * g, 128 * g + 128),
                       identity=ident.all())
                for h in range(4):
                    OP("vector", "tensor_scalar", [bk], [kf], out=kf.k(g, 256 * h, 256 * h + 256), in0=bk.h.k(0, 256 * h, 256 * h + 256),
                       scalar1=kdf.k(0, h, h + 1), scalar2=None, op0=ALU.mult)
                bk.free()
            yn = AR.alloc("yn", BF16, G, D)
            PTs = {}

            def scores_gen(g):
                c0, c1 = 128 * g, 128 * g + 128
                (bs,) = yield from gbanks(1)
                PH[0] = "p2/ret_sc"
                for h in range(4):
                    for dd in range(2):
                        OP("tensor", "matmul", [kT, qf], [bs], out=bs.f.k(0, 128 * h, 128 * h + 128), lhsT=kT.k(2 * h + dd, c0, c1),
                           rhs=qf.k(2 * h + dd, c0, c1), start=(dd == 0), stop=(dd == 1))
                PT = AR.alloc("PT", BF16, 4, 128)
                PTs[g] = PT
                OP("vector", "tensor_tensor", [bs], [PT], out=PT.all(), in0=bs.f.all(), in1=DTt.all(), op=ALU.mult)
                bs.free()
                yield

            def head_gen(g, h):
                c0, c1 = 128 * g, 128 * g + 128
                pr_ = (ti * G + g) % 2
                PT = PTs[g]
                by, bk = yield from gbanks(2)
                PH[0] = "p2/ret_hd"
                yo = by.f.k(0, 0, 256)
                OP("tensor", "matmul", [PT, vt], [by], out=yo, lhsT=PT.k(h), rhs=vt.k(g, 256 * h, 256 * h + 256), start=True, stop=False)
                for dd in range(2):
                    OP("tensor", "matmul", [qf, SbfR[h]], [by], out=yo, lhsT=qf.k(2 * h + dd, c0, c1),
                       rhs=Sbf[pr_].k(h, 256 * dd, 256 * dd + 256), start=False, stop=False)
                for dd in range(2):
                    OP("tensor", "matmul", [qb, Sb[g]], [by], out=yo, lhsT=qb.k(2 * h + dd, c0, c1),
                       rhs=Sb[g].k(h, 256 * dd, 256 * dd + 256), start=False, stop=(dd == 1))
                for dd in range(2):
                    OP("tensor", "matmul", [kf, vt], [bk], out=bk.f.k(0, 256 * dd, 256 * dd + 256),
                       lhsT=kf.k(g, 256 * h + 128 * dd, 256 * h + 128 * dd + 128), rhs=vt.k(g, 256 * h, 256 * h + 256),
                       start=True, stop=True)
                yield
                st = sm(6)
                mv = sm(2)
                OP("vector", "bn_stats", [by], [st], out=st.a, in_=yo)
                OP("vector", "bn_aggr", [st], [mv], out=mv.a, in_=st.a)
                rs = sm()
                OP("vector", "tensor_scalar", [mv], [rs], out=rs.a, in0=mv.sub(1), scalar1=EPS, scalar2=None, op0=ALU.add)
                OP("vector", "scalar_tensor_tensor", [S32res[h], bk], [S32res[h]], out=S32.k(h), in0=S32.k(h), scalar=cdf.k(0, h, h + 1),
                   in1=bk.f.all(), op0=ALU.mult, op1=ALU.add)
                bk.free()
                yield
                rs2 = sm()
                OP("gpsimd", "tensor_tensor", [rs], [rs2], out=rs2.a, in0=rs.a, in1=nhalf.all(), op=ALU.pow)
                OP("scalar", "activation", [S32res[h]], [SbfR[h]], out=Sbf[1 - pr_].k(h), in_=S32.k(h), func=AF.Copy)
                yield
                OP("vector", "tensor_scalar", [by, mv, rs2], [yn], out=yn.k(g, 256 * h, 256 * h + 256), in0=yo, scalar1=mv.sub(0), scalar2=rs2.a,
                   op0=ALU.subtract, op1=ALU.mult)
                by.free()
                if h == 3:
                    AR.release(PT)
                yield

            def ret_thread():
                mk = []
                for g in range(G):
                    mk.append(lambda g=g: scores_gen(g))
                    for h in range(4):
                        mk.append(lambda g=g, h=h: head_gen(g, h))
                yield from skewed(mk)

            dsl = AR.alloc("dsl", BF16, 8, T)
            poolp = AR.alloc("poolp", BF16, 8, T)

            def pool_thread():
                if last:
                    OP("gpsimd", "memset", [], [pring], ap=pring.ks(0, 8, T + 8, T + 16), constant=0.0)
                W = T + 16
                for gi, w in enumerate((2, 4, 8, 16)):
                    wa = wa_p
                    sk = 2 * gi
                    ln = W
                    step = 1
                    cur = None
                    bufs = [wa, wa]
                    bi = 0
                    while step < w:
                        ln2 = ln - step
                        dst = bufs[bi]
                        if cur is None:
                            in0 = pring.ks(sk, sk + 2, 0, ln2)
                            in1 = pring.ks(sk, sk + 2, step, step + ln2)
                            rd = [pring]
                        else:
                            in0 = cur.ks(0, 2, 0, ln2)
                            in1 = cur.ks(0, 2, step, step + ln2)
                            rd = [cur]
                        OP("gpsimd", "tensor_tensor", rd, [dst], out=dst.ks(0, 2, 0, ln2), in0=in0, in1=in1, op=ALU.add)
                        cur = dst
                        bi = 1 - bi
                        ln = ln2
                        step *= 2
                    yield
                    o = 8 - w // 2
                    OP("vector", "scalar_tensor_tensor", [cur, pring], [dsl], out=dsl.ks(sk, sk + 2), in0=cur.ks(0, 2, o, o + T),
                       scalar=1.0 / w, in1=pring.ks(sk, sk + 2, 8, 8 + T), op0=ALU.mult, op1=ALU.subtract)
                    if first or last:
                        e0 = 0 if first else T - 8
                        ic = icf if first else icl
                        tt = AR.alloc("edge", F32, 2, 8)
                        icb = AP(tensor=ic.base.tensor, offset=ic.base.offset + 8 * gi, ap=[ic.p, [0, 2], [1, 8]])
                        OP("gpsimd", "tensor_tensor", [cur], [tt], out=tt.ks(0, 2), in0=cur.ks(0, 2, o + e0, o + e0 + 8), in1=icb, op=ALU.mult)
                        OP("gpsimd", "tensor_tensor", [tt, pring], [dsl], out=dsl.ks(sk, sk + 2, e0, e0 + 8), in0=tt.ks(0, 2),
                           in1=pring.ks(sk, sk + 2, 8 + e0, 16 + e0), op=ALU.subtract)
                        AR.release(tt)
                    yield
                OP("gpsimd", "tensor_copy", [pring], [pring], out=pring.ks(0, 8, 0, 8), in_=pring.ks(0, 8, T, T + 8))
                w = wuse(BLK["poolw"][0])
                for gi in range(4):
                    for oc in range(2):
                        (bk,) = yield from gbanks(1)
                        PH[0] = "p2/pool"
                        for k in range(2):
                            OP("tensor", "matmul", [w, dsl], [bk], out=bk.f.k(0, 0, T), lhsT=w.k(2 * gi + k, 128 * oc, 128 * oc + 128),
                               rhs=dsl.k(2 * gi + k), start=(k == 0), stop=(k == 1))
                        c = 2 * gi + oc
                        OP("scalar", "activation", [bk], [poolp], out=poolp.k(c), in_=bk.f.k(0, 0, T), func=AF.Copy, scale=psc.k(0, c, c + 1))
                        bk.free()
                    yield
                wdone()

            memT = AR.alloc("memT", BF16, 8, T)

            def mem_head(h):
                pr = AR.alloc("probs", BF16, 2, T)
                bks = []
                bks_ = yield from gbanks(2)
                PH[0] = "p2/mem1"
                for mc in range(2):
                    bk = bks_[mc]
                    for dd in range(2):
                        OP("tensor", "matmul", [kmT, mq], [bk], out=bk.f.k(0, 0, T), lhsT=kmT.k(2 * h + dd, 128 * mc, 128 * mc + 128),
                           rhs=mq.k(2 * h + dd), start=(dd == 0), stop=(dd == 1))
                    bks.append(bk)
                yield
                for mc in range(2):
                    OP("scalar", "activation", [bks[mc]], [pr], out=pr.k(mc), in_=bks[mc].f.k(0, 0, T), func=AF.Exp, scale=1.0 / 16)
                    bks[mc].free()
                yield
                bd, bo0, bo1 = yield from gbanks(3)
                PH[0] = "p2/mem2"
                for mc in range(2):
                    OP("tensor", "matmul", [pr], [bd], out=bd.f.k(0, 0, T), lhsT=ones_bf.all(), rhs=pr.k(mc), start=(mc == 0), stop=(mc == 1))
                bos = []
                for ec in range(2):
                    bo = (bo0, bo1)[ec]
                    for mc in range(2):
                        OP("tensor", "matmul", [vm, pr], [bo], out=bo.f.k(0, 0, T), lhsT=vm.k(mc, 256 * h + 128 * ec, 256 * h + 128 * ec + 128),
                           rhs=pr.k(mc), start=(mc == 0), stop=(mc == 1))
                    bos.append(bo)
                yield
                rec = AR.alloc("rec", F32, 1, T)
                OP("scalar", "activation", [bd], [rec], out=rec.all(), in_=bd.f.k(0, 0, T), func=AF.Ln)
                OP("scalar", "activation", [rec], [rec], out=rec.all(), in_=rec.all(), func=AF.Exp, scale=-1.0)
                bd.free()
                yield
                for ec in range(2):
                    OP("vector", "tensor_tensor", [bos[ec], rec], [memT], out=memT.k(2 * h + ec), in0=bos[ec].f.k(0, 0, T), in1=rec.all(), op=ALU.mult)
                    bos[ec].free()
                AR.release(pr, rec)
                yield

            def mem_thread():
                for h in range(4):
                    yield from mem_head(h)

            run_threads([ret_thread(), pool_thread(), mem_thread()])
            AR.release(kT, vt, qf, qb, kf, *Sb)
            AR.release(dsl, mq)
            PH[0] = "p2"
            gates = [AR.alloc(f"gates{j}", BF16, 8, T) for j in range(3)]
            for i in range(6):
                def ev(oc, bk, i=i):
                    OP("scalar", "activation", [bk], [gates[(4 * i + oc) // 8]], out=gates[(4 * i + oc) // 8].k((4 * i + oc) % 8), in_=bk.f.k(0, 0, T), func=AF.Sigmoid)
                projA("in", 12 + i, hT, T, ev)
            AR.release(hT)
            PH[0] = "p2/merge"
            merged = AR.alloc("merged", BF16, 8, T)
            retT = AR.alloc("retT", BF16, 8, T)
            srcs = (poolp, memT, retT)
            MNAMES = ("pool_out", "mem_out", "ret_out")
            GIDX = (1, 2, 0)

            def retT_gen():
                for fc in range(8):
                    (bk,) = yield from gbanks(1)
                    PH[0] = "p2/retT"
                    for g in range(G):
                        OP("tensor", "transpose", [yn], [bk], out=bk.h.k(0, 128 * g, 128 * g + 128), in_=yn.k(g, 128 * fc, 128 * fc + 128),
                           identity=ident.all())
                    OP("vector", "scalar_tensor_tensor", [bk, silu], [retT], out=retT.k(fc), in0=bk.h.k(0, 0, T), scalar=rgn.k(0, fc, fc + 1),
                       in1=silu.k(fc), op0=ALU.mult, op1=ALU.mult)
                    bk.free()
                AR.release(yn, silu)
                yield
            accs8 = [AR.alloc(f"macc{q}", F32, 1, T) for q in range(8)]
            wsm = {}
            if True:
                def mg(half, j, oc4):
                    accs = accs8[4 * half:4 * half + 4]
                    oc = 4 * half + oc4
                    (bk,) = yield from gbanks(1)
                    PH[0] = "p2/merge"
                    if oc4 == 0:
                        wsm[0] = wuse(BLK[MNAMES[j]][half])
                    w = wsm[0]
                    for k in range(8):
                        OP("tensor", "matmul", [w, srcs[j]], [bk], out=bk.f.k(0, 0, T), lhsT=w.k(k, 128 * oc4, 128 * oc4 + 128),
                           rhs=srcs[j].k(k), start=(k == 0), stop=(k == 7))
                    if oc4 == 3:
                        wdone()
                    yield
                    if j == 0:
                        OP("vector", "tensor_tensor", [bk, gates[GIDX[j]]], [accs[oc4]], out=accs[oc4].all(), in0=bk.f.k(0, 0, T), in1=gates[GIDX[j]].k(oc), op=ALU.mult)
                        bk.free()
                        yield
                    else:
                        tmp = AR.alloc("mtmp", F32, 1, T)
                        OP("vector", "tensor_tensor", [bk, gates[GIDX[j]]], [tmp], out=tmp.all(), in0=bk.f.k(0, 0, T), in1=gates[GIDX[j]].k(oc), op=ALU.mult)
                        bk.free()
                        yield
                        if j == 1:
                            OP("gpsimd", "tensor_tensor", [accs[oc4], tmp], [accs[oc4]], out=accs[oc4].all(), in0=accs[oc4].all(), in1=tmp.all(), op=ALU.add)
                        else:
                            OP("gpsimd", "tensor_tensor", [accs[oc4], tmp], [merged], out=merged.k(oc), in0=accs[oc4].all(), in1=tmp.all(), op=ALU.add)
                        AR.release(tmp)
                        yield

                mk_ = []
                for half in range(2):
                    for j in range(3):
                        if half == 0 and j == 2:
                            mk_.append(retT_gen)
                        for oc4 in range(4):
                            mk_.append(lambda half=half, j=j, oc4=oc4: mg(half, j, oc4))
                run_threads([skewed(mk_)])
                AR.release(*accs8)
            AR.release(retT, poolp, memT, *gates)
            PH[0] = "p2wo"
            h2T = AR.alloc("h2T", BF16, 8, T + 3)
            OP("gpsimd", "tensor_copy", [carry], [h2T], out=h2T.ks(0, 8, 0, 2), in_=carry.ks(0, 8))
            if last:
                OP("gpsimd", "memset", [], [h2T], ap=h2T.ks(0, 8, T + 2, T + 3), constant=0.0)
            wo = [inflight[0][1], inflight[1][1]]
            assert inflight[0][0] == BLK["o"][0] and inflight[1][0] == BLK["o"][1]
            for g in range(G):
                bk2 = []
                for half in range(2):
                    bk = BankNow()
                    for k in range(8):
                        OP("tensor", "matmul", [wo[half], merged], [bk], out=bk.f.all(), lhsT=merged.k(k, 128 * g, 128 * g + 128),
                           rhs=wo[half].k(k), start=(k == 0), stop=(k == 7))
                    bk2.append(bk)
                junk = AR.alloc("junk3", BF16, 1, 512)
                ssl = []
                for half in range(2):
                    ss = sm()
                    OP("scalar", "activation", [bk2[half]], [junk, ss], out=junk.all(), in_=bk2[half].f.all(), func=AF.Square, accum_out=ss.a)
                    ssl.append(ss)
                r = rstd_from(ssl, D)
                tt = AR.alloc("wot", F32, 1, D)
                for half in range(2):
                    OP("vector", "scalar_tensor_tensor", [bk2[half], r], [tt], out=tt.k(0, 512 * half, 512 * half + 512),
                       in0=bk2[half].f.all(), scalar=r.a, in1=gpost.k(0, 512 * half, 512 * half + 512), op0=ALU.mult, op1=ALU.mult)
                OP("gpsimd", "tensor_tensor", [tt, xs[g]], [xs[g]], out=xs[g].all(), in0=tt.all(), in1=xs[g].all(), op=ALU.add)
                for bk in bk2:
                    bk.free()
                AR.release(junk, tt)
                norm_transpose(xs[g].all(), xs[g], gfpre, h2T, 2 + 128 * g)
            wdone()
            wdone()
            AR.release(merged)
            OP("gpsimd", "tensor_copy", [h2T], [carry], out=carry.ks(0, 8), in_=h2T.ks(0, 8, T, T + 2))
            NO = T + 1 if last else T
            NC = NO + 2
            wcur = {}

            def pair_gen(j):
                bi_, jj = j // 2, j % 2
                bg, bv = yield from gbanks(2)
                PH[0] = "p2/ffn_up"
                if jj == 0:
                    wcur[0] = wuse(BLK["up"][bi_])
                w = wcur[0]
                for k in range(8):
                    OP("tensor", "matmul", [w, h2T], [bg], out=bg.f.k(0, 0, NC), lhsT=w.k(k, 128 * jj, 128 * jj + 128), rhs=h2T.k(k, 0, NC),
                       start=(k == 0), stop=(k == 7))
                for k in range(8):
                    OP("tensor", "matmul", [w, h2T], [bv], out=bv.f.k(0, 0, NC), lhsT=w.k(k, 256 + 128 * jj, 256 + 128 * jj + 128),
                       rhs=h2T.k(k, 0, NC), start=(k == 0), stop=(k == 7))
                if jj == 1:
                    wdone()
                yield
                cvs = []
                for (bk, ci) in ((bg, j), (bv, NPAIR + j)):
                    cv = AR.alloc("cv", F32, 1, T + 1)
                    OP("scalar", "activation", [bk], [cv], out=cv.k(0, 0, NO), in_=bk.f.k(0, 0, NO), func=AF.Identity,
                       scale=cw.k(0, ci, ci + 1), bias=cb.k(0, ci, ci + 1))
                    cvs.append(cv)
                yield
                for (bk, cv, ci) in ((bg, cvs[0], j), (bv, cvs[1], NPAIR + j)):
                    OP("vector", "scalar_tensor_tensor", [bk, cv], [cv], out=cv.k(0, 0, NO), in0=bk.f.k(0, 1, 1 + NO), scalar=cw.k(1, ci, ci + 1),
                       in1=cv.k(0, 0, NO), op0=ALU.mult, op1=ALU.add)
                    OP("vector", "scalar_tensor_tensor", [bk, cv], [cv], out=cv.k(0, 0, NO), in0=bk.f.k(0, 2, 2 + NO), scalar=cw.k(2, ci, ci + 1),
                       in1=cv.k(0, 0, NO), op0=ALU.mult, op1=ALU.add)
                    bk.free()
                cg, cv = cvs
                yield
                sq = AR.alloc("gsq", F32, 1, T + 1)
                OP("scalar", "activation", [cg], [sq], out=sq.k(0, 0, NO), in_=cg.k(0, 0, NO), func=AF.Square)
                gm = AR.alloc("gm", F32, 1, T + 1)
                OP("gpsimd", "tensor_tensor", [cg, cv], [gm], out=gm.k(0, 0, NO), in0=cg.k(0, 0, NO), in1=cv.k(0, 0, NO), op=ALU.mult)
                yield
                OP("gpsimd", "tensor_scalar", [sq], [sq], out=sq.k(0, 0, NO), in0=sq.k(0, 0, NO), scalar1=0.044715, scalar2=1.0,
                   op0=ALU.mult, op1=ALU.add)
                OP("gpsimd", "tensor_tensor", [sq, cg], [sq], out=sq.k(0, 0, NO), in0=sq.k(0, 0, NO), in1=cg.k(0, 0, NO), op=ALU.mult)
                yield
                OP("scalar", "activation", [sq], [sq], out=sq.k(0, 0, NO), in_=sq.k(0, 0, NO), func=AF.Sigmoid, scale=1.5957691216057308)
                yield
                OP("vector", "tensor_tensor", [gm, sq], [Gr], out=Gr.k(j, 127, 127 + NO), in0=gm.k(0, 0, NO), in1=sq.k(0, 0, NO), op=ALU.mult)
                AR.release(cg, cv, sq, gm)
                yield

            run_threads([skewed([(lambda j=j: pair_gen(j)) for j in range(NPAIR)])])
            AR.release(h2T)
            if not last:
                HD = head(ti + 1, hT_next[0])
                hT_next[0] = None
            PH[0] = "p2/ffn_down"
            groups = list(range(0 if not first else 1, G + (1 if last else 0)))
            slots = []
            toks = []
            for gcol in groups:
                gidx = ti * G + gcol - 1
                slots.append(xring[gidx % NX])
                toks.append(128 * gidx)
            ffn_down(groups, slots, toks)
            if not last:
                OP("gpsimd", "tensor_copy", [Gr], [Gr], out=Gr.ks(0, NPAIR, 0, 128), in_=Gr.ks(0, NPAIR, T, T + 128))
        AR.release(*xring, pring, wa_p, Gr, carry, S32, _sbf1, kmT, vm)
        tok0 += SL
        ch0 += NCH

    assert not wq and not inflight, (wq, inflight)
    S.emit()
    es.close()
    build_program.tags = {e: [i.tag for i in S.streams[e] if not i.is_dma] for e in ENGS}
    build_program.stats = dict(peak_pages=AR.peak, pages=AR.np_,
                               ninstr={e: len(S.streams[e]) for e in ENGS})
    return nc


T_TILE = 256
_cache = {}


def kernel(**inputs):
    f = lambda a: np.ascontiguousarray(np.asarray(a, dtype=np.float32))
    xp = f(inputs["x_prompt"])
    xs = f(inputs["x_sample"])
    mp = f(inputs["mem_prompt"])
    ms = f(inputs["mem_sample"])
    NC = 8
    nb_p = xp.shape[0] // NC
    nb_s = xs.shape[0] // NC
    SP, SS = xp.shape[1], xs.shape[1]
    seq_lens = [SP] * nb_p + [SS] * nb_s
    key = (tuple(seq_lens), T_TILE)
    if key not in _cache:
        _cache[key] = build_program(seq_lens, T_TILE)
    nc = _cache[key]
    shared = {}
    for nm in ("g_mix_pre", "g_mem", "ret_gn", "pool_scale", "g_ffn_pre", "conv_b"):
        shared[nm] = f(inputs[nm])[0].reshape(-1)
    for nm in ("g_mix_post", "g_ffn_post", "decay_fwd", "decay_bwd"):
        shared[nm] = f(inputs[nm])[0].reshape(1, -1)
    for nm in ("w_in", "w_ret_out", "pool_w", "w_pool_out", "w_mem_kv", "w_mem_out", "w_o", "w_up", "conv_w", "w_down"):
        shared[nm] = f(inputs[nm])[0]
    in_maps = []
    for c in range(NC):
        xc = np.concatenate([xp[c * nb_p + i] for i in range(nb_p)] + [xs[c * nb_s + i] for i in range(nb_s)], axis=0)
        mc = np.concatenate([mp[c * nb_p + i] for i in range(nb_p)] + [ms[c * nb_s + i] for i in range(nb_s)], axis=0)
        m = dict(shared)
        m["x"] = np.ascontiguousarray(xc)
        m["mem"] = np.ascontiguousarray(mc)
        in_maps.append(m)
    res = run_bass_kernel_spmd(nc, in_maps, core_ids=list(range(NC)))
    yp = np.empty_like(xp)
    ys = np.empty_like(xs)
    for c in range(NC):
        y = res.results[c]["y"]
        o = 0
        for i in range(nb_p):
            yp[c * nb_p + i] = y[o:o + SP]
            o += SP
        for i in range(nb_s):
            ys[c * nb_s + i] = y[o:o + SS]
            o += SS
    return (yp, ys)
```

```python
import math
import numpy as np
import concourse.bass as bass
import concourse.mybir as mybir
from concourse.bass_types import AP
from concourse.bass_utils import run_bass_kernel_spmd
from contextlib import ExitStack

F32 = mybir.dt.float32
BF16 = mybir.dt.bfloat16
I32 = mybir.dt.int32
ALU = mybir.AluOpType
AF = mybir.ActivationFunctionType

ENGS = ("sync", "tensor", "vector", "scalar", "gpsimd")
PH = ["init"]
EPOCH = 6000
NDMASEM = 24


class Res:
    __slots__ = ("name", "last_w", "readers")

    def __init__(self, name):
        self.name = name
        self.last_w = None
        self.readers = []


class Instr:
    __slots__ = ("eng", "fn", "is_dma", "deps", "signal", "sig_no", "dma_slot", "dma_val", "tag")

    def __init__(self, eng, fn, is_dma):
        self.eng = eng
        self.fn = fn
        self.is_dma = is_dma
        self.deps = []
        self.signal = False
        self.sig_no = None
        self.dma_slot = None
        self.dma_val = None


class Sched:
    def __init__(self, nc):
        self.nc = nc
        self.streams = {e: [] for e in ENGS}
        self.ndma = {e: 0 for e in ENGS}

    def res(self, name):
        return Res(name)

    def op(self, eng, fn, reads=(), writes=(), is_dma=False):
        ins = Instr(eng, fn, is_dma)
        ins.tag = PH[0]
        deps = {}
        for r in reads:
            lw = r.last_w
            if lw is not None:
                deps[id(lw)] = (lw, True)
        for w in writes:
            lw = w.last_w
            if lw is not None:
                deps[id(lw)] = (lw, True)
            for rd in w.readers:
                if id(rd) not in deps:
                    deps[id(rd)] = (rd, False)
        for d, hard in deps.values():
            if (not d.is_dma) and (not is_dma) and d.eng == eng:
                if eng == "tensor" or not hard:
                    continue
            ins.deps.append(d)
            d.signal = True
        if is_dma:
            ins.signal = True
            k = self.ndma[eng]
            self.ndma[eng] = k + 1
            ins.dma_slot = k % NDMASEM
            ins.dma_val = 16 * (k // NDMASEM + 1)
        for r in reads:
            r.readers.append(ins)
        for w in writes:
            w.last_w = ins
            w.readers = []
        self.streams[eng].append(ins)
        return ins

    def emit(self):
        nc = self.nc
        with ExitStack() as es:
            csem = {}
            for e in ENGS:
                n = 0
                for ins in self.streams[e]:
                    if ins.signal and not ins.is_dma:
                        n += 1
                        ins.sig_no = n
                nep = max((n + EPOCH - 1) // EPOCH, 1)
                csem[e] = [es.enter_context(nc.semaphore(f"c_{e}_{k}")) for k in range(nep)]
            dsem = {}
            for e in ENGS:
                if self.ndma[e]:
                    dsem[e] = [es.enter_context(nc.semaphore(f"d_{e}_{k}"))
                               for k in range(min(NDMASEM, self.ndma[e]))]
            streams = self.streams

            def emit_stream(ename, eng):
                waited = {}
                maxep = {}
                dma_hist = {}
                for ins in streams[ename]:
                    need = []
                    for d in ins.deps:
                        if d.is_dma:
                            need.append((("d", d.eng, d.dma_slot), dsem[d.eng][d.dma_slot], d.dma_val))
                        else:
                            ep = (d.sig_no - 1) // EPOCH
                            if maxep.get(d.eng, -1) > ep:
                                continue
                            need.append((("c", d.eng, ep), csem[d.eng][ep], d.sig_no - ep * EPOCH))
                    if ins.is_dma:
                        prev = dma_hist.get(ins.dma_slot)
                        if prev is not None:
                            need.append((("d", ename, ins.dma_slot), dsem[ename][ins.dma_slot], prev.dma_val))
                        dma_hist[ins.dma_slot] = ins
                    best = {}
                    for key, sem, val in need:
                        if waited.get(key, 0) >= val:
                            continue
                        if key not in best or best[key][1] < val:
                            best[key] = (sem, val)
                    for key, (sem, val) in best.items():
                        eng.wait_ge(sem, val)
                        waited[key] = val
                        if key[0] == "c":
                            maxep[key[1]] = max(maxep.get(key[1], -1), key[2])
                    h = ins.fn(eng)
                    if ins.is_dma:
                        h.then_inc(dsem[ename][ins.dma_slot], 16)
                    elif ins.signal:
                        ep = (ins.sig_no - 1) // EPOCH
                        h.then_inc(csem[ename][ep], 1)
                for slot, prev in dma_hist.items():
                    if waited.get(("d", ename, slot), 0) < prev.dma_val:
                        eng.wait_ge(dsem[ename][slot], prev.dma_val)

            with nc.Block() as block:
                @block.sync
                def _(e):
                    emit_stream("sync", e)

                @block.tensor
                def _(e):
                    emit_stream("tensor", e)

                @block.vector
                def _(e):
                    emit_stream("vector", e)

                @block.scalar
                def _(e):
                    emit_stream("scalar", e)

                @block.gpsimd
                def _(e):
                    emit_stream("gpsimd", e)


class View:
    def __init__(self, base, K, N, res, name=""):
        self.base = base
        self.K = K
        self.N = N
        self.res = res
        self.p = list(base.ap[0])
        self.name = name

    def ap(self, dims, off=0):
        return AP(tensor=self.base.tensor, offset=self.base.offset + off,
                  ap=[self.p] + [list(d) for d in dims])

    def k(self, k, c0=0, c1=None):
        c1 = self.N if c1 is None else c1
        return self.ap([[1, c1 - c0]], k * self.N + c0)

    def ks(self, k0, k1, c0=0, c1=None):
        c1 = self.N if c1 is None else c1
        return self.ap([[self.N, k1 - k0], [1, c1 - c0]], k0 * self.N + c0)

    def all(self):
        return self.ap([[1, self.K * self.N]])


PAGE = 1024


class Arena:
    def __init__(self, S, tens, nbytes):
        self.t = tens
        self.np_ = nbytes // PAGE
        self.free = [True] * self.np_
        self.res = [S.res(f"pg{i}") for i in range(self.np_)]
        self.peak = 0

    def alloc(self, name, dtype, K, N):
        esz = 4 if dtype in (F32, I32) else 2
        npg = (K * N * esz + PAGE - 1) // PAGE
        i0 = -1
        ptr = getattr(self, "ptr", 0)
        if npg >= 4:
            run = 0
            for i in range(self.np_ - 1, -1, -1):
                run = run + 1 if self.free[i] else 0
                if run == npg:
                    i0 = i
                    break
            if i0 >= 0:
                for i in range(i0, i0 + npg):
                    self.free[i] = False
                self.peak = max(self.peak, self.np_ - sum(self.free))
                b = self.t[:, i0 * PAGE // 4:(i0 + npg) * PAGE // 4]
                if dtype != F32:
                    b = b.bitcast(dtype)
                v = View(b, K, N, self.res[i0:i0 + npg], name)
                v.pages = (i0, npg)
                return v
        for lo, hi in ((ptr, self.np_), (0, self.np_)):
            run = 0
            for i in range(lo, hi):
                run = run + 1 if self.free[i] else 0
                if run == npg:
                    i0 = i - npg + 1
                    break
            if i0 >= 0:
                break
        if i0 >= 0:
            self.ptr = i0 + npg
        if i0 < 0:
            raise RuntimeError(f"arena full allocating {name} ({npg} pages); free={sum(self.free)}")
        for i in range(i0, i0 + npg):
            self.free[i] = False
        self.peak = max(self.peak, self.np_ - sum(self.free))
        b = self.t[:, i0 * PAGE // 4:(i0 + npg) * PAGE // 4]
        if dtype != F32:
            b = b.bitcast(dtype)
        v = View(b, K, N, self.res[i0:i0 + npg], name)
        v.pages = (i0, npg)
        return v

    def release(self, *views):
        for v in views:
            i0, npg = v.pages
            for i in range(i0, i0 + npg):
                assert not self.free[i], v.name
                self.free[i] = True


D = 1024
DFF = 2816
NPAIR = 22
NMEM = 256
EPS = 1e-6
TWO_PI = 2.0 * math.pi


def build_program(seq_lens, T):
    G = T // 128
    NSEQ = len(seq_lens)
    NTOK = sum(seq_lens)
    SMAX = max(seq_lens)
    for s in seq_lens:
        assert s % T == 0 and s // T >= 2
    nc = bass.Bass("TRN2", target_bir_lowering=False)

    def din(name, shape):
        return nc.dram_tensor(name, shape, F32, kind="ExternalInput").ap()

    x_d = din("x", [NTOK, D])
    mem_d = din("mem", [NSEQ * NMEM, D])
    g_mix_pre_d = din("g_mix_pre", [D])
    g_mix_post_d = din("g_mix_post", [1, D])
    g_mem_d = din("g_mem", [D])
    w_in_d = din("w_in", [D, 9216])
    decay_fwd_d = din("decay_fwd", [1, 4])
    decay_bwd_d = din("decay_bwd", [1, 4])
    ret_gn_d = din("ret_gn", [D])
    w_ret_out_d = din("w_ret_out", [D, D])
    pool_w_d = din("pool_w", [4, 256, 256])
    pool_scale_d = din("pool_scale", [D])
    w_pool_out_d = din("w_pool_out", [D, D])
    w_mem_kv_d = din("w_mem_kv", [D, 2048])
    w_mem_out_d = din("w_mem_out", [D, D])
    w_o_d = din("w_o", [D, D])
    g_ffn_pre_d = din("g_ffn_pre", [D])
    g_ffn_post_d = din("g_ffn_post", [1, D])
    w_up_d = din("w_up", [D, 2 * DFF])
    conv_w_d = din("conv_w", [3, 2 * DFF])
    conv_b_d = din("conv_b", [2 * DFF])
    w_down_d = din("w_down", [DFF, D])
    y_d = nc.dram_tensor("y", [NTOK, D], F32, kind="ExternalOutput").ap()

    BLK = {}
    nb = 0
    for name, n in [("in", 18), ("ret_out", 2), ("pool_out", 2), ("mem_out", 2), ("o", 2),
                    ("poolw", 1), ("up", 11), ("down", 6), ("memkv", 4)]:
        BLK[name] = list(range(nb, nb + n))
        nb += n
    NBLK = nb
    wb_d = nc.dram_tensor("wb_scr", [NBLK, 128, 8 * 512], BF16, kind="Internal").ap()
    cs_d = nc.dram_tensor("cs_scr", [2, 128, SMAX], F32, kind="Internal").ap()
    NCHT = NTOK // 128
    sb_d = nc.dram_tensor("sb_scr", [NCHT, 128, 2048], BF16, kind="Internal").ap()
    NT2T = NTOK // T
    kts_d = nc.dram_tensor("kts_scr", [NT2T, 128, 8 * T], BF16, kind="Internal").ap()
    vts_d = nc.dram_tensor("vts_scr", [NT2T, 128, G * D], BF16, kind="Internal").ap()

    S = Sched(nc)
    es = ExitStack()

    def sbt(name, shape, dt):
        return es.enter_context(nc.sbuf_tensor(name, shape, dt))

    def fixed(name, dt, K, N):
        t = sbt(name, [128, K * N], dt)
        return View(t[:], K, N, [S.res(name)], name)

    def flat(lst):
        out = []
        for a in lst:
            if a is None:
                continue
            if hasattr(a, "res"):
                out.extend(a.res)
            elif isinstance(a, (list, tuple)):
                out.extend(flat(a))
            else:
                out.append(a)
        return out

    def OP(eng, meth, reads, writes, **kw):
        S.op(eng, lambda e: getattr(e, meth)(**kw), flat(reads), flat(writes))

    def DMA(eng, out, in_, reads, writes, **kw):
        S.op(eng, lambda e: e.dma_start(out=out, in_=in_, **kw), flat(reads), flat(writes), is_dma=True)

    R_W = 5
    wring = [fixed(f"wring{i}", BF16, 8, 512) for i in range(R_W)]
    ident = fixed("ident", BF16, 1, 128)
    identf = fixed("identf", F32, 1, 128)
    ones_bf = fixed("ones_bf", BF16, 1, 128)
    onesf = fixed("onesf", F32, 1, 128)
    dec = fixed("dec", F32, 1, 8)
    lg = fixed("lg", F32, 1, 8)
    tmp8 = fixed("tmp8", F32, 1, 8)
    iot_i = fixed("iot_i", I32, 1, 512)
    io_c1 = fixed("io_c1", F32, 1, 128)
    io_rc = fixed("io_rc", F32, 1, 128)
    io_mc = fixed("io_mc", F32, 1, 128)
    io_p = fixed("io_p", F32, 1, 1)
    a127 = fixed("a127", F32, 1, 1)
    p1 = fixed("p1", F32, 1, 1)
    AFt = fixed("AFt", F32, 4, 128)
    ABt = fixed("ABt", F32, 4, 128)
    DTt = fixed("DTt", F32, 4, 128)
    kdf = fixed("kdf", F32, 1, 4)
    kdb = fixed("kdb", F32, 1, 4)
    cdf = fixed("cdf", F32, 1, 4)
    cdb = fixed("cdb", F32, 1, 4)
    e1col = fixed("e1col", F32, 1, 4)
    nhalf = fixed("nhalf", F32, 1, 1)
    inv = fixed("inv", F32, 1, 1)
    gpre = fixed("gpre", F32, 1, 8)
    gfpre = fixed("gfpre", F32, 1, 8)
    gmem = fixed("gmem", F32, 1, 8)
    rgn = fixed("rgn", F32, 1, 8)
    psc = fixed("psc", F32, 1, 8)
    cw = fixed("cw", F32, 3, 44)
    cb = fixed("cb", F32, 1, 44)
    gpost = fixed("gpost", F32, 1, D)
    gfpost = fixed("gfpost", F32, 1, D)
    icf = fixed("icf", F32, 4, 8)
    icl = fixed("icl", F32, 4, 8)
    small = fixed("small", F32, 1, 64)
    dummy = {e: fixed(f"dummy_{e}", F32, 1, 4) for e in ("vector", "scalar", "gpsimd")}
    tmpA = fixed("tmpA", F32, 1, 128)
    csbuf = [fixed(f"csbuf{i}", F32, 2, T) for i in range(2)]
    tmpB = fixed("tmpB", F32, 1, 128)

    rem = nc.sbuf_bytes_remaining
    rem = rem() if callable(rem) else rem
    ARENA_BYTES = (rem - 2048) // PAGE * PAGE
    arena_t = sbt("arena", [128, ARENA_BYTES // 4], F32)
    AR = Arena(S, arena_t, ARENA_BYTES)

    pbanks = []
    for i in range(8):
        t = es.enter_context(nc.psum_tensor(f"bank{i}", [128, 512], F32))
        pbanks.append(t)
    half_res = [S.res(f"bankh{i}") for i in range(16)]
    half_free = [True] * 16
    half_stamp = [0] * 16
    stamp = [0]

    class Bank:
        def __init__(self, halves):
            self.halves = halves
            self.res = [half_res[h] for h in halves]
            t = pbanks[halves[0] // 2]
            if len(halves) == 2:
                base = t[:]
                self.f = View(base, 1, 512, self.res)
                self.h = View(base.bitcast(BF16), 1, 1024, self.res)
            else:
                o = 256 * (halves[0] % 2)
                base = t[:, o:o + 256]
                self.f = View(base, 1, 256, self.res)
                self.h = View(base.bitcast(BF16), 1, 512, self.res)

        def free(self):
            for h in self.halves:
                assert not half_free[h]
                half_free[h] = True
                stamp[0] += 1
                half_stamp[h] = stamp[0]

    def try_bank(half=False):
        best = None
        if half:
            for h in range(16):
                if half_free[h]:
                    key = (half_free[h ^ 1], half_stamp[h])
                    if best is None or key < best[0]:
                        best = (key, [h])
        else:
            for bnk in range(8):
                if half_free[2 * bnk] and half_free[2 * bnk + 1]:
                    key = max(half_stamp[2 * bnk], half_stamp[2 * bnk + 1])
                    if best is None or key < best[0]:
                        best = (key, [2 * bnk, 2 * bnk + 1])
        if best is None:
            return None
        for h in best[1]:
            half_free[h] = False
        return Bank(best[1])

    def BankNow(half=False):
        bk_ = try_bank(half)
        assert bk_ is not None, "no free PSUM bank"
        return bk_

    def gbanks(k):
        n = 0
        while True:
            nfree = sum(1 for b_ in range(8) if half_free[2 * b_] and half_free[2 * b_ + 1])
            if nfree >= k:
                return [try_bank(False) for _ in range(k)]
            n += 1
            assert n < 10000, "PSUM bank starvation"
            yield

    small_ctr = [0]
    small_res = [S.res(f"small{i}") for i in range(64)]

    class SmallV:
        def __init__(self, o, n):
            self.o = o
            self.n = n
            self.res = small_res[o:o + n]
            self.a = small.k(0, o, o + n)

        def sub(self, i):
            return small.k(0, self.o + i, self.o + i + 1)

    def sm(n=1):
        i = small_ctr[0]
        if i % 64 + n > 64:
            i = (i // 64 + 1) * 64
        small_ctr[0] = i + n
        return SmallV(i % 64, n)

    CONST = []

    def cvec(view, src, pattern, **kw):
        DMA("sync", view.all() if pattern is None else pattern, src, [], [view],
            allow_slow_non_contiguous=True, **kw)
        CONST.append(view)

    WB = [S.res(f"wb{i}") for i in range(NBLK)]

    def wbv(b):
        return wb_d[b].rearrange("p (k n) -> p k n", k=8)

    def wsrc(w, c0, c1):
        return w.rearrange("(k p) n -> p k n", p=128)[:, :, c0:c1]

    for i, b in enumerate(BLK["in"]):
        DMA("gpsimd", wbv(b), wsrc(w_in_d, 512 * i, 512 * i + 512), [], [WB[b]])
    for nm, w in (("ret_out", w_ret_out_d), ("pool_out", w_pool_out_d), ("mem_out", w_mem_out_d), ("o", w_o_d)):
        for i, b in enumerate(BLK[nm]):
            DMA("gpsimd", wbv(b), wsrc(w, 512 * i, 512 * i + 512), [], [WB[b]])
    b = BLK["poolw"][0]
    for g in range(4):
        DMA("gpsimd", wbv(b)[:, 2 * g:2 * g + 2, 0:256],
            pool_w_d[g].rearrange("(k p) n -> p k n", p=128), [], [WB[b]])
    for i, b in enumerate(BLK["up"]):
        DMA("gpsimd", wbv(b)[:, :, 0:256], wsrc(w_up_d, 256 * i, 256 * i + 256), [], [WB[b]])
        DMA("gpsimd", wbv(b)[:, :, 256:512], wsrc(w_up_d, DFF + 256 * i, DFF + 256 * i + 256), [], [WB[b]])
    for half in range(2):
        for kg in range(3):
            b = BLK["down"][half * 3 + kg]
            nk = min(8, NPAIR - 8 * kg)
            DMA("gpsimd", wbv(b)[:, 0:nk, :],
                w_down_d.rearrange("(k p) n -> p k n", p=128)[:, 8 * kg:8 * kg + nk, 512 * half:512 * half + 512],
                [], [WB[b]])
    for i, b in enumerate(BLK["memkv"]):
        DMA("gpsimd", wbv(b), wsrc(w_mem_kv_d, 512 * i, 512 * i + 512), [], [WB[b]])

    for v, src in ((gpre, g_mix_pre_d), (gfpre, g_ffn_pre_d), (gmem, g_mem_d), (rgn, ret_gn_d), (psc, pool_scale_d)):
        cvec(v, src.rearrange("(k p) -> p k", p=128), None)
    cvec(cw, conv_w_d.rearrange("t (c p) -> p t c", p=128), cw.ks(0, 3))
    cvec(cb, conv_b_d.rearrange("(c p) -> p c", p=128), None)

    def pbc(src, n):
        return AP(tensor=src.tensor, offset=src.offset, ap=[[0, 128], [1, n]])

    cvec(gpost, pbc(g_mix_post_d, D), None)
    cvec(gfpost, pbc(g_ffn_post_d, D), None)
    DMA("sync", dec.k(0, 0, 4), pbc(decay_fwd_d, 4), [], [dec])
    DMA("sync", dec.k(0, 4, 8), pbc(decay_bwd_d, 4), [], [dec])

    OP("gpsimd", "memset", [], [identf], ap=identf.all(), constant=0.0)
    OP("gpsimd", "memset", [], [onesf], ap=onesf.all(), constant=1.0)
    OP("gpsimd", "memset", [], [nhalf], ap=nhalf.all(), constant=-0.5)
    OP("gpsimd", "affine_select", [identf], [identf], out=identf.all(), in_=identf.all(), pattern=[[-1, 128]],
       compare_op=ALU.not_equal, fill=1.0, base=0, channel_multiplier=1)
    OP("vector", "tensor_copy", [identf], [ident], out=ident.all(), in_=identf.all())
    OP("vector", "tensor_copy", [onesf], [ones_bf], out=ones_bf.all(), in_=onesf.all())

    def iota_f(dst, pattern, base, cm, n):
        OP("gpsimd", "iota", [dst, iot_i], [iot_i], out=iot_i.k(0, 0, n), pattern=pattern, base=base, channel_multiplier=cm)
        OP("vector", "tensor_copy", [iot_i], [dst], out=dst.all(), in_=iot_i.k(0, 0, n))

    iota_f(io_c1, [[1, 128]], 1, 0, 128)
    iota_f(io_mc, [[-1, 128]], 0, 1, 128)
    iota_f(io_p, [[0, 1]], 0, 1, 1)
    OP("vector", "tensor_scalar", [io_c1], [io_rc], out=io_rc.all(), in0=io_c1.all(), scalar1=-1.0, scalar2=129.0,
       op0=ALU.mult, op1=ALU.add)
    OP("vector", "tensor_scalar", [io_p], [a127], out=a127.all(), in0=io_p.all(), scalar1=-1.0, scalar2=127.0,
       op0=ALU.mult, op1=ALU.add)
    OP("vector", "tensor_scalar", [io_p], [p1], out=p1.all(), in0=io_p.all(), scalar1=1.0, scalar2=None, op0=ALU.add)
    OP("scalar", "activation", [dec], [tmp8], out=tmp8.all(), in_=dec.all(), func=AF.Exp, scale=-1.0)
    OP("vector", "tensor_scalar", [tmp8], [tmp8], out=tmp8.all(), in0=tmp8.all(), scalar1=1.0, scalar2=None, op0=ALU.add)
    OP("scalar", "activation", [tmp8], [lg], out=lg.all(), in_=tmp8.all(), func=AF.Ln)
    OP("vector", "tensor_scalar", [lg], [lg], out=lg.all(), in0=lg.all(), scalar1=-1.0, scalar2=None, op0=ALU.mult)
    for h in range(4):
        lf = lg.k(0, h, h + 1)
        lb = lg.k(0, 4 + h, 5 + h)
        OP("scalar", "activation", [io_c1, lg], [AFt], out=AFt.k(h), in_=io_c1.all(), func=AF.Exp, scale=lf)
        OP("scalar", "activation", [io_rc, lg], [ABt], out=ABt.k(h), in_=io_rc.all(), func=AF.Exp, scale=lb)
        OP("scalar", "activation", [a127, lg], [kdf], out=kdf.k(0, h, h + 1), in_=a127.all(), func=AF.Exp, scale=lf)
        OP("scalar", "activation", [io_p, lg], [kdb], out=kdb.k(0, h, h + 1), in_=io_p.all(), func=AF.Exp, scale=lb)
        OP("scalar", "activation", [lg], [cdf], out=cdf.k(0, h, h + 1), in_=lf, func=AF.Exp, scale=128.0)
        OP("scalar", "activation", [lg], [cdb], out=cdb.k(0, h, h + 1), in_=lb, func=AF.Exp, scale=128.0)
        OP("vector", "tensor_scalar", [lg], [tmp8], out=tmp8.k(0, 0, 1), in0=lf, scalar1=-1.0, scalar2=None, op0=ALU.mult)
        OP("scalar", "activation", [p1, tmp8], [e1col], out=e1col.k(0, h, h + 1), in_=p1.all(), func=AF.Exp,
           scale=tmp8.k(0, 0, 1))
        OP("vector", "tensor_scalar", [io_mc, lg], [tmpA], out=tmpA.all(), in0=io_mc.all(), scalar1=lb, scalar2=None, op0=ALU.mult)
        OP("vector", "tensor_scalar", [io_c1, lg], [tmpB], out=tmpB.all(), in0=io_c1.all(), scalar1=lf, scalar2=None, op0=ALU.mult)
        OP("vector", "tensor_tensor", [tmpA, tmpB], [tmpA], out=tmpA.all(), in0=tmpA.all(), in1=tmpB.all(), op=ALU.subtract)
        OP("scalar", "activation", [tmpA], [tmpA], out=tmpA.all(), in_=tmpA.all(), func=AF.Exp)
        OP("gpsimd", "affine_select", [tmpA], [tmpA], out=tmpA.all(), in_=tmpA.all(), pattern=[[-1, 128]],
           compare_op=ALU.is_gt, fill=0.0, base=0, channel_multiplier=1)
        OP("vector", "tensor_scalar", [onesf, e1col], [tmpB], out=tmpB.all(), in0=onesf.all(), scalar1=e1col.k(0, h, h + 1),
           scalar2=None, op0=ALU.mult)
        OP("gpsimd", "affine_select", [tmpB], [tmpB], out=tmpB.all(), in_=tmpB.all(), pattern=[[1, 128]],
           compare_op=ALU.is_ge, fill=0.0, base=0, channel_multiplier=-1)
        OP("vector", "tensor_tensor", [tmpA, tmpB], [tmpA], out=tmpA.all(), in0=tmpA.all(), in1=tmpB.all(), op=ALU.add)
        OP("vector", "tensor_scalar", [tmpA], [DTt], out=DTt.k(h), in0=tmpA.all(), scalar1=1.0 / 16, scalar2=None, op0=ALU.mult)
    OP("vector", "tensor_scalar", [kdf], [kdf], out=kdf.all(), in0=kdf.all(), scalar1=1.0 / 16, scalar2=None, op0=ALU.mult)
    OP("vector", "tensor_scalar", [kdb], [kdb], out=kdb.all(), in0=kdb.all(), scalar1=1.0 / 16, scalar2=None, op0=ALU.mult)
    OP("gpsimd", "iota", [iot_i], [iot_i], out=iot_i.k(0, 0, 8), pattern=[[1, 8]], base=0, channel_multiplier=0)
    OP("vector", "tensor_copy", [iot_i], [tmpB], out=tmpB.k(0, 0, 8), in_=iot_i.k(0, 0, 8))
    for gi, w in enumerate((2, 4, 8, 16)):
        OP("vector", "tensor_scalar", [tmpB], [icf], out=icf.k(gi), in0=tmpB.k(0, 0, 8), scalar1=float(w // 2), scalar2=float(w),
           op0=ALU.add, op1=ALU.min)
        OP("vector", "tensor_scalar", [tmpB], [icl], out=icl.k(gi), in0=tmpB.k(0, 0, 8), scalar1=-1.0, scalar2=float(8 + w // 2),
           op0=ALU.mult, op1=ALU.add)
        OP("vector", "tensor_scalar", [icl], [icl], out=icl.k(gi), in0=icl.k(gi), scalar1=float(w), scalar2=None, op0=ALU.min)
    OP("vector", "reciprocal", [icf], [icf], out=icf.all(), in_=icf.all())
    OP("vector", "reciprocal", [icl], [icl], out=icl.all(), in_=icl.all())
    OP("scalar", "activation", [io_p], [inv], out=inv.all(), in_=io_p.all(), func=AF.Exp, scale=-math.log(10000.0) / 128.0)
    CS = S.res("cs_scr")
    rt = [AR.alloc(f"rt{i}", F32, 1, 512) for i in range(4)]
    rti = AR.alloc("rti", I32, 1, 512)
    for blk in range(SMAX // 512 if SMAX % 512 == 0 else SMAX // 512 + 1):
        n = min(512, SMAX - blk * 512)
        OP("gpsimd", "iota", [iot_i], [iot_i], out=iot_i.k(0, 0, n), pattern=[[1, n]], base=blk * 512, channel_multiplier=0)
        OP("vector", "tensor_copy", [iot_i], [rt[0]], out=rt[0].k(0, 0, n), in_=iot_i.k(0, 0, n))
        OP("vector", "tensor_scalar", [rt[0], inv], [rt[0]], out=rt[0].k(0, 0, n), in0=rt[0].k(0, 0, n), scalar1=inv.all(),
           scalar2=None, op0=ALU.mult)
        for which in range(2):
            a = rt[1 + which]
            OP("vector", "tensor_scalar", [rt[0]], [a], out=a.k(0, 0, n), in0=rt[0].k(0, 0, n),
               scalar1=(math.pi / 2 if which == 0 else 0.0), scalar2=None, op0=ALU.add)
            OP("vector", "tensor_scalar", [a], [rt[3]], out=rt[3].k(0, 0, n), in0=a.k(0, 0, n), scalar1=1.0 / TWO_PI,
               scalar2=None, op0=ALU.mult)
            OP("vector", "tensor_copy", [rt[3]], [rti], out=rti.k(0, 0, n), in_=rt[3].k(0, 0, n))
            OP("vector", "tensor_copy", [rti], [rt[3]], out=rt[3].k(0, 0, n), in_=rti.k(0, 0, n))
            OP("vector", "scalar_tensor_tensor", [rt[3], a], [a], out=a.k(0, 0, n), in0=rt[3].k(0, 0, n), scalar=-TWO_PI,
               in1=a.k(0, 0, n), op0=ALU.mult, op1=ALU.add)
            OP("vector", "tensor_scalar", [a], [a], out=a.k(0, 0, n), in0=a.k(0, 0, n), scalar1=-3.1415925, scalar2=3.1415925,
               op0=ALU.max, op1=ALU.min)
            OP("scalar", "activation", [a], [a], out=a.k(0, 0, n), in_=a.k(0, 0, n), func=AF.Sin)
            DMA("sync", cs_d[which][:, blk * 512:blk * 512 + n], a.k(0, 0, n), [a], [CS])
    AR.release(*rt, rti)

    CONST += [ident, ones_bf, onesf, AFt, ABt, DTt, kdf, kdb, cdf, cdb, nhalf, icf, icl]
    for e in ("vector", "scalar", "gpsimd"):
        if e == "scalar":
            OP(e, "activation", CONST, [dummy[e]], out=dummy[e].k(0, 0, 1), in_=nhalf.all(), func=AF.Copy)
        else:
            OP(e, "memset", CONST, [dummy[e]], ap=dummy[e].all(), constant=0.0)
    fb = BankNow()
    OP("tensor", "transpose", CONST, [fb], out=fb.h.k(0, 0, 128), in_=ident.all(), identity=ident.all())
    fb.free()

    wq = []
    inflight = []
    wslot = [0]

    def pump():
        while len(inflight) < R_W and wq:
            b = wq.pop(0)
            v = wring[wslot[0] % R_W]
            wslot[0] += 1
            DMA("sync", v.all(), wb_d[b], [WB[b]], [v])
            inflight.append((b, v))

    def announce(blocks):
        wq.extend(blocks)
        pump()

    def wuse(b):
        bb, v = inflight[0]
        assert bb == b, (bb, b)
        return v

    def wdone():
        inflight.pop(0)
        pump()

    def rstd_from(ss_list, n):
        r = sm()
        if len(ss_list) == 2:
            OP("vector", "tensor_tensor", ss_list, [r], out=r.a, in0=ss_list[0].a, in1=ss_list[1].a, op=ALU.add)
            src = r
        else:
            src = ss_list[0]
        OP("vector", "tensor_scalar", [src], [r], out=r.a, in0=src.a, scalar1=1.0 / n, scalar2=EPS, op0=ALU.mult, op1=ALU.add)
        r2 = sm()
        OP("gpsimd", "tensor_tensor", [r], [r2], out=r2.a, in0=r.a, in1=nhalf.all(), op=ALU.pow)
        return r2

    def norm_transpose(xin_ap, xin_res, gvec, dstT, col0):
        PH[0] = f"{PH[0].split('/')[0]}/norm"
        junk = AR.alloc("junk", BF16, 1, D)
        xn = AR.alloc("xn", BF16, 1, D)
        ss = sm()
        OP("scalar", "activation", [xin_res], [junk, ss], out=junk.all(), in_=xin_ap, func=AF.Square, accum_out=ss.a)
        r = rstd_from([ss], D)
        OP("scalar", "activation", [xin_res, r], [xn], out=xn.all(), in_=xin_ap, func=AF.Copy, scale=r.a)
        bk = BankNow()
        for k in range(8):
            OP("tensor", "transpose", [xn], [bk], out=bk.h.k(0, 128 * k, 128 * k + 128), in_=xn.k(0, 128 * k, 128 * k + 128),
               identity=ident.all())
        gb = AP(tensor=gvec.base.tensor, offset=gvec.base.offset, ap=[gvec.p, [1, 8], [0, 128]])
        OP("vector", "tensor_tensor", [bk], [dstT], out=dstT.ks(0, 8, col0, col0 + 128),
           in0=bk.h.ap([[128, 8], [1, 128]]), in1=gb, op=ALU.mult)
        bk.free()
        AR.release(junk, xn)

    def projA(blkname, idx, rhsT, ncols, evac, nk=8):
        PH[0] = f"{PH[0].split('/')[0]}/A_{blkname}{idx}"
        b = BLK[blkname][idx]
        w = wuse(b)
        for oc in range(4):
            bk = BankNow()
            for k in range(nk):
                OP("tensor", "matmul", [w, rhsT], [bk], out=bk.f.k(0, 0, ncols), lhsT=w.k(k, 128 * oc, 128 * oc + 128),
                   rhs=rhsT.k(k, 0, ncols), start=(k == 0), stop=(k == nk - 1))
            if not evac(oc, bk):
                bk.free()
        wdone()

    def projB(blkname, idx, lhsT_view, ngroups, evac):
        PH[0] = f"{PH[0].split('/')[0]}/B_{blkname}{idx}"
        b = BLK[blkname][idx]
        w = wuse(b)
        for g in range(ngroups):
            bk = BankNow()
            for k in range(8):
                OP("tensor", "matmul", [w, lhsT_view], [bk], out=bk.f.all(), lhsT=lhsT_view.k(k, 128 * g, 128 * g + 128),
                   rhs=w.k(k), start=(k == 0), stop=(k == 7))
            evac(g, bk)
            bk.free()
        wdone()

    def rotary_pair(b1, b2, cs, out1, out2, post=None):
        t = [AR.alloc(f"rot{i}", F32, 1, T) for i in range(4)]
        OP("vector", "tensor_tensor", [b1, cs], [t[0]], out=t[0].all(), in0=b1.f.k(0, 0, T), in1=cs.k(0), op=ALU.mult)
        OP("vector", "tensor_tensor", [b2, cs], [t[1]], out=t[1].all(), in0=b2.f.k(0, 0, T), in1=cs.k(1), op=ALU.mult)
        OP("vector", "tensor_tensor", [b1, cs], [t[2]], out=t[2].all(), in0=b1.f.k(0, 0, T), in1=cs.k(1), op=ALU.mult)
        OP("vector", "tensor_tensor", [b2, cs], [t[3]], out=t[3].all(), in0=b2.f.k(0, 0, T), in1=cs.k(0), op=ALU.mult)
        out1(t[0], t[1], ALU.subtract)
        out2(t[2], t[3], ALU.add)
        AR.release(*t)

    def _tick(act):
        for g_ in list(act):
            try:
                next(g_)
            except StopIteration:
                act.remove(g_)

    def skewed(makers):
        act = []
        for mk in makers:
            act.append(mk())
            _tick(act)
            yield
        while act:
            _tick(act)
            yield

    def run_threads(threads):
        act = list(threads)
        while act:
            _tick(act)

    tok0 = 0
    ch0 = 0
    for si, SL in enumerate(seq_lens):
        NT = SL // T
        NCH = SL // 128
        SBR = [S.res(f"sbs{si}_{c}") for c in range(NCH)]

        def xrows(t0, n=128):
            return x_d[tok0 + t0:tok0 + t0 + n, :]

        PH[0] = "memkv"
        kmT = AR.alloc("kmT", BF16, 8, NMEM)
        vm = AR.alloc("vm", BF16, 2, D)
        mhT = AR.alloc("mhT", BF16, 8, NMEM)
        announce(BLK["memkv"])
        for g in range(2):
            xm = AR.alloc("xm", F32, 1, D)
            DMA("sync", xm.all(), mem_d[si * NMEM + 128 * g:si * NMEM + 128 * g + 128, :], [], [xm])
            norm_transpose(xm.all(), xm, gmem, mhT, 128 * g)
            AR.release(xm)
        for i in range(2):
            def ev(oc, bk, i=i):
                OP("scalar", "activation", [bk], [kmT], out=kmT.k(4 * i + oc), in_=bk.f.k(0, 0, NMEM), func=AF.Copy)
            projA("memkv", i, mhT, NMEM, ev)
        for i in range(2):
            def ev(g, bk, i=i):
                OP("scalar", "activation", [bk], [vm], out=vm.k(g, 512 * i, 512 * i + 512), in_=bk.f.all(), func=AF.Copy)
            projB("memkv", 2 + i, mhT, 2, ev)
        AR.release(mhT)


        T1 = max(t_ for t_ in (512, 256) if SL % t_ == 0 and t_ >= T)
        G1 = T1 // 128
        RR = T1 // T
        NT1 = SL // T1
        S32 = AR.alloc("S32b", F32, 4, 512)
        S32r = [S.res(f"S32b_{h}") for h in range(4)]
        KTS = [S.res(f"kts{si}_{t}") for t in range(NT)]
        VTS = [S.res(f"vts{si}_{t}") for t in range(NT)]
        tile0 = tok0 // T
        Sbf = [AR.alloc(f"Sbfb{i}", BF16, 4, 512) for i in range(2)]
        OP("gpsimd", "memset", [], [S32] + S32r, ap=S32.all(), constant=0.0)
        OP("gpsimd", "memset", [], [Sbf[0]], ap=Sbf[0].all(), constant=0.0)
        p1_blocks = [BLK["in"][i] for i in (2, 3, 4, 5)]
        tiles1 = list(reversed(range(NT1)))
        xorder = [(t1, g) for t1 in tiles1 for g in range(G1)]
        xfifo = []
        xnext = [0]

        def xpump(depth=6):
            while len(xfifo) < depth and xnext[0] < len(xorder):
                t1, g = xorder[xnext[0]]
                xnext[0] += 1
                xp = AR.alloc("xp", F32, 1, D)
                DMA("sync", xp.all(), xrows(t1 * T1 + 128 * g), [], [xp])
                xfifo.append(xp)

        def front(t1):
            PH[0] = "p1"
            announce(p1_blocks)
            hT1 = AR.alloc("hT1", BF16, 8, T1)
            for g in range(G1):
                xpump()
                xp = xfifo.pop(0)
                norm_transpose(xp.all(), xp, gpre, hT1, 128 * g)
                AR.release(xp)
                xpump()
            cs1 = AR.alloc("cs1", F32, 2, T1)
            for which in range(2):
                DMA("sync", cs1.k(which), cs_d[which][:, t1 * T1:t1 * T1 + T1], [CS], [cs1])
            kT1 = [AR.alloc(f"kT1_{r}", BF16, 8, T) for r in range(RR)]
            PH[0] = "p1/k"
            for i in range(2):
                w = wuse(BLK["in"][2 + i])
                for pr_ in range(2):
                    bks = []
                    for oc in (2 * pr_, 2 * pr_ + 1):
                        bk = BankNow()
                        for k in range(8):
                            OP("tensor", "matmul", [w, hT1], [bk], out=bk.f.k(0, 0, T1), lhsT=w.k(k, 128 * oc, 128 * oc + 128),
                               rhs=hT1.k(k), start=(k == 0), stop=(k == 7))
                        bks.append(bk)
                    c = 4 * i + 2 * pr_
                    t = [AR.alloc(f"rot{q}", F32, 1, T1) for q in range(4)]
                    OP("vector", "tensor_tensor", [bks[0], cs1], [t[0]], out=t[0].all(), in0=bks[0].f.k(0, 0, T1), in1=cs1.k(0), op=ALU.mult)
                    OP("vector", "tensor_tensor", [bks[1], cs1], [t[1]], out=t[1].all(), in0=bks[1].f.k(0, 0, T1), in1=cs1.k(1), op=ALU.mult)
                    OP("vector", "tensor_tensor", [bks[0], cs1], [t[2]], out=t[2].all(), in0=bks[0].f.k(0, 0, T1), in1=cs1.k(1), op=ALU.mult)
                    OP("vector", "tensor_tensor", [bks[1], cs1], [t[3]], out=t[3].all(), in0=bks[1].f.k(0, 0, T1), in1=cs1.k(0), op=ALU.mult)
                    bks[0].free()
                    bks[1].free()
                    for r in range(RR):
                        OP("gpsimd", "tensor_tensor", [t[0], t[1]], [kT1[r]], out=kT1[r].k(c), in0=t[0].k(0, r * T, r * T + T),
                           in1=t[1].k(0, r * T, r * T + T), op=ALU.subtract)
                        OP("gpsimd", "tensor_tensor", [t[2], t[3]], [kT1[r]], out=kT1[r].k(c + 1), in0=t[2].k(0, r * T, r * T + T),
                           in1=t[3].k(0, r * T, r * T + T), op=ALU.add)
                    AR.release(*t)
                wdone()
            AR.release(cs1)
            for r in range(RR):
                DMA("sync", kts_d[tile0 + t1 * RR + r], kT1[r].all(), [kT1[r]], [KTS[t1 * RR + r]])
            PH[0] = "p1/v"
            vt1 = AR.alloc("vt1", BF16, G1, D)
            for i in range(2):
                w = wuse(BLK["in"][4 + i])
                for g in range(G1):
                    bk = BankNow()
                    for k in range(8):
                        OP("tensor", "matmul", [w, hT1], [bk], out=bk.f.all(), lhsT=hT1.k(k, 128 * g, 128 * g + 128),
                           rhs=w.k(k), start=(k == 0), stop=(k == 7))
                    OP("scalar", "activation", [bk], [vt1], out=vt1.k(g, 512 * i, 512 * i + 512), in_=bk.f.all(), func=AF.Copy)
                    bk.free()
                wdone()
            AR.release(hT1)
            for r in range(RR):
                DMA("sync", vts_d[tile0 + t1 * RR + r], vt1.ap([[1, G * D]], r * G * D), [vt1], [VTS[t1 * RR + r]])
            PH[0] = "p1/kb"
            kb1 = AR.alloc("kb1", BF16, G1, D)
            for g in range(G1):
                bk = BankNow()
                for c in range(8):
                    OP("tensor", "transpose", [kT1[g // G]], [bk], out=bk.h.k(0, 128 * c, 128 * c + 128), in_=kT1[g // G].k(c, 128 * (g % G), 128 * (g % G) + 128),
                       identity=ident.all())
                for h in range(4):
                    OP("vector", "tensor_scalar", [bk], [kb1], out=kb1.k(g, 256 * h, 256 * h + 256), in0=bk.h.k(0, 256 * h, 256 * h + 256),
                       scalar1=kdb.k(0, h, h + 1), scalar2=None, op0=ALU.mult)
                bk.free()
            AR.release(*kT1)
            return (t1, kb1, vt1)

        def back(fr):
            t1, kb1, vt1 = fr
            PH[0] = "p1/state"
            for g in reversed(range(G1)):
                gc = t1 * G1 + g
                par_ = (NCH - 1 - gc) % 2
                DMA("sync", sb_d[ch0 + gc], Sbf[par_].all(), [Sbf[par_]], [SBR[gc]])
                for h in range(4):
                    bk = BankNow()
                    for dd in range(2):
                        OP("tensor", "matmul", [kb1, vt1], [bk], out=bk.f.k(0, 256 * dd, 256 * dd + 256),
                           lhsT=kb1.k(g, 256 * h + 128 * dd, 256 * h + 128 * dd + 128), rhs=vt1.k(g, 256 * h, 256 * h + 256),
                           start=True, stop=True)
                    OP("vector", "scalar_tensor_tensor", [S32r[h], bk], [S32r[h]], out=S32.k(h), in0=S32.k(h), scalar=cdb.k(0, h, h + 1),
                       in1=bk.f.all(), op0=ALU.mult, op1=ALU.add)
                    bk.free()
                    OP("scalar", "activation", [S32r[h]], [Sbf[1 - par_]], out=Sbf[1 - par_].k(h), in_=S32.k(h), func=AF.Copy)
            AR.release(kb1, vt1)

        prev = None
        for t1 in tiles1:
            cur = front(t1)
            if prev is not None:
                back(prev)
            prev = cur
        back(prev)
        assert not xfifo
        AR.release(S32, *Sbf)

        NX = 2 * G + 1
        xring = [AR.alloc(f"xr{i}", F32, 1, D) for i in range(NX)]
        pring = AR.alloc("pring", F32, 8, T + 16)
        wa_p = AR.alloc("wa", F32, 2, T + 16)
        Gr = AR.alloc("Gring", BF16, NPAIR, T + 128)
        carry = AR.alloc("h2carry", BF16, 8, 2)
        S32 = AR.alloc("S32f", F32, 4, 512)
        S32res = [S.res(f"S32f_{h}") for h in range(4)]
        _sbf1 = AR.alloc("Sbff", BF16, 4, 512)
        Sbf = [_sbf1, _sbf1]
        SbfR = [S.res(f"SbfR{h}") for h in range(4)]
        UW = T + 3
        for v in (pring, carry):
            OP("gpsimd", "memset", [], [v], ap=v.all(), constant=0.0)
        OP("gpsimd", "memset", [], [_sbf1] + SbfR, ap=_sbf1.all(), constant=0.0)
        OP("gpsimd", "memset", [], [S32] + S32res, ap=S32.all(), constant=0.0)
        par = 0

        def load_x(ti):
            for g in range(G):
                v = xring[(ti * G + g) % NX]
                DMA("sync", v.all(), xrows(ti * T + 128 * g), [], [v])

        def step_blocks(is_last):
            return ([BLK["in"][i] for i in (10, 11, 6, 7, 8, 9)]
                    + BLK["poolw"]
                    + [BLK["in"][i] for i in (12, 13, 14, 15, 16, 17)]
                    + [BLK["pool_out"][0], BLK["mem_out"][0], BLK["ret_out"][0],
                       BLK["pool_out"][1], BLK["mem_out"][1], BLK["ret_out"][1]]
                    + BLK["o"] + BLK["up"]
                    + ([] if is_last else [BLK["in"][i] for i in (0, 1)]) + BLK["down"])

        def ffn_down(groups, x1slots, ytoks):
            ng = len(groups)
            y2 = [AR.alloc(f"y2_{i}", F32, 1, D) for i in range(ng)]
            for half in range(2):
                bks = [BankNow() for _ in range(ng)]
                for kg in range(3):
                    b = BLK["down"][half * 3 + kg]
                    w = wuse(b)
                    nk = min(8, NPAIR - 8 * kg)
                    for gi, gcol in enumerate(groups):
                        for k in range(nk):
                            kk = 8 * kg + k
                            OP("tensor", "matmul", [w, Gr], [bks[gi]], out=bks[gi].f.all(),
                               lhsT=Gr.k(kk, 128 * gcol, 128 * gcol + 128), rhs=w.k(k), start=(kk == 0), stop=(kk == NPAIR - 1))
                    wdone()
                for gi in range(ng):
                    OP("scalar", "activation", [bks[gi]], [y2[gi]], out=y2[gi].k(0, 512 * half, 512 * half + 512),
                       in_=bks[gi].f.all(), func=AF.Copy)
                    bks[gi].free()
            for gi in range(ng):
                junk = AR.alloc("junk2", BF16, 1, D)
                ss = sm()
                OP("scalar", "activation", [y2[gi]], [junk, ss], out=junk.all(), in_=y2[gi].all(), func=AF.Square, accum_out=ss.a)
                r = rstd_from([ss], D)
                OP("vector", "scalar_tensor_tensor", [y2[gi], r], [y2[gi]], out=y2[gi].all(), in0=y2[gi].all(), scalar=r.a,
                   in1=gfpost.all(), op0=ALU.mult, op1=ALU.mult)
                xs = x1slots[gi]
                OP("gpsimd", "tensor_tensor", [y2[gi], xs], [xs], out=xs.all(), in0=y2[gi].all(), in1=xs.all(), op=ALU.add)
                DMA("gpsimd", y_d[tok0 + ytoks[gi]:tok0 + ytoks[gi] + 128, :], xs.all(), [xs], [])
                AR.release(junk)
            AR.release(*y2)

        def load_cs(ti):
            cs_ = csbuf[ti % 2]
            for which in range(2):
                DMA("sync", cs_.k(which), cs_d[which][:, ti * T:ti * T + T], [CS], [cs_])

        def head(ti, hT):
            PH[0] = "p2h"
            cs = csbuf[ti % 2]
            kT = AR.alloc("kT", BF16, 8, T)
            DMA("sync", kT.all(), kts_d[tile0 + ti], [KTS[ti]], [kT])
            vt = AR.alloc("vt", BF16, G, D)
            DMA("sync", vt.all(), vts_d[tile0 + ti], [VTS[ti]], [vt])
            qf = AR.alloc("qf", BF16, 8, T)
            qb = AR.alloc("qb", BF16, 8, T)
            for i in range(2):
                held = {}

                def ev(oc, bk, i=i, held=held):
                    if oc % 2 == 0:
                        held[0] = bk
                        return True
                    c = 4 * i + oc - 1
                    h = c // 2
                    o = [AR.alloc("qo0", F32, 1, T), AR.alloc("qo1", F32, 1, T)]

                    def mk(j):
                        def f(a, b_, op, j=j):
                            OP("gpsimd", "tensor_tensor", [a, b_], [o[j]], out=o[j].all(), in0=a.all(), in1=b_.all(), op=op)
                            src = o[j].ap([[128, G], [1, 128]])
                            for tab, dst in ((AFt, qf), (ABt, qb)):
                                tb = AP(tensor=tab.base.tensor, offset=tab.base.offset + 128 * h, ap=[tab.p, [0, G], [1, 128]])
                                OP("vector", "tensor_tensor", [o[j]], [dst], out=dst.ap([[128, G], [1, 128]], (c + j) * T),
                                   in0=src, in1=tb, op=ALU.mult)
                        return f
                    rotary_pair(held[0], bk, cs, mk(0), mk(1))
                    held[0].free()
                    AR.release(*o)
                projA("in", 0 + i, hT, T, ev)
            return dict(kT=kT, vt=vt, qf=qf, qb=qb, hT=hT)

        load_x(0)
        announce([BLK["in"][i] for i in (0, 1)])
        announce(step_blocks(NT == 1))
        hT_next = [None]
        hT0 = AR.alloc("hT", BF16, 8, T)
        for g in range(G):
            norm_transpose(xring[g % NX].all(), xring[g % NX], gpre, hT0, 128 * g)
        load_cs(0)
        HD = head(0, hT0)
        for ti in range(NT):
            first = ti == 0
            last = ti == NT - 1
            PH[0] = "p2"
            if not last:
                load_x(ti + 1)
                announce(step_blocks(ti + 1 == NT - 1))
            xs = [xring[(ti * G + g) % NX] for g in range(G)]
            Sb = [AR.alloc(f"Sb{g}", BF16, 4, 512) for g in range(G)]
            for g in range(G):
                DMA("sync", Sb[g].all(), sb_d[ch0 + ti * G + g], [SBR[ti * G + g]], [Sb[g]])
            kT, vt, qf, qb, hT = HD["kT"], HD["vt"], HD["qf"], HD["qb"], HD["hT"]
            mq = AR.alloc("mq", BF16, 8, T)
            for i in range(2):
                def ev(oc, bk, i=i):
                    OP("scalar", "activation", [bk], [mq], out=mq.k(4 * i + oc), in_=bk.f.k(0, 0, T), func=AF.Copy)
                projA("in", 10 + i, hT, T, ev)
            PH[0] = "p2"
            silu = AR.alloc("silu", BF16, 8, T)
            for i in range(2):
                def ev(oc, bk, i=i):
                    OP("scalar", "activation", [bk], [silu], out=silu.k(4 * i + oc), in_=bk.f.k(0, 0, T), func=AF.Silu)
                projA("in", 6 + i, hT, T, ev)
            if not last:
                PH[0] = "p2pre"
                hT_next[0] = AR.alloc("hT", BF16, 8, T)
                for g in range(G):
                    xn_ = xring[((ti + 1) * G + g) % NX]
                    norm_transpose(xn_.all(), xn_, gpre, hT_next[0], 128 * g)
            PH[0] = "p2"
            for i in range(2):
                PH[0] = f"p2/A_in{8 + i}"
                w = wuse(BLK["in"][8 + i])
                for oc in range(4):
                    bk = BankNow()
                    for k in range(8):
                        OP("tensor", "matmul", [w, hT], [bk], out=bk.f.k(0, 0, T), lhsT=w.k(k, 128 * oc, 128 * oc + 128),
                           rhs=hT.k(k), start=(k == 0), stop=(k == 7))
                    nw = T
                    if not last:
                        for k in range(8):
                            OP("tensor", "matmul", [w, hT_next[0]], [bk], out=bk.f.k(0, T, T + 8), lhsT=w.k(k, 128 * oc, 128 * oc + 128),
                               rhs=hT_next[0].k(k, 0, 8), start=(k == 0), stop=(k == 7))
                        nw = T + 8
                    OP("scalar", "activation", [bk], [pring], out=pring.k(4 * i + oc, 8, 8 + nw), in_=bk.f.k(0, 0, nw), func=AF.Copy)
                    bk.free()
                wdone()
            PH[0] = "p2/kf_tr"
            kf = AR.alloc("kf", BF16, G, D)
            for g in range(G):
                bk = BankNow()
                for c in range(8):
                    OP("tensor", "transpose", [kT], [bk], out=bk.h.k(0, 128 * c, 128 * c + 128), in_=kT.k(c, 128 * g, 128 * g + 128),
                       identity=ident.all())
                for h in range(4):
                    OP("vector", "tensor_scalar", [bk], [kf], out=kf.k(g, 256 * h, 256 * h + 256), in0=bk.h.k(0, 256 * h, 256 * h + 256),
                       scalar1=kdf.k(0, h, h + 1), scalar2=None, op0=ALU.mult)
                bk.free()
            yn = AR.alloc("yn", BF16, G, D)
            PTs = {}

            def scores_gen(g):
                c0, c1 = 128 * g, 128 * g + 128
                (bs,) = yield from gbanks(1)
                PH[0] = "p2/ret_sc"
                for h in range(4):
                    for dd in range(2):
                        OP("tensor", "matmul", [kT, qf], [bs], out=bs.f.k(0, 128 * h, 128 * h + 128), lhsT=kT.k(2 * h + dd, c0, c1),
                           rhs=qf.k(2 * h + dd, c0, c1), start=(dd == 0), stop=(dd == 1))
                PT = AR.alloc("PT", BF16, 4, 128)
                PTs[g] = PT
                OP("vector", "tensor_tensor", [bs], [PT], out=PT.all(), in0=bs.f.all(), in1=DTt.all(), op=ALU.mult)
                bs.free()
                yield

            def head_gen(g, h):
                c0, c1 = 128 * g, 128 * g + 128
                pr_ = (ti * G + g) % 2
                PT = PTs[g]
                by, bk = yield from gbanks(2)
                PH[0] = "p2/ret_hd"
                yo = by.f.k(0, 0, 256)
                OP("tensor", "matmul", [PT, vt], [by], out=yo, lhsT=PT.k(h), rhs=vt.k(g, 256 * h, 256 * h + 256), start=True, stop=False)
                for dd in range(2):
                    OP("tensor", "matmul", [qf, SbfR[h]], [by], out=yo, lhsT=qf.k(2 * h + dd, c0, c1),
                       rhs=Sbf[pr_].k(h, 256 * dd, 256 * dd + 256), start=False, stop=False)
                for dd in range(2):
                    OP("tensor", "matmul", [qb, Sb[g]], [by], out=yo, lhsT=qb.k(2 * h + dd, c0, c1),
                       rhs=Sb[g].k(h, 256 * dd, 256 * dd + 256), start=False, stop=(dd == 1))
                for dd in range(2):
                    OP("tensor", "matmul", [kf, vt], [bk], out=bk.f.k(0, 256 * dd, 256 * dd + 256),
                       lhsT=kf.k(g, 256 * h + 128 * dd, 256 * h + 128 * dd + 128), rhs=vt.k(g, 256 * h, 256 * h + 256),
                       start=True, stop=True)
                yield
                st = sm(6)
                mv = sm(2)
                OP("vector", "bn_stats", [by], [st], out=st.a, in_=yo)
                OP("vector", "bn_aggr", [st], [mv], out=mv.a, in_=st.a)
                rs = sm()
                OP("vector", "tensor_scalar", [mv], [rs], out=rs.a, in0=mv.sub(1), scalar1=EPS, scalar2=None, op0=ALU.add)
                OP("vector", "scalar_tensor_tensor", [S32res[h], bk], [S32res[h]], out=S32.k(h), in0=S32.k(h), scalar=cdf.k(0, h, h + 1),
                   in1=bk.f.all(), op0=ALU.mult, op1=ALU.add)
                bk.free()
                yield
                rs2 = sm()
                OP("gpsimd", "tensor_tensor", [rs], [rs2], out=rs2.a, in0=rs.a, in1=nhalf.all(), op=ALU.pow)
                OP("scalar", "activation", [S32res[h]], [SbfR[h]], out=Sbf[1 - pr_].k(h), in_=S32.k(h), func=AF.Copy)
                yield
                OP("vector", "tensor_scalar", [by, mv, rs2], [yn], out=yn.k(g, 256 * h, 256 * h + 256), in0=yo, scalar1=mv.sub(0), scalar2=rs2.a,
                   op0=ALU.subtract, op1=ALU.mult)
                by.free()
                if h == 3:
                    AR.release(PT)
                yield

            def ret_thread():
                mk = []
                for g in range(G):
                    mk.append(lambda g=g: scores_gen(g))
                    for h in range(4):
                        mk.append(lambda g=g, h=h: head_gen(g, h))
                yield from skewed(mk)

            dsl = AR.alloc("dsl", BF16, 8, T)
            poolp = AR.alloc("poolp", BF16, 8, T)

            def pool_thread():
                if last:
                    OP("gpsimd", "memset", [], [pring], ap=pring.ks(0, 8, T + 8, T + 16), constant=0.0)
                W = T + 16
                for gi, w in enumerate((2, 4, 8, 16)):
                    wa = wa_p
                    sk = 2 * gi
                    ln = W
                    step = 1
                    cur = None
                    bufs = [wa, wa]
                    bi = 0
                    while step < w:
                        ln2 = ln - step
                        dst = bufs[bi]
                        if cur is None:
                            in0 = pring.ks(sk, sk + 2, 0, ln2)
                            in1 = pring.ks(sk, sk + 2, step, step + ln2)
                            rd = [pring]
                        else:
                            in0 = cur.ks(0, 2, 0, ln2)
                            in1 = cur.ks(0, 2, step, step + ln2)
                            rd = [cur]
                        OP("gpsimd", "tensor_tensor", rd, [dst], out=dst.ks(0, 2, 0, ln2), in0=in0, in1=in1, op=ALU.add)
                        cur = dst
                        bi = 1 - bi
                        ln = ln2
                        step *= 2
                    yield
                    o = 8 - w // 2
                    OP("vector", "scalar_tensor_tensor", [cur, pring], [dsl], out=dsl.ks(sk, sk + 2), in0=cur.ks(0, 2, o, o + T),
                       scalar=1.0 / w, in1=pring.ks(sk, sk + 2, 8, 8 + T), op0=ALU.mult, op1=ALU.subtract)
                    if first or last:
                        e0 = 0 if first else T - 8
                        ic = icf if first else icl
                        tt = AR.alloc("edge", F32, 2, 8)
                        icb = AP(tensor=ic.base.tensor, offset=ic.base.offset + 8 * gi, ap=[ic.p, [0, 2], [1, 8]])
                        OP("gpsimd", "tensor_tensor", [cur], [tt], out=tt.ks(0, 2), in0=cur.ks(0, 2, o + e0, o + e0 + 8), in1=icb, op=ALU.mult)
                        OP("gpsimd", "tensor_tensor", [tt, pring], [dsl], out=dsl.ks(sk, sk + 2, e0, e0 + 8), in0=tt.ks(0, 2),
                           in1=pring.ks(sk, sk + 2, 8 + e0, 16 + e0), op=ALU.subtract)
                        AR.release(tt)
                    yield
                OP("gpsimd", "tensor_copy", [pring], [pring], out=pring.ks(0, 8, 0, 8), in_=pring.ks(0, 8, T, T + 8))
                w = wuse(BLK["poolw"][0])
                for gi in range(4):
                    for oc in range(2):
                        (bk,) = yield from gbanks(1)
                        PH[0] = "p2/pool"
                        for k in range(2):
                            OP("tensor", "matmul", [w, dsl], [bk], out=bk.f.k(0, 0, T), lhsT=w.k(2 * gi + k, 128 * oc, 128 * oc + 128),
                               rhs=dsl.k(2 * gi + k), start=(k == 0), stop=(k == 1))
                        c = 2 * gi + oc
                        OP("scalar", "activation", [bk], [poolp], out=poolp.k(c), in_=bk.f.k(0, 0, T), func=AF.Copy, scale=psc.k(0, c, c + 1))
                        bk.free()
                    yield
                wdone()

            memT = AR.alloc("memT", BF16, 8, T)

            def mem_head(h):
                pr = AR.alloc("probs", BF16, 2, T)
                bks = []
                bks_ = yield from gbanks(2)
                PH[0] = "p2/mem1"
                for mc in range(2):
                    bk = bks_[mc]
                    for dd in range(2):
                        OP("tensor", "matmul", [kmT, mq], [bk], out=bk.f.k(0, 0, T), lhsT=kmT.k(2 * h + dd, 128 * mc, 128 * mc + 128),
                           rhs=mq.k(2 * h + dd), start=(dd == 0), stop=(dd == 1))
                    bks.append(bk)
                yield
                for mc in range(2):
                    OP("scalar", "activation", [bks[mc]], [pr], out=pr.k(mc), in_=bks[mc].f.k(0, 0, T), func=AF.Exp, scale=1.0 / 16)
                    bks[mc].free()
                yield
                bd, bo0, bo1 = yield from gbanks(3)
                PH[0] = "p2/mem2"
                for mc in range(2):
                    OP("tensor", "matmul", [pr], [bd], out=bd.f.k(0, 0, T), lhsT=ones_bf.all(), rhs=pr.k(mc), start=(mc == 0), stop=(mc == 1))
                bos = []
                for ec in range(2):
                    bo = (bo0, bo1)[ec]
                    for mc in range(2):
                        OP("tensor", "matmul", [vm, pr], [bo], out=bo.f.k(0, 0, T), lhsT=vm.k(mc, 256 * h + 128 * ec, 256 * h + 128 * ec + 128),
                           rhs=pr.k(mc), start=(mc == 0), stop=(mc == 1))
                    bos.append(bo)
                yield
                rec = AR.alloc("rec", F32, 1, T)
                OP("scalar", "activation", [bd], [rec], out=rec.all(), in_=bd.f.k(0, 0, T), func=AF.Ln)
                OP("scalar", "activation", [rec], [rec], out=rec.all(), in_=rec.all(), func=AF.Exp, scale=-1.0)
                bd.free()
                yield
                for ec in range(2):
                    OP("vector", "tensor_tensor", [bos[ec], rec], [memT], out=memT.k(2 * h + ec), in0=bos[ec].f.k(0, 0, T), in1=rec.all(), op=ALU.mult)
                    bos[ec].free()
                AR.release(pr, rec)
                yield

            def mem_thread():
                for h in range(4):
                    yield from mem_head(h)

            run_threads([ret_thread(), pool_thread(), mem_thread()])
            AR.release(kT, vt, qf, qb, kf, *Sb)
            AR.release(dsl, mq)
            PH[0] = "p2"
            gates = [AR.alloc(f"gates{j}", BF16, 8, T) for j in range(3)]
            for i in range(6):
                def ev(oc, bk, i=i):
                    OP("scalar", "activation", [bk], [gates[(4 * i + oc) // 8]], out=gates[(4 * i + oc) // 8].k((4 * i + oc) % 8), in_=bk.f.k(0, 0, T), func=AF.Sigmoid)
                projA("in", 12 + i, hT, T, ev)
            AR.release(hT)
            PH[0] = "p2/merge"
            merged = AR.alloc("merged", BF16, 8, T)
            retT = AR.alloc("retT", BF16, 8, T)
            srcs = (poolp, memT, retT)
            MNAMES = ("pool_out", "mem_out", "ret_out")
            GIDX = (1, 2, 0)

            def retT_gen():
                for fc in range(8):
                    (bk,) = yield from gbanks(1)
                    PH[0] = "p2/retT"
                    for g in range(G):
                        OP("tensor", "transpose", [yn], [bk], out=bk.h.k(0, 128 * g, 128 * g + 128), in_=yn.k(g, 128 * fc, 128 * fc + 128),
                           identity=ident.all())
                    OP("vector", "scalar_tensor_tensor", [bk, silu], [retT], out=retT.k(fc), in0=bk.h.k(0, 0, T), scalar=rgn.k(0, fc, fc + 1),
                       in1=silu.k(fc), op0=ALU.mult, op1=ALU.mult)
                    bk.free()
                AR.release(yn, silu)
                yield
            accs8 = [AR.alloc(f"macc{q}", F32, 1, T) for q in range(8)]
            wsm = {}
            if True:
                def mg(half, j, oc4):
                    accs = accs8[4 * half:4 * half + 4]
                    oc = 4 * half + oc4
                    (bk,) = yield from gbanks(1)
                    PH[0] = "p2/merge"
                    if oc4 == 0:
                        wsm[0] = wuse(BLK[MNAMES[j]][half])
                    w = wsm[0]
                    for k in range(8):
                        OP("tensor", "matmul", [w, srcs[j]], [bk], out=bk.f.k(0, 0, T), lhsT=w.k(k, 128 * oc4, 128 * oc4 + 128),
                           rhs=srcs[j].k(k), start=(k == 0), stop=(k == 7))
                    if oc4 == 3:
                        wdone()
                    yield
                    if j == 0:
                        OP("vector", "tensor_tensor", [bk, gates[GIDX[j]]], [accs[oc4]], out=accs[oc4].all(), in0=bk.f.k(0, 0, T), in1=gates[GIDX[j]].k(oc), op=ALU.mult)
                        bk.free()
                        yield
                    else:
                        tmp = AR.alloc("mtmp", F32, 1, T)
                        OP("vector", "tensor_tensor", [bk, gates[GIDX[j]]], [tmp], out=tmp.all(), in0=bk.f.k(0, 0, T), in1=gates[GIDX[j]].k(oc), op=ALU.mult)
                        bk.free()
                        yield
                        if j == 1:
                            OP("gpsimd", "tensor_tensor", [accs[oc4], tmp], [accs[oc4]], out=accs[oc4].all(), in0=accs[oc4].all(), in1=tmp.all(), op=ALU.add)
                        else:
                            OP("gpsimd", "tensor_tensor", [accs[oc4], tmp], [merged], out=merged.k(oc), in0=accs[oc4].all(), in1=tmp.all(), op=ALU.add)
                        AR.release(tmp)
                        yield

                mk_ = []
                for half in range(2):
                    for j in range(3):
                        if half == 0 and j == 2:
                            mk_.append(retT_gen)
                        for oc4 in range(4):
                            mk_.append(lambda half=half, j=j, oc4=oc4: mg(half, j, oc4))
                run_threads([skewed(mk_)])
                AR.release(*accs8)
            AR.release(retT, poolp, memT, *gates)
            if not last:
                load_cs(ti + 1)
            PH[0] = "p2wo"
            h2T = AR.alloc("h2T", BF16, 8, T + 3)
            OP("gpsimd", "tensor_copy", [carry], [h2T], out=h2T.ks(0, 8, 0, 2), in_=carry.ks(0, 8))
            if last:
                OP("gpsimd", "memset", [], [h2T], ap=h2T.ks(0, 8, T + 2, T + 3), constant=0.0)
            wo = [inflight[0][1], inflight[1][1]]
            assert inflight[0][0] == BLK["o"][0] and inflight[1][0] == BLK["o"][1]
            for g in range(G):
                bk2 = []
                for half in range(2):
                    bk = BankNow()
                    for k in range(8):
                        OP("tensor", "matmul", [wo[half], merged], [bk], out=bk.f.all(), lhsT=merged.k(k, 128 * g, 128 * g + 128),
                           rhs=wo[half].k(k), start=(k == 0), stop=(k == 7))
                    bk2.append(bk)
                junk = AR.alloc("junk3", BF16, 1, 512)
                ssl = []
                for half in range(2):
                    ss = sm()
                    OP("scalar", "activation", [bk2[half]], [junk, ss], out=junk.all(), in_=bk2[half].f.all(), func=AF.Square, accum_out=ss.a)
                    ssl.append(ss)
                r = rstd_from(ssl, D)
                tt = AR.alloc("wot", F32, 1, D)
                for half in range(2):
                    OP("vector", "scalar_tensor_tensor", [bk2[half], r], [tt], out=tt.k(0, 512 * half, 512 * half + 512),
                       in0=bk2[half].f.all(), scalar=r.a, in1=gpost.k(0, 512 * half, 512 * half + 512), op0=ALU.mult, op1=ALU.mult)
                OP("gpsimd", "tensor_tensor", [tt, xs[g]], [xs[g]], out=xs[g].all(), in0=tt.all(), in1=xs[g].all(), op=ALU.add)
                for bk in bk2:
                    bk.free()
                AR.release(junk, tt)
                norm_transpose(xs[g].all(), xs[g], gfpre, h2T, 2 + 128 * g)
            wdone()
            wdone()
            AR.release(merged)
            OP("gpsimd", "tensor_copy", [h2T], [carry], out=carry.ks(0, 8), in_=h2T.ks(0, 8, T, T + 2))
            NO = T + 1 if last else T
            NC = NO + 2
            wcur = {}

            def pair_gen(j):
                bi_, jj = j // 2, j % 2
                bg, bv = yield from gbanks(2)
                PH[0] = "p2/ffn_up"
                if jj == 0:
                    wcur[0] = wuse(BLK["up"][bi_])
                w = wcur[0]
                for k in range(8):
                    OP("tensor", "matmul", [w, h2T], [bg], out=bg.f.k(0, 0, NC), lhsT=w.k(k, 128 * jj, 128 * jj + 128), rhs=h2T.k(k, 0, NC),
                       start=(k == 0), stop=(k == 7))
                for k in range(8):
                    OP("tensor", "matmul", [w, h2T], [bv], out=bv.f.k(0, 0, NC), lhsT=w.k(k, 256 + 128 * jj, 256 + 128 * jj + 128),
                       rhs=h2T.k(k, 0, NC), start=(k == 0), stop=(k == 7))
                if jj == 1:
                    wdone()
                yield
                cvs = []
                for (bk, ci) in ((bg, j), (bv, NPAIR + j)):
                    cv = AR.alloc("cv", F32, 1, T + 1)
                    OP("scalar", "activation", [bk], [cv], out=cv.k(0, 0, NO), in_=bk.f.k(0, 0, NO), func=AF.Identity,
                       scale=cw.k(0, ci, ci + 1), bias=cb.k(0, ci, ci + 1))
                    cvs.append(cv)
                yield
                for (bk, cv, ci) in ((bg, cvs[0], j), (bv, cvs[1], NPAIR + j)):
                    OP("vector", "scalar_tensor_tensor", [bk, cv], [cv], out=cv.k(0, 0, NO), in0=bk.f.k(0, 1, 1 + NO), scalar=cw.k(1, ci, ci + 1),
                       in1=cv.k(0, 0, NO), op0=ALU.mult, op1=ALU.add)
                    OP("vector", "scalar_tensor_tensor", [bk, cv], [cv], out=cv.k(0, 0, NO), in0=bk.f.k(0, 2, 2 + NO), scalar=cw.k(2, ci, ci + 1),
                       in1=cv.k(0, 0, NO), op0=ALU.mult, op1=ALU.add)
                    bk.free()
                cg, cv = cvs
                yield
                sq = AR.alloc("gsq", F32, 1, T + 1)
                OP("scalar", "activation", [cg], [sq], out=sq.k(0, 0, NO), in_=cg.k(0, 0, NO), func=AF.Square)
                gm = AR.alloc("gm", F32, 1, T + 1)
                OP("gpsimd", "tensor_tensor", [cg, cv], [gm], out=gm.k(0, 0, NO), in0=cg.k(0, 0, NO), in1=cv.k(0, 0, NO), op=ALU.mult)
                yield
                OP("gpsimd", "tensor_scalar", [sq], [sq], out=sq.k(0, 0, NO), in0=sq.k(0, 0, NO), scalar1=0.044715, scalar2=1.0,
                   op0=ALU.mult, op1=ALU.add)
                OP("gpsimd", "tensor_tensor", [sq, cg], [sq], out=sq.k(0, 0, NO), in0=sq.k(0, 0, NO), in1=cg.k(0, 0, NO), op=ALU.mult)
                yield
                OP("scalar", "activation", [sq], [sq], out=sq.k(0, 0, NO), in_=sq.k(0, 0, NO), func=AF.Sigmoid, scale=1.5957691216057308)
                yield
                OP("vector", "tensor_tensor", [gm, sq], [Gr], out=Gr.k(j, 127, 127 + NO), in0=gm.k(0, 0, NO), in1=sq.k(0, 0, NO), op=ALU.mult)
                AR.release(cg, cv, sq, gm)
                yield

            run_threads([skewed([(lambda j=j: pair_gen(j)) for j in range(NPAIR)])])
            AR.release(h2T)
            if not last:
                HD = head(ti + 1, hT_next[0])
                hT_next[0] = None
            PH[0] = "p2/ffn_down"
            groups = list(range(0 if not first else 1, G + (1 if last else 0)))
            slots = []
            toks = []
            for gcol in groups:
                gidx = ti * G + gcol - 1
                slots.append(xring[gidx % NX])
                toks.append(128 * gidx)
            ffn_down(groups, slots, toks)
            if not last:
                OP("gpsimd", "tensor_copy", [Gr], [Gr], out=Gr.ks(0, NPAIR, 0, 128), in_=Gr.ks(0, NPAIR, T, T + 128))
        AR.release(*xring, pring, wa_p, Gr, carry, S32, _sbf1, kmT, vm)
        tok0 += SL
        ch0 += NCH

    assert not wq and not inflight, (wq, inflight)
    S.emit()
    es.close()
    build_program.tags = {e: [i.tag for i in S.streams[e] if not i.is_dma] for e in ENGS}
    build_program.stats = dict(peak_pages=AR.peak, pages=AR.np_,
                               ninstr={e: len(S.streams[e]) for e in ENGS})
    return nc


T_TILE = 256
_cache = {}


def kernel(**inputs):
    f = lambda a: np.ascontiguousarray(np.asarray(a, dtype=np.float32))
    xp = f(inputs["x_prompt"])
    xs = f(inputs["x_sample"])
    mp = f(inputs["mem_prompt"])
    ms = f(inputs["mem_sample"])
    NC = 8
    nb_p = xp.shape[0] // NC
    nb_s = xs.shape[0] // NC
    SP, SS = xp.shape[1], xs.shape[1]
    seq_lens = [SP] * nb_p + [SS] * nb_s
    key = (tuple(seq_lens), T_TILE)
    if key not in _cache:
        _cache[key] = build_program(seq_lens, T_TILE)
    nc = _cache[key]
    shared = {}
    for nm in ("g_mix_pre", "g_mem", "ret_gn", "pool_scale", "g_ffn_pre", "conv_b"):
        shared[nm] = f(inputs[nm])[0].reshape(-1)
    for nm in ("g_mix_post", "g_ffn_post", "decay_fwd", "decay_bwd"):
        shared[nm] = f(inputs[nm])[0].reshape(1, -1)
    for nm in ("w_in", "w_ret_out", "pool_w", "w_pool_out", "w_mem_kv", "w_mem_out", "w_o", "w_up", "conv_w", "w_down"):
        shared[nm] = f(inputs[nm])[0]
    in_maps = []
    for c in range(NC):
        xc = np.concatenate([xp[c * nb_p + i] for i in range(nb_p)] + [xs[c * nb_s + i] for i in range(nb_s)], axis=0)
        mc = np.concatenate([mp[c * nb_p + i] for i in range(nb_p)] + [ms[c * nb_s + i] for i in range(nb_s)], axis=0)
        m = dict(shared)
        m["x"] = np.ascontiguousarray(xc)
        m["mem"] = np.ascontiguousarray(mc)
        in_maps.append(m)
    res = run_bass_kernel_spmd(nc, in_maps, core_ids=list(range(NC)))
    yp = np.empty_like(xp)
    ys = np.empty_like(xs)
    for c in range(NC):
        y = res.results[c]["y"]
        o = 0
        for i in range(nb_p):
            yp[c * nb_p + i] = y[o:o + SP]
            o += SP
        for i in range(nb_s):
            ys[c * nb_s + i] = y[o:o + SS]
            o += SS
    return (yp, ys)
```
